# Optimizing a Trainium2 kernel written in Bass

```python
import jax, jax.numpy as jnp
from jax import lax
import numpy as np

D_MODEL = 1024
BATCH = 16
SEQ = 2048
DEPTH = 1

HGRN_HEADS = 8
HGRN_KEY_DIM = 128
HGRN_VAL_DIM = D_MODEL // HGRN_HEADS
HGRN_FWIDTH = HGRN_HEADS * HGRN_KEY_DIM
HGRN_WIDTH = HGRN_HEADS * HGRN_VAL_DIM
CHUNK = 16
POOL_WINDOWS = (2, 4, 8, 16)
POOL_GROUPS = len(POOL_WINDOWS)
POOL_WIDTH = D_MODEL
POOL_GROUP_DIM = POOL_WIDTH // POOL_GROUPS
D_FF = -(-(8 * D_MODEL) // (3 * 256)) * 256
RMS_EPS = 1e-6
IN_SPLITS = (HGRN_FWIDTH, HGRN_FWIDTH, HGRN_FWIDTH, HGRN_WIDTH, HGRN_WIDTH, POOL_WIDTH, D_MODEL, D_MODEL)
IN_WIDTH = sum(IN_SPLITS)

kernel_name = "hgrn2_multipool_gated_hybrid_encoder"


def rmsnorm(x, g):
    xf = x.astype(jnp.float32)
    y = xf * lax.rsqrt(jnp.mean(xf * xf, axis=-1, keepdims=True) + RMS_EPS)
    return (y * g.astype(jnp.float32)).astype(x.dtype)


def chunk_gated_recurrence(q, k, v, log_f):
    B, H, L, N = q.shape
    Dv = v.shape[-1]
    n_chunks = L // CHUNK

    def to_chunks(t):
        return jnp.moveaxis(t.reshape(B, H, n_chunks, CHUNK, t.shape[-1]), 2, 0)

    qc, kc, vc, gc = to_chunks(q), to_chunks(k), to_chunks(v), to_chunks(log_f)
    mask = jnp.tril(jnp.ones((CHUNK, CHUNK), dtype=bool))

    def step(S, inp):
        qi, ki, vi, gi = inp
        b = jnp.cumsum(gi, axis=-2)
        b_last = b[..., -1:, :]
        q_dec = qi * jnp.exp(b)
        k_inv = ki * jnp.exp(-b)
        scores = jnp.einsum('bhin,bhjn->bhij', q_dec, k_inv)
        scores = jnp.where(mask, scores, 0.0)
        o = (jnp.einsum('bhij,bhjd->bhid', scores, vi)
             + jnp.einsum('bhin,bhnd->bhid', q_dec, S))
        k_end = ki * jnp.exp(b_last - b)
        S = (jnp.exp(b_last)[..., 0, :, None] * S
             + jnp.einsum('bhjn,bhjd->bhnd', k_end, vi))
        return S, o

    S0 = jnp.zeros((B, H, N, Dv), q.dtype)
    _, o = lax.scan(step, S0, (qc, kc, vc, gc))
    return jnp.moveaxis(o, 0, 2).reshape(B, H, L, Dv)


def hgrn2_bidirectional(q_raw, ff_raw, fb_raw, i_raw, og_raw, lb, norm_g):
    B, L, _ = q_raw.shape
    f32 = jnp.float32

    def heads(t, d):
        return t.reshape(B, L, HGRN_HEADS, d).transpose(0, 2, 1, 3)

    q = heads(jax.nn.silu(q_raw.astype(f32)), HGRN_KEY_DIM)
    v = heads(i_raw.astype(f32), HGRN_VAL_DIM)
    f_fwd = lb[0] + (1.0 - lb[0]) * jax.nn.sigmoid(ff_raw.astype(f32))
    f_bwd = lb[1] + (1.0 - lb[1]) * jax.nn.sigmoid(fb_raw.astype(f32))
    f = jnp.concatenate([heads(f_fwd, HGRN_KEY_DIM), heads(f_bwd, HGRN_KEY_DIM)[:, :, ::-1]], axis=1)
    qq = jnp.concatenate([q, q[:, :, ::-1]], axis=1)
    vv = jnp.concatenate([v, v[:, :, ::-1]], axis=1)
    o = chunk_gated_recurrence(qq, 1.0 - f, vv, jnp.log(f))
    o = o[:, :HGRN_HEADS] + o[:, HGRN_HEADS:, ::-1]
    o = o * lax.rsqrt(jnp.mean(o * o, axis=-1, keepdims=True) + RMS_EPS)
    o = o.transpose(0, 2, 1, 3).reshape(B, L, HGRN_WIDTH) * norm_g.astype(f32)
    return (o * jax.nn.silu(og_raw.astype(f32))).astype(q_raw.dtype)


def multiscale_pool(p, w_grp, scale):
    B, L, _ = p.shape
    f32 = jnp.float32
    pg = p.astype(f32).reshape(B, L, POOL_GROUPS, POOL_GROUP_DIM)
    cs = jnp.concatenate([jnp.zeros((B, 1, POOL_GROUPS, POOL_GROUP_DIM), f32),
                          jnp.cumsum(pg, axis=1)], axis=1)
    half = jnp.array([w // 2 for w in POOL_WINDOWS], dtype=jnp.int32)
    t = jnp.arange(L, dtype=jnp.int32)[:, None]
    lo = jnp.clip(t - half + 1, 0, L)
    hi = jnp.clip(t + half + 1, 0, L)
    gidx = jnp.arange(POOL_GROUPS, dtype=jnp.int32)[None, :]
    win_sum = cs[:, hi, gidx, :] - cs[:, lo, gidx, :]
    count = (hi - lo).astype(f32)[..., None]
    y = win_sum / count - pg
    y = jnp.einsum('blgc,gcd->blgd', y, w_grp.astype(f32))
    return (y.reshape(B, L, POOL_WIDTH) * scale.astype(f32)).astype(p.dtype)


def setup_inputs(seed: int = 0) -> dict:
    key = jax.random.key(seed)
    ks = jax.random.split(key, 16)
    f32 = jnp.float32
    nrm = lambda k, shape, fan_in: jax.random.normal(k, shape, f32) * (fan_in ** -0.5)
    gain = lambda k, shape: 1.0 + 0.02 * jax.random.normal(k, shape, f32)
    return {
        "x": jax.random.normal(ks[0], (BATCH, SEQ, D_MODEL), f32),
        "g_mix": gain(ks[1], (DEPTH, D_MODEL)),
        "w_in": nrm(ks[2], (DEPTH, D_MODEL, IN_WIDTH), D_MODEL),
        "lb_logits": 0.1 * jax.random.normal(ks[3], (2, DEPTH + 1, HGRN_FWIDTH), f32),
        "hgrn_norm_g": gain(ks[4], (DEPTH, HGRN_WIDTH)),
        "pool_w": nrm(ks[5], (DEPTH, POOL_GROUPS, POOL_GROUP_DIM, POOL_GROUP_DIM), POOL_GROUP_DIM),
        "pool_scale": gain(ks[6], (DEPTH, POOL_WIDTH)),
        "w_branch_a": nrm(ks[7], (DEPTH, HGRN_WIDTH, D_MODEL), HGRN_WIDTH),
        "w_branch_b": nrm(ks[8], (DEPTH, POOL_WIDTH, D_MODEL), POOL_WIDTH),
        "w_out": nrm(ks[9], (DEPTH, D_MODEL, D_MODEL), D_MODEL),
        "g_ffn": gain(ks[10], (DEPTH, D_MODEL)),
        "w_ffn_in": nrm(ks[11], (DEPTH, D_MODEL, 2 * D_FF), D_MODEL),
        "w_ffn_out": nrm(ks[12], (DEPTH, D_FF, D_MODEL), D_FF),
        "g_final": gain(ks[13], (D_MODEL,)),
    }


def reference(x, g_mix, w_in, lb_logits, hgrn_norm_g, pool_w, pool_scale, w_branch_a,
              w_branch_b, w_out, g_ffn, w_ffn_in, w_ffn_out, g_final):
    lb_all = jnp.cumsum(jax.nn.softmax(lb_logits.astype(jnp.float32), axis=1), axis=1)
    offsets = np.cumsum(IN_SPLITS)[:-1].tolist()
    h = x
    for l in range(DEPTH):
        u = rmsnorm(h, g_mix[l])
        proj = jnp.einsum('bsd,de->bse', u, w_in[l])
        q_r, ff_r, fb_r, i_r, og_r, p_r, ga_r, gb_r = jnp.split(proj, offsets, axis=-1)
        y_a = hgrn2_bidirectional(q_r, ff_r, fb_r, i_r, og_r, lb_all[:, l], hgrn_norm_g[l])
        y_b = multiscale_pool(p_r, pool_w[l], pool_scale[l])
        z_a = jnp.einsum('bse,ed->bsd', y_a, w_branch_a[l])
        z_b = jnp.einsum('bse,ed->bsd', y_b, w_branch_b[l])
        merged = jax.nn.sigmoid(ga_r) * z_a + jax.nn.sigmoid(gb_r) * z_b
        h = h + jnp.einsum('bsd,de->bse', merged, w_out[l])
        u = rmsnorm(h, g_ffn[l])
        gate, up = jnp.split(jnp.einsum('bsd,df->bsf', u, w_ffn_in[l]), 2, axis=-1)
        h = h + jnp.einsum('bsf,fd->bsd', jax.nn.silu(gate) * up, w_ffn_out[l])
    return rmsnorm(h, g_final)
```

```python
import numpy as np
import ml_dtypes
from contextlib import ExitStack
import concourse.bass as bass
import concourse.mybir as mybir
from concourse.bass_utils import run_bass_kernel_spmd

F32 = mybir.dt.float32
BF16 = mybir.dt.bfloat16
AF = mybir.ActivationFunctionType
ALU = mybir.AluOpType

NCORES = 8
D = 1024
SEQ = 2048
NSEQ = 2
NB = SEQ // 128
DFF = 2816
NFC = DFF // 128
EPS = 1e-6
ENGS = ["pe", "act", "dve", "pool", "sp"]
DEBUG = False


class St:
    __slots__ = ("w", "r", "dsem", "dcount")

    def __init__(self):
        self.w = {}
        self.r = {}
        self.dsem = None
        self.dcount = 0


class Buf:
    def __init__(self, t, st=None):
        self.t = t
        self.st = st if st is not None else St()


class Item:
    __slots__ = ("waits", "fn", "inc")

    def __init__(self, waits, fn, inc):
        self.waits = waits
        self.fn = fn
        self.inc = inc


class Prog:
    def __init__(self):
        self.q = {e: [] for e in ENGS}
        self.cnt = {e: 0 for e in ENGS}
        self.seen = {e: {} for e in ENGS}
        self.ndma = 0
        self.dsts = []

    def barrier(self):
        for eng in ENGS:
            waits = []
            seen = self.seen[eng]
            for f in ("pe", "act", "dve", "pool"):
                if f != eng and self.cnt[f] > seen.get(f, 0):
                    seen[f] = self.cnt[f]
                    waits.append((f, self.cnt[f]))
            for st in self.dsts:
                key = ("dma", st.dsem)
                if st.dcount > seen.get(key, 0):
                    seen[key] = st.dcount
                    waits.append((key, st.dcount))
            if waits:
                self.q[eng].append(Item(waits, None, None))

    def _waits(self, eng, reads, writes):
        need = {}

        def add(key, val):
            if val > need.get(key, 0):
                need[key] = val

        for st in reads:
            for k, c in st.w.items():
                add(k, c)
        for st in writes:
            for k, c in st.w.items():
                if k != eng:
                    add(k, c)
            for k, c in st.r.items():
                if k != eng:
                    add(k, c)
        if eng == "pe":
            need.pop("pe", None)
        out = []
        seen = self.seen[eng]
        for key, val in need.items():
            if seen.get(key, 0) < val:
                seen[key] = val
                out.append((key, val))
        return out

    def op(self, eng, fn, reads=(), writes=(), signal=True):
        reads = [b.st for b in reads]
        writes = [b.st for b in writes]
        waits = self._waits(eng, reads, writes)
        if signal:
            self.cnt[eng] += 1
            c = self.cnt[eng]
            inc = (eng, 1)
        else:
            c = self.cnt[eng] + 1
            inc = None
        for st in writes:
            st.w = {eng: c}
            st.r = {}
        for st in reads:
            st.r[eng] = c
        self.q[eng].append(Item(waits, fn, inc))

    def dma(self, queue, fn, buf, is_load=True):
        st = buf.st
        if st.dsem is None:
            st.dsem = self.ndma
            self.ndma += 1
            self.dsts.append(st)
        if is_load:
            own = ("dma", st.dsem)
            prev_load = st.w.pop(own, None)
            waits = self._waits(queue, [], [st])
            if prev_load is not None:
                st.w[own] = prev_load
        else:
            waits = self._waits(queue, [st], [])
        st.dcount += 16
        key = ("dma", st.dsem)
        if is_load:
            st.w = {key: st.dcount}
            st.r = {}
        else:
            st.r[key] = st.dcount
        self.q[queue].append(Item(waits, fn, (key, 16)))

    def wait_all(self, eng, bufs):
        waits = self._waits(eng, [], [b.st for b in bufs])
        if waits:
            self.q[eng].append(Item(waits, None, None))

    def emit(self, block, sems_eng, sems_dma):
        def semof(key):
            if isinstance(key, tuple):
                return sems_dma[key[1]]
            return sems_eng[key]

        def body(engname):
            def _f(e):
                for it in self.q[engname]:
                    for key, val in it.waits:
                        e.wait_ge(semof(key), val)
                    if it.fn is not None:
                        ins = it.fn(e)
                        if it.inc is not None:
                            ins.then_inc(semof(it.inc[0]), it.inc[1])
            return _f

        block.tensor(body("pe"))
        block.scalar(body("act"))
        block.vector(body("dve"))
        block.gpsimd(body("pool"))
        block.sync(body("sp"))


def _pool_mats():
    wins = (2, 4, 8, 16)
    L = SEQ
    mats = np.zeros((128, 20, 128), np.float32)
    t = np.arange(L)
    for g, w in enumerate(wins):
        half = w // 2
        lo = np.clip(t - half + 1, 0, L)
        hi = np.clip(t + half + 1, 0, L)
        Pm = np.zeros((L, L), np.float32)
        for tt in range(L):
            Pm[tt, lo[tt]:hi[tt]] = 1.0 / float(hi[tt] - lo[tt])
        Pm -= np.eye(L, dtype=np.float32)

        def blk(tb, sb):
            return Pm[tb * 128:(tb + 1) * 128, sb * 128:(sb + 1) * 128].T

        mats[:, g * 5 + 0, :] = blk(5, 4)
        mats[:, g * 5 + 1, :] = blk(5, 5)
        mats[:, g * 5 + 2, :] = blk(5, 6)
        mats[:, g * 5 + 3, :] = blk(0, 0)
        mats[:, g * 5 + 4, :] = blk(NB - 1, NB - 1)
    return mats


def _const_bf16():
    cb = np.zeros((128, 26, 128), np.float32)
    cb[:, 0, :] = np.eye(128)
    cb[:, 1, :] = 1.0 / 128.0
    s = np.arange(128)[:, None]
    t = np.arange(128)[None, :]
    cb[:, 2, :] = (s <= t)
    cb[:, 3, :] = (s >= t)
    cb[:, 4, :] = (s <= t)
    cb[:, 5, :] = (s >= t)
    cb[:, 6:26, :] = _pool_mats()
    return cb.reshape(128, 26 * 128).astype(ml_dtypes.bfloat16)


def build_program():
    nc = bass.Bass("TRN2", target_bir_lowering=False)
    x_d = nc.dram_tensor("x", [NSEQ * SEQ, D], F32, kind="ExternalInput").ap()
    win_d = nc.dram_tensor("w_in", [D, 8 * D], F32, kind="ExternalInput").ap()
    wa_d = nc.dram_tensor("w_a", [D, D], F32, kind="ExternalInput").ap()
    wb_d = nc.dram_tensor("w_b", [D, D], F32, kind="ExternalInput").ap()
    wo_d = nc.dram_tensor("w_o", [D, D], F32, kind="ExternalInput").ap()
    pw_d = nc.dram_tensor("pool_w", [4, 256, 256], F32, kind="ExternalInput").ap()
    w1_d = nc.dram_tensor("w_ffn_in", [D, 2 * DFF], F32, kind="ExternalInput").ap()
    w2_d = nc.dram_tensor("w_ffn_out", [DFF, D], F32, kind="ExternalInput").ap()
    gv_d = nc.dram_tensor("gvec", [3, D], F32, kind="ExternalInput").ap()
    cols_d = nc.dram_tensor("cols", [128, 48], F32, kind="ExternalInput").ap()
    cb_d = nc.dram_tensor("cb", [128, 26 * 128], BF16, kind="ExternalInput").ap()
    out_d = nc.dram_tensor("out", [NSEQ * SEQ, D], F32, kind="ExternalOutput").ap()
    s_slab = nc.dram_tensor("s_slab", [12, 128, 8, 512], BF16).ap()
    s_w1 = nc.dram_tensor("s_w1", [NFC, 128, 2, 8, 128], BF16).ap()
    s_w2 = nc.dram_tensor("s_w2", [DFF, D], BF16).ap()

    P = Prog()
    with ExitStack() as es:
        def sb(name, shape, dt):
            return Buf(es.enter_context(nc.sbuf_tensor("sb_" + name, shape, dt)))

        gbc = [sb("gbc%d" % i, [128, D], BF16 if i < 2 else F32) for i in range(3)]
        cols = sb("cols", [128, 48], F32)
        cb = sb("cb", [128, 26, 128], BF16)
        lbt = sb("lbt", [128, 5, 16], F32)
        uT = sb("uT", [128, 8, SEQ], BF16)
        yaT = sb("yaT", [128, 8, SEQ], BF16)
        ident = cb.t[:, 0, :]
        onesm = cb.t[:, 1, :]
        masks4 = cb.t[:, 2:6, :]

        def pmat(g, kind):
            return cb.t[:, 6 + g * 5 + kind, :]

        PSB = [Buf(es.enter_context(nc.psum_tensor("psb%d" % i, [128, 512], F32))) for i in range(8)]
        ps_i = [0]

        def bank():
            b = PSB[ps_i[0] % 8]
            ps_i[0] += 1
            return b

        def v4(b):
            return b.t[:, :].rearrange("p (a b) -> p a b", a=4)

        def vt(b):
            return b.t[:, :].bitcast(BF16).rearrange("p (a b) -> p a b", a=8)

        xt = [sb("xt%d" % i, [128, D], F32) for i in range(2)]
        xs = [sb("xs%d" % i, [128, D], BF16) for i in range(2)]
        smask = sb("smask", [128, NB, 129], BF16)
        ssq = sb("ssq", [128, 3, 16], F32)
        ss2 = sb("ss2", [128, 3, 4], F32)
        ss3 = sb("ss3", [128, 3, 4], F32)

        ARENA = 111104
        AR = es.enter_context(nc.sbuf_tensor("AR", [128, ARENA // 2], BF16))
        cur = [0]

        def sb(name, shape, dt):
            n = 1
            for k in shape[1:]:
                n *= k
            nbytes = n * (4 if dt == F32 else 2)
            off = cur[0]
            cur[0] += (nbytes + 3) // 4 * 4
            assert cur[0] <= ARENA, (name, cur[0])
            ap = AR[:, off // 2:(off + nbytes) // 2]
            if dt == F32:
                ap = ap.bitcast(F32)
            if len(shape) == 3:
                ap = ap.rearrange("p (a b) -> p a b", a=shape[1])
            elif len(shape) == 4:
                ap = ap.rearrange("p (a b c) -> p a b c", a=shape[1], b=shape[2])
            return Buf(ap)

        wh = [sb("wh%d" % i, [128, 5, 8, 128], BF16) for i in range(2)]
        sgq = [sb("sgq%d" % i, [128, 512], BF16) for i in range(2)]
        q_s = sb("q_s", [128, SEQ], BF16)
        sog = [sb("sog%d" % i, [128, SEQ], BF16) for i in range(2)]
        kT = [sb("kT%d" % i, [128, SEQ], BF16) for i in range(2)]
        G = [sb("G%d" % i, [128, NB, 129], F32) for i in range(2)]
        qd = [sb("qd%d" % i, [128, SEQ], BF16) for i in range(2)]
        kiT = [sb("kiT%d" % i, [128, SEQ], BF16) for i in range(2)]
        eA0 = sb("eA", [128, SEQ], BF16)
        eA = [eA0, eA0]
        ki = [sb("ki%d" % i, [128, NB, 128], BF16) for i in range(2)]
        v_h = sb("v_h", [128, NB, 128], BF16)
        used = [sb("used%d" % i, [128, NB, 128], BF16) for i in range(2)]
        Ed = [sb("Ed%d" % i, [128, NB, 1], F32) for i in range(2)]
        Emat = sb("Emat", [128, 1024], F32)
        mlt = [sb("mlt%d" % i, [128, 16], F32) for i in range(2)]
        msk = [sb("msk%d" % i, [128, 4, 128], BF16) for i in range(4)]
        sq = sgq
        lnr = sb("lnr", [128, 512], F32)

        print("arena phase2 bytes", cur[0])
        cur[0] = 0
        wpool = sb("wpool", [128, 4, 2, 256], BF16)
        WA = [sb("WA%d" % i, [128, 8, 512], BF16) for i in range(3)]
        wgu = [sb("wgu%d" % i, [128, 2, 8, 128], BF16) for i in range(2)]
        w2s = [sb("w2s%d" % i, [128, D], BF16) for i in range(4)]
        R1 = sb("R1", [128, 12288], BF16).t
        pblk = Buf(R1[:, 0:6144].rearrange("p (a b) -> p a b", a=6))
        yT = Buf(R1[:, 6144:10240].rearrange("p (a b) -> p a b", a=8))
        ybT = Buf(R1[:, 0:4096].rearrange("p (a b) -> p a b", a=8))
        sga = Buf(R1[:, 4096:8192].rearrange("p (a b) -> p a b", a=8))
        sgb = Buf(R1[:, 8192:12288].rearrange("p (a b) -> p a b", a=8))
        actT = Buf(R1[:, 0:11264].rearrange("p (a b) -> p a b", a=NFC))
        h2 = sb("h2", [128, 4, D], F32)
        mT = sb("mT", [128, 8, 512], BF16)
        u2T = sb("u2T", [128, 8, 512], BF16)
        ost = xt
        tz = [sb("tz%d" % i, [128, 512], BF16) for i in range(2)]
        sgt = [sb("sgt%d" % i, [128, 512], BF16) for i in range(2)]
        print("arena phase3 bytes", cur[0])

        dbg_bufs = []

        def dump(name, ap, buf, shape, dt):
            if not DEBUG:
                return
            dd = nc.dram_tensor("dbg_" + name, list(shape), dt, kind="ExternalOutput").ap()
            P.dma("sp", (lambda dd, ap: lambda e: e.dma_start(out=dd, in_=ap))(dd, ap), buf, is_load=False)
            dbg_bufs.append(buf)

        def L(meth, *a, **kw):
            return lambda e: getattr(e, meth)(*a, **kw)

        for i in range(3):
            P.dma("pool" if i < 2 else "sp", L("dma_start", out=gbc[i].t[:], in_=gv_d[i].partition_broadcast(128)), gbc[i])
        P.dma("sp", L("dma_start", out=cols.t[:], in_=cols_d), cols)
        P.dma("sp", L("dma_start", out=cb.t[:], in_=cb_d.rearrange("p (a b) -> p a b", a=26)), cb)
        P.op("pool", L("memset", smask.t[:, :, :], 1.0), [], [smask])
        P.op("pool", L("memset", smask.t[:, :, 0:1], 0.0), [], [smask])
        for a in range(2):
            P.op("dve", L("tensor_tensor", out=lbt.t[:, 0, a * 8:(a + 1) * 8],
                          in0=cols.t[:, 16 + (a * 2 + 1) * 8:16 + (a * 2 + 2) * 8],
                          in1=cols.t[:, 16 + (a * 2) * 8:16 + (a * 2 + 1) * 8], op=ALU.subtract), [cols], [lbt])
        P.op("act", L("activation", out=lbt.t[:, 4, :], in_=lbt.t[:, 0, :], func=AF.Exp), [lbt], [lbt])
        P.op("dve", L("tensor_scalar_add", out=lbt.t[:, 0, :], in0=lbt.t[:, 4, :], scalar1=1.0), [lbt], [lbt])
        P.op("dve", L("reciprocal", out=lbt.t[:, 1, :], in_=lbt.t[:, 0, :]), [lbt], [lbt])
        P.op("dve", L("tensor_scalar", out=lbt.t[:, 2, :], in0=lbt.t[:, 1, :], scalar1=-1.0, scalar2=1.0,
                      op0=ALU.mult, op1=ALU.add), [lbt], [lbt])
        P.op("dve", L("tensor_scalar_add", out=lbt.t[:, 3, :], in0=lbt.t[:, 1, :], scalar1=-1.0), [lbt], [lbt])

        def rms_stats(src_ap, src_buf, ssb, col, junk):
            P.op("act", L("activation", out=junk.t[:], in_=src_ap, func=AF.Square,
                          accum_out=ssb.t[:, 0, col:col + 1]), [src_buf], [junk, ssb])
            P.op("act", L("activation", out=ssb.t[:, 1, col:col + 1], in_=ssb.t[:, 0, col:col + 1], func=AF.Ln,
                          scale=1.0 / D, bias=EPS), [ssb], [ssb])
            P.op("act", L("activation", out=ssb.t[:, 2, col:col + 1], in_=ssb.t[:, 1, col:col + 1], func=AF.Exp,
                          scale=-0.5), [ssb], [ssb])

        def fm_prep(src_ap, src_buf, rstd_ap, rstd_buf, gb, slot):
            P.op("dve", L("scalar_tensor_tensor", out=xs[slot].t[:], in0=src_ap, scalar=rstd_ap, in1=gb.t[:],
                          op0=ALU.mult, op1=ALU.mult), [src_buf, rstd_buf, gb], [xs[slot]])

        def fm_transpose(slot, dst_ap, dst_buf, k):
            pb = bank()
            for c in range(8):
                P.op("pe", L("transpose", vt(pb)[:, c, :], xs[slot].t[:, c * 128:(c + 1) * 128], ident),
                     [xs[slot], cb], [pb], signal=(c == 7))
            if k % 2 == 0:
                P.op("act", L("activation", out=dst_ap, in_=vt(pb), func=AF.Copy), [pb], [dst_buf])
            else:
                P.op("dve", L("tensor_copy", out=dst_ap, in_=vt(pb)), [pb], [dst_buf])

        def to_feature_major(src_ap, src_buf, rstd_ap, rstd_buf, gb, slot, dst_ap, dst_buf, k):
            fm_prep(src_ap, src_buf, rstd_ap, rstd_buf, gb, slot)
            fm_transpose(slot, dst_ap, dst_buf, k)

        wa_i = [0]

        def load_slab(idx):
            b = WA[wa_i[0] % 3]
            wa_i[0] += 1
            P.dma("sp", L("dma_start", out=b.t[:, :, :], in_=s_slab[idx]), b)
            return b

        def load_head(h):
            b = wh[h % 2]
            for seg in range(5):
                P.dma("pool", L("dma_start", out=b.t[:, seg, :, :],
                                in_=win_d[:, seg * D + h * 128:seg * D + (h + 1) * 128].rearrange("(c p) n -> p c n", p=128)), b)
            return b

        conv_dummy = [Buf(None) for _ in range(4)]
        conv_jobs = []
        for c in range(8):
            rows = slice(c * 128, (c + 1) * 128)
            conv_jobs.append((s_slab[0:6, :, c, :].rearrange("s p n -> p s n"),
                              win_d[rows, 5 * D:8 * D].rearrange("p (s n) -> p s n", n=512)))
            for mi, wd in enumerate((wa_d, wb_d, wo_d)):
                conv_jobs.append((s_slab[6 + 2 * mi:8 + 2 * mi, :, c, :].rearrange("s p n -> p s n"),
                                  wd[rows, :].rearrange("p (s n) -> p s n", n=512)))
            conv_jobs.append((s_w1[:, :, :, c, :].rearrange("f p k n -> p k f n"),
                              w1_d[rows, :].rearrange("p (k f n) -> p k f n", k=2, n=128)))
        for f2 in range(0, NFC, 2):
            conv_jobs.append((s_w2[f2 * 128:(f2 + 2) * 128, :], w2_d[f2 * 128:(f2 + 2) * 128, :]))
        conv_i = [0]

        def issue_conv(n):
            for _ in range(n):
                if conv_i[0] >= len(conv_jobs):
                    return
                o_ap, i_ap = conv_jobs[conv_i[0]]
                P.dma("pool", L("dma_start", out=o_ap, in_=i_ap), conv_dummy[conv_i[0] % 4])
                conv_i[0] += 1

        def mm_group(out_ap, pairs, reads, pb, last_signal=True):
            n = len(pairs)
            for k, (lh, rh) in enumerate(pairs):
                P.op("pe", L("matmul", out_ap, lhsT=lh, rhs=rh, start=(k == 0), stop=(k == n - 1)), reads, [pb],
                     signal=(last_signal and k == n - 1))

        for s in range(NSEQ):
            row0 = s * SEQ
            for b in range(NB):
                sl = b % 2
                P.dma("sp", L("dma_start", out=xt[sl].t[:], in_=x_d[row0 + b * 128:row0 + (b + 1) * 128, :]), xt[sl])
                rms_stats(xt[sl].t[:], xt[sl], ssq, b, xs[sl])
                to_feature_major(xt[sl].t[:], xt[sl], ssq.t[:, 2, b:b + 1], ssq, gbc[0], sl,
                                 uT.t[:, :, b * 128:(b + 1) * 128], uT, b)

            if s == 0:
                dump("uT", uT.t[:, :, :], uT, [128, 8, SEQ], BF16)
            P.barrier()
            for d_ in range(2):
                P.op("pool", L("memset", used[d_].t[:, :, :], 0.0), [], [used[d_]])
                P.op("pool", L("memset", G[d_].t[:, :, 0:1], 0.0), [], [G[d_]])
                P.op("pool", L("memset", mlt[d_].t[:, :], 0.0), [], [mlt[d_]])

            def proj_and_sig(h, wcur):
                sg_o = sog[h % 2]
                for tt in range(4):
                    tsl = slice(tt * 512, (tt + 1) * 512)
                    banks = []
                    for si in (0, 1, 2, 4):
                        pb = bank()
                        banks.append(pb)
                        mm_group(pb.t[:, :], [(wcur.t[:, si, c, :], uT.t[:, c, tsl]) for c in range(8)], [wcur, uT], pb)
                    pq, pf, pbk, pog = banks
                    for (pz, dst, k) in ((pq, q_s, 0), (pog, sg_o, 1)):
                        sg = sgq[k]
                        P.op("act", L("activation", out=sg.t[:, :], in_=pz.t[:, :], func=AF.Sigmoid), [pz], [sg])
                        P.op("dve", L("tensor_tensor", out=dst.t[:, tsl], in0=pz.t[:, :], in1=sg.t[:, :], op=ALU.mult), [pz, sg], [dst])
                    for d_, pz in ((0, pf), (1, pbk)):
                        gsl = G[d_].t[:, tt * 4:(tt + 1) * 4, 1:129]
                        col = d_ * 8 + h
                        P.op("act", L("activation", out=gsl, in_=v4(pz), func=AF.Sigmoid), [pz], [G[d_]])
                        P.op("dve", L("tensor_scalar", out=kT[d_].t[:, tsl].rearrange("p (a b) -> p a b", a=4), in0=gsl,
                                      scalar1=lbt.t[:, 3, col:col + 1], scalar2=lbt.t[:, 2, col:col + 1],
                                      op0=ALU.mult, op1=ALU.add), [G[d_], lbt], [kT[d_]])

            def decay(h):
                for d_ in range(2):
                    col = d_ * 8 + h
                    gin = G[d_].t[:, :, 1:129]
                    P.op("act", L("activation", out=gin, in_=gin, func=AF.Ln, scale=lbt.t[:, 2, col:col + 1],
                                  bias=lbt.t[:, 1, col:col + 1]), [G[d_], lbt], [G[d_]])
                for d_ in range(2):
                    gfl = G[d_].t[:, :, :].rearrange("p a b -> p (a b)")
                    P.op("dve", L("tensor_tensor_scan", out=gfl, data0=smask.t[:, :, :].rearrange("p a b -> p (a b)"), data1=gfl,
                                  initial=0.0, op0=ALU.mult, op1=ALU.add), [G[d_], smask], [G[d_]])
                for d_ in range(2):
                    qv = qd[d_].t[:, :].rearrange("p (a b) -> p a b", a=NB)
                    ev = eA[d_].t[:, :].rearrange("p (a b) -> p a b", a=NB)
                    if d_ == 0:
                        cq, sq_, sk_ = G[0].t[:, :, 1:129], 1.0, -1.0
                    else:
                        cq, sq_, sk_ = G[1].t[:, :, 0:128], -1.0, 1.0
                    P.op("act", L("activation", out=ev, in_=cq, func=AF.Exp, scale=sk_), [G[d_]], [eA[d_]])
                    P.op("act", L("activation", out=qv, in_=cq, func=AF.Exp, scale=sq_), [G[d_]], [qd[d_]])
                    P.op("act", L("activation", out=Ed[d_].t[:, :, :], in_=G[d_].t[:, :, 128:129], func=AF.Exp), [G[d_]], [Ed[d_]])
                    P.op("dve", L("tensor_tensor", out=kiT[d_].t[:, :], in0=kT[d_].t[:, :], in1=eA[d_].t[:, :], op=ALU.mult),
                         [kT[d_], eA[d_]], [kiT[d_]])
                    P.op("pool", L("tensor_tensor", out=qd[d_].t[:, :], in0=qd[d_].t[:, :], in1=q_s.t[:, :], op=ALU.mult),
                         [qd[d_], q_s], [qd[d_]])

            def vproj(h, wcur):
                for jg in range(4):
                    pb = bank()
                    for jj in range(4):
                        j = jg * 4 + jj
                        mm_group(v4(pb)[:, jj, :], [(uT.t[:, c, j * 128:(j + 1) * 128], wcur.t[:, 3, c, :]) for c in range(8)],
                                 [uT, wcur], pb, last_signal=(jj == 3))
                    P.op("act", L("activation", out=v_h.t[:, jg * 4:(jg + 1) * 4, :], in_=v4(pb), func=AF.Copy), [pb], [v_h])

            def rec(h):
                sg_o = sog[h % 2]
                for d_ in range(2):
                    for jg in range(2):
                        pb = bank()
                        for jj in range(8):
                            j = jg * 8 + jj
                            P.op("pe", L("transpose", vt(pb)[:, jj, :], kiT[d_].t[:, j * 128:(j + 1) * 128], ident),
                                 [kiT[d_], cb], [pb], signal=(jj == 7))
                        if jg == 0:
                            P.op("dve", L("tensor_copy", out=ki[d_].t[:, jg * 8:(jg + 1) * 8, :], in_=vt(pb)), [pb], [ki[d_]])
                        else:
                            P.op("act", L("activation", out=ki[d_].t[:, jg * 8:(jg + 1) * 8, :], in_=vt(pb), func=AF.Copy), [pb], [ki[d_]])
                for d_ in range(2):
                    order = list(range(NB)) if d_ == 0 else list(range(NB - 1, -1, -1))
                    if d_ == 0:
                        P.op("pool", L("tensor_copy", out=mlt[0].t[:, 0:15], in_=Ed[0].t[:, 0:15, 0]), [Ed[0]], [mlt[0]])
                    else:
                        P.op("pool", L("tensor_copy", out=mlt[1].t[:, 0:15], in_=Ed[1].t[:, 14::-1, 0]), [Ed[1]], [mlt[1]])
                    em3 = Emat.t[:, :].rearrange("p (v j) -> p v j", j=16)
                    P.op("pool", L("tensor_copy", out=em3, in_=mlt[d_].t[:, :].unsqueeze(1).broadcast_to([128, 64, 16])), [mlt[d_]], [Emat])
                    for g4 in range(4):
                        pb = bank()
                        for slot in range(4):
                            j = order[g4 * 4 + slot]
                            P.op("pe", L("matmul", v4(pb)[:, slot, :], lhsT=ki[d_].t[:, j, :], rhs=v_h.t[:, j, :], start=True, stop=True),
                                 [ki[d_], v_h], [pb], signal=(slot == 3))
                        for hf in range(2):
                            dst = xt[hf].t[:, :].rearrange("p (v j) -> p v j", j=16)[:, :, g4 * 4:(g4 + 1) * 4].rearrange("p v j -> p j v")
                            src = v4(pb)[:, :, hf * 64:(hf + 1) * 64]
                            if g4 % 2 == 0:
                                P.op("act", L("activation", out=dst, in_=src, func=AF.Copy), [pb], [xt[hf]])
                            else:
                                P.op("dve", L("tensor_copy", out=dst, in_=src), [pb], [xt[hf]])
                    for hf in range(2):
                        P.op("dve", L("tensor_tensor_scan", out=xt[hf].t[:, :], data0=xt[hf].t[:, :], data1=Emat.t[:, :], initial=0.0,
                                      op0=ALU.add, op1=ALU.mult), [Emat, xt[hf]], [xt[hf]])
                        w3 = xt[hf].t[:, :].rearrange("p (v j) -> p v j", j=16)
                        P.op("act", L("activation", out=used[d_].t[:, 1:16, hf * 64:(hf + 1) * 64],
                                      in_=w3[:, :, 0:15].rearrange("p v j -> p j v"), func=AF.Copy), [xt[hf]], [used[d_]])
                pscs = {}

                def scores(g4):
                    pscs[g4] = [bank(), bank()]
                    for jj in range(4):
                        j = g4 * 4 + jj
                        bsl = slice(j * 128, (j + 1) * 128)
                        psc = pscs[g4][jj // 2]
                        so = 2 * (jj % 2)
                        for d_ in range(2):
                            P.op("pe", L("matmul", v4(psc)[:, so + d_, :], lhsT=kiT[d_].t[:, bsl], rhs=qd[d_].t[:, bsl], start=True, stop=True),
                                 [kiT[d_], qd[d_]], [psc], signal=(jj % 2 == 1 and d_ == 1))

                pos = {}

                def omain(g4):
                    for b2 in range(2):
                        mk = msk[2 * (g4 % 2) + b2]
                        P.op("dve", L("tensor_tensor", out=mk.t[:, :, :], in0=v4(pscs[g4][b2]), in1=masks4, op=ALU.mult),
                             [pscs[g4][b2], cb], [mk])
                    po = bank()
                    pos[g4] = po
                    for jj in range(4):
                        j = g4 * 4 + jj
                        bsl = slice(j * 128, (j + 1) * 128)
                        mk = msk[2 * (g4 % 2) + jj // 2]
                        so = 2 * (jj % 2)
                        osl = slice(jj * 128, (jj + 1) * 128)
                        mm_group(po.t[:, osl], [(v_h.t[:, j, :], mk.t[:, so, :]), (v_h.t[:, j, :], mk.t[:, so + 1, :]),
                                                (used[0].t[:, j, :], qd[0].t[:, bsl]), (used[1].t[:, NB - 1 - j, :], qd[1].t[:, bsl])],
                                 [v_h, mk, used[0], used[1], qd[0], qd[1]], po)
                    P.op("act", L("activation", out=sq[g4 % 2].t[:, :], in_=po.t[:, :], func=AF.Square), [po], [sq[g4 % 2]])

                def otail(g4):
                    po = pos[g4]
                    tsl = slice(g4 * 512, (g4 + 1) * 512)
                    pm = bank()
                    P.op("pe", L("matmul", pm.t[:, :], lhsT=onesm, rhs=sq[g4 % 2].t[:, :], start=True, stop=True), [cb, sq[g4 % 2]], [pm])
                    P.op("act", L("activation", out=lnr.t[:, :], in_=pm.t[:, :], func=AF.Ln, bias=EPS), [pm], [lnr])
                    P.op("act", L("activation", out=lnr.t[:, :], in_=lnr.t[:, :], func=AF.Exp, scale=-0.5), [lnr], [lnr])
                    P.op("dve", L("tensor_tensor", out=lnr.t[:, :], in0=po.t[:, :], in1=lnr.t[:, :], op=ALU.mult), [po, lnr], [lnr])
                    P.op("dve", L("scalar_tensor_tensor", out=yaT.t[:, h, tsl], in0=lnr.t[:, :], scalar=cols.t[:, h:h + 1], in1=sg_o.t[:, tsl],
                                  op0=ALU.mult, op1=ALU.mult), [lnr, cols, sg_o], [yaT])

                scores(0)
                scores(1)
                omain(0)
                for g4 in range(1, 4):
                    if g4 + 1 < 4:
                        scores(g4 + 1)
                    omain(g4)
                    otail(g4 - 1)
                otail(3)

            whs = {0: load_head(0)}
            proj_and_sig(0, whs[0])
            for h in range(8):
                if h + 1 < 8:
                    whs[h + 1] = load_head(h + 1)
                if s == 0:
                    issue_conv(7)
                decay(h)
                vproj(h, whs[h])
                if h + 1 < 8:
                    proj_and_sig(h + 1, whs[h + 1])
                rec(h)

            if s == 0:
                dump("yaT", yaT.t[:, :, :], yaT, [128, 8, SEQ], BF16)
            P.barrier()
            P.dma("pool", L("dma_start", out=wpool.t[:, :, :, :], in_=pw_d.rearrange("g (k p) n -> p g k n", p=128)), wpool)
            for tt in range(4):
                tsl = slice(tt * 512, (tt + 1) * 512)
                blks = [jb for jb in range(4 * tt - 1, 4 * tt + 5) if 0 <= jb < NB]
                slot_of = {jb: i for i, jb in enumerate(blks)}
                for half in range(2):
                    wsl = load_slab(half)
                    for k, jb in enumerate(blks):
                        pb = bank()
                        mm_group(pb.t[:, :], [(uT.t[:, c, jb * 128:(jb + 1) * 128], wsl.t[:, c, :]) for c in range(8)], [uT, wsl], pb)
                        dst = pblk.t[:, slot_of[jb], half * 512:(half + 1) * 512]
                        if k % 2 == 0:
                            P.op("act", L("activation", out=dst, in_=pb.t[:, :], func=AF.Copy), [pb], [pblk])
                        else:
                            P.op("dve", L("tensor_copy", out=dst, in_=pb.t[:, :]), [pb], [pblk])
                for c in range(8):
                    g = c // 2
                    pb = bank()
                    for jj in range(4):
                        j = 4 * tt + jj
                        srcs = []
                        if j - 1 >= 0:
                            srcs.append((j - 1, 0))
                        srcs.append((j, 3 if j == 0 else (4 if j == NB - 1 else 1)))
                        if j + 1 < NB:
                            srcs.append((j + 1, 2))
                        mm_group(pb.t[:, jj * 128:(jj + 1) * 128],
                                 [(pblk.t[:, slot_of[jb], c * 128:(c + 1) * 128], pmat(g, kind)) for (jb, kind) in srcs],
                                 [pblk, cb], pb, last_signal=(jj == 3))
                    if c % 2 == 0:
                        P.op("act", L("activation", out=yT.t[:, c, :], in_=pb.t[:, :], func=AF.Copy), [pb], [yT])
                    else:
                        P.op("dve", L("tensor_copy", out=yT.t[:, c, :], in_=pb.t[:, :]), [pb], [yT])
                for c2 in range(8):
                    g, hf = c2 // 2, c2 % 2
                    pb = bank()
                    mm_group(pb.t[:, :], [(wpool.t[:, g, kc, hf * 128:(hf + 1) * 128], yT.t[:, 2 * g + kc, :]) for kc in range(2)], [wpool, yT], pb)
                    P.op("dve", L("tensor_scalar", out=ybT.t[:, c2, :], in0=pb.t[:, :], scalar1=cols.t[:, 8 + c2:9 + c2], scalar2=None,
                                  op0=ALU.mult), [pb, cols], [ybT])
                if s == 0 and tt == 0:
                    dump("ybT", ybT.t[:, :, :], ybT, [128, 8, 512], BF16)
                for gi, (gbuf, seg) in enumerate(((sga, 6), (sgb, 7))):
                    for half in range(2):
                        wsl = load_slab((seg - 5) * 2 + half)
                        for cc in range(4):
                            pb = bank()
                            mm_group(pb.t[:, :], [(wsl.t[:, c, cc * 128:(cc + 1) * 128], uT.t[:, c, tsl]) for c in range(8)], [wsl, uT], pb)
                            P.op("act", L("activation", out=gbuf.t[:, half * 4 + cc, :], in_=pb.t[:, :], func=AF.Sigmoid), [pb], [gbuf])
                for half in range(2):
                    wsa = load_slab(6 + half)
                    wsb = load_slab(8 + half)
                    for cc in range(4):
                        dc = half * 4 + cc
                        pa = bank()
                        pb2 = bank()
                        mm_group(pa.t[:, :], [(wsa.t[:, c, cc * 128:(cc + 1) * 128], yaT.t[:, c, tsl]) for c in range(8)], [wsa, yaT], pa)
                        mm_group(pb2.t[:, :], [(wsb.t[:, c, cc * 128:(cc + 1) * 128], ybT.t[:, c, :]) for c in range(8)], [wsb, ybT], pb2)
                        P.op("dve", L("tensor_tensor", out=tz[0].t[:, :], in0=pa.t[:, :], in1=sga.t[:, dc, :], op=ALU.mult), [pa, sga], [tz[0]])
                        P.op("dve", L("tensor_tensor", out=tz[1].t[:, :], in0=pb2.t[:, :], in1=sgb.t[:, dc, :], op=ALU.mult), [pb2, sgb], [tz[1]])
                        P.op("pool", L("tensor_tensor", out=mT.t[:, dc, :], in0=tz[0].t[:, :], in1=tz[1].t[:, :], op=ALU.add), [tz[0], tz[1]], [mT])
                if s == 0 and tt == 0:
                    dump("sga", sga.t[:, :, :], sga, [128, 8, 512], BF16)
                    dump("mT", mT.t[:, :, :], mT, [128, 8, 512], BF16)
                wso = [load_slab(10 + half) for half in range(2)]
                def blk_mm(jj):
                    j = 4 * tt + jj
                    sl = jj % 2
                    P.dma("sp", L("dma_start", out=xt[sl].t[:], in_=x_d[row0 + j * 128:row0 + (j + 1) * 128, :]), xt[sl])
                    for half in range(2):
                        pb = bank()
                        mm_group(pb.t[:, :], [(mT.t[:, c, jj * 128:(jj + 1) * 128], wso[half].t[:, c, :]) for c in range(8)], [mT, wso[half]], pb)
                        P.op("dve", L("tensor_tensor", out=h2.t[:, jj, half * 512:(half + 1) * 512], in0=pb.t[:, :],
                                      in1=xt[sl].t[:, half * 512:(half + 1) * 512], op=ALU.add), [pb, xt[sl]], [h2])
                    rms_stats(h2.t[:, jj, :], h2, ss2, jj, xs[sl])
                    fm_prep(h2.t[:, jj, :], h2, ss2.t[:, 2, jj:jj + 1], ss2, gbc[1], sl)

                def blk_tr(jj):
                    fm_transpose(jj % 2, u2T.t[:, :, jj * 128:(jj + 1) * 128], u2T, jj)

                blk_mm(0)
                for jj in range(1, 4):
                    blk_mm(jj)
                    blk_tr(jj - 1)
                blk_tr(3)
                if s == 0 and tt == 0:
                    dump("h2", h2.t[:, :, :], h2, [128, 4, D], F32)
                    dump("u2T", u2T.t[:, :, :], u2T, [128, 8, 512], BF16)
                for fc in range(NFC):
                    wg = wgu[fc % 2]
                    P.dma("sp", L("dma_start", out=wg.t[:, :, :, :], in_=s_w1[fc]), wg)
                    pg = bank()
                    pu = bank()
                    mm_group(pg.t[:, :], [(wg.t[:, 0, c, :], u2T.t[:, c, :]) for c in range(8)], [wg, u2T], pg)
                    mm_group(pu.t[:, :], [(wg.t[:, 1, c, :], u2T.t[:, c, :]) for c in range(8)], [wg, u2T], pu)
                    sg = sgt[fc % 2]
                    P.op("act", L("activation", out=sg.t[:, :], in_=pg.t[:, :], func=AF.Silu), [pg], [sg])
                    P.op("dve", L("tensor_tensor", out=actT.t[:, fc, :], in0=pu.t[:, :], in1=sg.t[:, :], op=ALU.mult), [pu, sg], [actT])
                if s == 0 and tt == 0:
                    dump("actT", actT.t[:, :, :], actT, [128, NFC, 512], BF16)
                for pp in range(2):
                    jjs = (2 * pp, 2 * pp + 1)
                    obanks = {jj: [bank(), bank()] for jj in jjs}
                    for fc in range(NFC):
                        w2 = w2s[(pp * NFC + fc) % 4]
                        P.dma("sp", L("dma_start", out=w2.t[:, :], in_=s_w2[fc * 128:(fc + 1) * 128, :]), w2)
                        for jj in jjs:
                            for half in range(2):
                                pb = obanks[jj][half]
                                P.op("pe", L("matmul", pb.t[:, :], lhsT=actT.t[:, fc, jj * 128:(jj + 1) * 128], rhs=w2.t[:, half * 512:(half + 1) * 512],
                                             start=(fc == 0), stop=(fc == NFC - 1)), [actT, w2], [pb],
                                     signal=(fc == NFC - 1) or (jj == jjs[1] and half == 1))
                    for jj in jjs:
                        j = 4 * tt + jj
                        for half in range(2):
                            pb = obanks[jj][half]
                            P.op("dve", L("tensor_tensor", out=h2.t[:, jj, half * 512:(half + 1) * 512], in0=pb.t[:, :],
                                          in1=h2.t[:, jj, half * 512:(half + 1) * 512], op=ALU.add), [pb, h2], [h2])
                        rms_stats(h2.t[:, jj, :], h2, ss3, jj, xs[jj % 2])
                        ob = ost[jj % 2]
                        P.op("dve", L("scalar_tensor_tensor", out=ob.t[:, :], in0=h2.t[:, jj, :], scalar=ss3.t[:, 2, jj:jj + 1], in1=gbc[2].t[:, :],
                                      op0=ALU.mult, op1=ALU.mult), [h2, ss3, gbc[2]], [ob])
                        P.dma("sp", L("dma_start", out=out_d[row0 + j * 128:row0 + (j + 1) * 128, :], in_=ob.t[:, :]), ob, is_load=False)
        P.wait_all("sp", list(ost) + dbg_bufs)

        sems_eng = {k: es.enter_context(nc.semaphore("sem_" + k)) for k in ["pe", "act", "dve", "pool"]}
        sems_dma = [es.enter_context(nc.semaphore("dsem%d" % i)) for i in range(P.ndma)]
        with nc.Block() as block:
            P.emit(block, sems_eng, sems_dma)
    return nc


_NC_CACHE = {}


def kernel(x, g_mix, w_in, lb_logits, hgrn_norm_g, pool_w, pool_scale, w_branch_a, w_branch_b, w_out,
           g_ffn, w_ffn_in, w_ffn_out, g_final):
    f32 = np.float32
    x = np.asarray(x, f32)
    B = x.shape[0]
    xs_ = x.reshape(NCORES, NSEQ * SEQ, D)
    gvec = np.ascontiguousarray(np.stack([np.asarray(g_mix, f32)[0], np.asarray(g_ffn, f32)[0], np.asarray(g_final, f32)]))
    cols = np.zeros((128, 48), f32)
    cols[:, 0:8] = np.asarray(hgrn_norm_g, f32)[0].reshape(8, 128).T
    cols[:, 8:16] = np.asarray(pool_scale, f32)[0].reshape(8, 128).T
    lbl = np.asarray(lb_logits, f32)
    cols[:, 16:48] = lbl.reshape(2, 2, 8, 128).transpose(3, 0, 1, 2).reshape(128, 32)
    shared = {
        "w_in": np.ascontiguousarray(np.asarray(w_in, f32)[0]),
        "w_a": np.ascontiguousarray(np.asarray(w_branch_a, f32)[0]),
        "w_b": np.ascontiguousarray(np.asarray(w_branch_b, f32)[0]),
        "w_o": np.ascontiguousarray(np.asarray(w_out, f32)[0]),
        "pool_w": np.ascontiguousarray(np.asarray(pool_w, f32)[0]),
        "w_ffn_in": np.ascontiguousarray(np.asarray(w_ffn_in, f32)[0]),
        "w_ffn_out": np.ascontiguousarray(np.asarray(w_ffn_out, f32)[0]),
        "gvec": gvec,
        "cols": cols,
        "cb": _const_bf16(),
    }
    if "nc" not in _NC_CACHE:
        _NC_CACHE["nc"] = build_program()
    nc = _NC_CACHE["nc"]
    in_maps = []
    for c in range(NCORES):
        m = dict(shared)
        m["x"] = np.ascontiguousarray(xs_[c])
        in_maps.append(m)
    res = run_bass_kernel_spmd(nc, in_maps, core_ids=list(range(NCORES)))
    out = np.stack([np.asarray(r["out"], f32) for r in res.results], axis=0)
    return out.reshape(B, SEQ, D)
```

```python
import numpy as np
import ml_dtypes
from contextlib import ExitStack
import concourse.bass as bass
import concourse.mybir as mybir
from concourse.bass_utils import run_bass_kernel_spmd

F32 = mybir.dt.float32
BF16 = mybir.dt.bfloat16
AF = mybir.ActivationFunctionType
ALU = mybir.AluOpType

NCORES = 8
D = 1024
SEQ = 2048
NSEQ = 2
NB = SEQ // 128
DFF = 2816
NFC = DFF // 128
EPS = 1e-6
ENGS = ["pe", "act", "dve", "pool", "sp"]
DEBUG = False


class St:
    __slots__ = ("w", "r", "dsem", "dcount")

    def __init__(self):
        self.w = {}
        self.r = {}
        self.dsem = None
        self.dcount = 0


class Buf:
    def __init__(self, t, st=None):
        self.t = t
        self.st = st if st is not None else St()


class Item:
    __slots__ = ("waits", "fn", "inc")

    def __init__(self, waits, fn, inc):
        self.waits = waits
        self.fn = fn
        self.inc = inc


class Prog:
    def __init__(self):
        self.q = {e: [] for e in ENGS}
        self.cnt = {e: 0 for e in ENGS}
        self.seen = {e: {} for e in ENGS}
        self.ndma = 0
        self.dsts = []

    def barrier(self):
        for eng in ENGS:
            waits = []
            seen = self.seen[eng]
            for f in ("pe", "act", "dve", "pool"):
                if f != eng and self.cnt[f] > seen.get(f, 0):
                    seen[f] = self.cnt[f]
                    waits.append((f, self.cnt[f]))
            for st in self.dsts:
                key = ("dma", st.dsem)
                if st.dcount > seen.get(key, 0):
                    seen[key] = st.dcount
                    waits.append((key, st.dcount))
            if waits:
                self.q[eng].append(Item(waits, None, None))

    def _waits(self, eng, reads, writes):
        need = {}

        def add(key, val):
            if val > need.get(key, 0):
                need[key] = val

        for st in reads:
            for k, c in st.w.items():
                add(k, c)
        for st in writes:
            for k, c in st.w.items():
                if k != eng:
                    add(k, c)
            for k, c in st.r.items():
                if k != eng:
                    add(k, c)
        if eng == "pe":
            need.pop("pe", None)
        out = []
        seen = self.seen[eng]
        for key, val in need.items():
            if seen.get(key, 0) < val:
                seen[key] = val
                out.append((key, val))
        return out

    def op(self, eng, fn, reads=(), writes=(), signal=True):
        reads = [b.st for b in reads]
        writes = [b.st for b in writes]
        waits = self._waits(eng, reads, writes)
        if signal:
            self.cnt[eng] += 1
            c = self.cnt[eng]
            inc = (eng, 1)
        else:
            c = self.cnt[eng] + 1
            inc = None
        for st in writes:
            st.w = {eng: c}
            st.r = {}
        for st in reads:
            st.r[eng] = c
        self.q[eng].append(Item(waits, fn, inc))

    def dma(self, queue, fn, buf, is_load=True):
        st = buf.st
        if st.dsem is None:
            st.dsem = self.ndma
            self.ndma += 1
            self.dsts.append(st)
        if is_load:
            own = ("dma", st.dsem)
            prev_load = st.w.pop(own, None)
            waits = self._waits(queue, [], [st])
            if prev_load is not None:
                st.w[own] = prev_load
        else:
            waits = self._waits(queue, [st], [])
        st.dcount += 16
        key = ("dma", st.dsem)
        if is_load:
            st.w = {key: st.dcount}
            st.r = {}
        else:
            st.r[key] = st.dcount
        self.q[queue].append(Item(waits, fn, (key, 16)))

    def wait_all(self, eng, bufs):
        waits = self._waits(eng, [], [b.st for b in bufs])
        if waits:
            self.q[eng].append(Item(waits, None, None))

    def emit(self, block, sems_eng, sems_dma):
        def semof(key):
            if isinstance(key, tuple):
                return sems_dma[key[1]]
            return sems_eng[key]

        def body(engname):
            def _f(e):
                for it in self.q[engname]:
                    for key, val in it.waits:
                        e.wait_ge(semof(key), val)
                    if it.fn is not None:
                        ins = it.fn(e)
                        if it.inc is not None:
                            ins.then_inc(semof(it.inc[0]), it.inc[1])
            return _f

        block.tensor(body("pe"))
        block.scalar(body("act"))
        block.vector(body("dve"))
        block.gpsimd(body("pool"))
        block.sync(body("sp"))


def _pool_mats():
    wins = (2, 4, 8, 16)
    L = SEQ
    mats = np.zeros((128, 20, 128), np.float32)
    t = np.arange(L)
    for g, w in enumerate(wins):
        half = w // 2
        lo = np.clip(t - half + 1, 0, L)
        hi = np.clip(t + half + 1, 0, L)
        Pm = np.zeros((L, L), np.float32)
        for tt in range(L):
            Pm[tt, lo[tt]:hi[tt]] = 1.0 / float(hi[tt] - lo[tt])
        Pm -= np.eye(L, dtype=np.float32)

        def blk(tb, sb):
            return Pm[tb * 128:(tb + 1) * 128, sb * 128:(sb + 1) * 128].T

        mats[:, g * 5 + 0, :] = blk(5, 4)
        mats[:, g * 5 + 1, :] = blk(5, 5)
        mats[:, g * 5 + 2, :] = blk(5, 6)
        mats[:, g * 5 + 3, :] = blk(0, 0)
        mats[:, g * 5 + 4, :] = blk(NB - 1, NB - 1)
    return mats


def _const_bf16():
    cb = np.zeros((128, 26, 128), np.float32)
    cb[:, 0, :] = np.eye(128)
    cb[:, 1, :] = 1.0 / 128.0
    s = np.arange(128)[:, None]
    t = np.arange(128)[None, :]
    cb[:, 2, :] = (s <= t)
    cb[:, 3, :] = (s >= t)
    cb[:, 4, :] = (s <= t)
    cb[:, 5, :] = (s >= t)
    cb[:, 6:26, :] = _pool_mats()
    return cb.reshape(128, 26 * 128).astype(ml_dtypes.bfloat16)


def build_program():
    nc = bass.Bass("TRN2", target_bir_lowering=False)
    x_d = nc.dram_tensor("x", [NSEQ * SEQ, D], F32, kind="ExternalInput").ap()
    win_d = nc.dram_tensor("w_in", [D, 8 * D], F32, kind="ExternalInput").ap()
    wa_d = nc.dram_tensor("w_a", [D, D], F32, kind="ExternalInput").ap()
    wb_d = nc.dram_tensor("w_b", [D, D], F32, kind="ExternalInput").ap()
    wo_d = nc.dram_tensor("w_o", [D, D], F32, kind="ExternalInput").ap()
    pw_d = nc.dram_tensor("pool_w", [4, 256, 256], F32, kind="ExternalInput").ap()
    w1_d = nc.dram_tensor("w_ffn_in", [D, 2 * DFF], F32, kind="ExternalInput").ap()
    w2_d = nc.dram_tensor("w_ffn_out", [DFF, D], F32, kind="ExternalInput").ap()
    gv_d = nc.dram_tensor("gvec", [3, D], F32, kind="ExternalInput").ap()
    cols_d = nc.dram_tensor("cols", [128, 48], F32, kind="ExternalInput").ap()
    cb_d = nc.dram_tensor("cb", [128, 26 * 128], BF16, kind="ExternalInput").ap()
    out_d = nc.dram_tensor("out", [NSEQ * SEQ, D], F32, kind="ExternalOutput").ap()
    s_slab = nc.dram_tensor("s_slab", [12, 128, 8, 512], BF16).ap()
    s_w1 = nc.dram_tensor("s_w1", [NFC, 128, 2, 8, 128], BF16).ap()
    s_w2 = nc.dram_tensor("s_w2", [DFF, D], BF16).ap()

    P = Prog()
    with ExitStack() as es:
        def sb(name, shape, dt):
            return Buf(es.enter_context(nc.sbuf_tensor("sb_" + name, shape, dt)))

        gbc = [sb("gbc%d" % i, [128, D], BF16 if i < 2 else F32) for i in range(3)]
        cols = sb("cols", [128, 48], F32)
        cb = sb("cb", [128, 26, 128], BF16)
        lbt = sb("lbt", [128, 5, 16], F32)
        uT = sb("uT", [128, 8, SEQ], BF16)
        yaT = sb("yaT", [128, 8, SEQ], BF16)
        ident = cb.t[:, 0, :]
        onesm = cb.t[:, 1, :]
        masks4 = cb.t[:, 2:6, :]

        def pmat(g, kind):
            return cb.t[:, 6 + g * 5 + kind, :]

        PSB = [Buf(es.enter_context(nc.psum_tensor("psb%d" % i, [128, 512], F32))) for i in range(8)]
        ps_i = [0]

        def bank():
            b = PSB[ps_i[0] % 8]
            ps_i[0] += 1
            return b

        def v4(b):
            return b.t[:, :].rearrange("p (a b) -> p a b", a=4)

        def vt(b):
            return b.t[:, :].bitcast(BF16).rearrange("p (a b) -> p a b", a=8)

        xt = [sb("xt%d" % i, [128, D], F32) for i in range(2)]
        xs = [sb("xs%d" % i, [128, D], BF16) for i in range(2)]
        smask = sb("smask", [128, NB, 129], BF16)
        ssq = sb("ssq", [128, 3, 16], F32)
        ss2 = sb("ss2", [128, 3, 4], F32)
        ss3 = sb("ss3", [128, 3, 4], F32)

        ARENA = 111104
        AR = es.enter_context(nc.sbuf_tensor("AR", [128, ARENA // 2], BF16))
        cur = [0]

        def sb(name, shape, dt):
            n = 1
            for k in shape[1:]:
                n *= k
            nbytes = n * (4 if dt == F32 else 2)
            off = cur[0]
            cur[0] += (nbytes + 3) // 4 * 4
            assert cur[0] <= ARENA, (name, cur[0])
            ap = AR[:, off // 2:(off + nbytes) // 2]
            if dt == F32:
                ap = ap.bitcast(F32)
            if len(shape) == 3:
                ap = ap.rearrange("p (a b) -> p a b", a=shape[1])
            elif len(shape) == 4:
                ap = ap.rearrange("p (a b c) -> p a b c", a=shape[1], b=shape[2])
            return Buf(ap)

        wh = [sb("wh%d" % i, [128, 5, 8, 128], BF16) for i in range(2)]
        sgq = [sb("sgq%d" % i, [128, 512], BF16) for i in range(2)]
        q_s = sb("q_s", [128, SEQ], BF16)
        sog = [sb("sog%d" % i, [128, SEQ], BF16) for i in range(2)]
        kT = [sb("kT%d" % i, [128, SEQ], BF16) for i in range(2)]
        G = [sb("G%d" % i, [128, NB, 129], F32) for i in range(2)]
        qd = [sb("qd%d" % i, [128, SEQ], BF16) for i in range(2)]
        kiT = [sb("kiT%d" % i, [128, SEQ], BF16) for i in range(2)]
        eA0 = sb("eA", [128, SEQ], BF16)
        eA = [eA0, eA0]
        ki = [sb("ki%d" % i, [128, NB, 128], BF16) for i in range(2)]
        v_h = sb("v_h", [128, NB, 128], BF16)
        used = [sb("used%d" % i, [128, NB, 128], BF16) for i in range(2)]
        Ed = [sb("Ed%d" % i, [128, NB, 1], F32) for i in range(2)]
        Emat = sb("Emat", [128, 1024], F32)
        mlt = [sb("mlt%d" % i, [128, 16], F32) for i in range(2)]
        msk = [sb("msk%d" % i, [128, 4, 128], BF16) for i in range(4)]
        sq = sgq
        lnr = sb("lnr", [128, 512], F32)

        print("arena phase2 bytes", cur[0])
        cur[0] = 0
        wpool = sb("wpool", [128, 4, 2, 256], BF16)
        WA = [sb("WA%d" % i, [128, 8, 512], BF16) for i in range(3)]
        wgu = [sb("wgu%d" % i, [128, 2, 8, 128], BF16) for i in range(2)]
        w2s = [sb("w2s%d" % i, [128, D], BF16) for i in range(4)]
        R1 = sb("R1", [128, 12288], BF16).t
        pblk = Buf(R1[:, 0:6144].rearrange("p (a b) -> p a b", a=6))
        yT = Buf(R1[:, 6144:10240].rearrange("p (a b) -> p a b", a=8))
        ybT = Buf(R1[:, 0:4096].rearrange("p (a b) -> p a b", a=8))
        sga = Buf(R1[:, 4096:8192].rearrange("p (a b) -> p a b", a=8))
        sgb = Buf(R1[:, 8192:12288].rearrange("p (a b) -> p a b", a=8))
        actT = Buf(R1[:, 0:11264].rearrange("p (a b) -> p a b", a=NFC))
        h2 = sb("h2", [128, 4, D], F32)
        mT = sb("mT", [128, 8, 512], BF16)
        u2T = sb("u2T", [128, 8, 512], BF16)
        ost = xt
        tz = [sb("tz%d" % i, [128, 512], BF16) for i in range(2)]
        sgt = [sb("sgt%d" % i, [128, 512], BF16) for i in range(2)]
        print("arena phase3 bytes", cur[0])

        dbg_bufs = []

        def dump(name, ap, buf, shape, dt):
            if not DEBUG:
                return
            dd = nc.dram_tensor("dbg_" + name, list(shape), dt, kind="ExternalOutput").ap()
            P.dma("sp", (lambda dd, ap: lambda e: e.dma_start(out=dd, in_=ap))(dd, ap), buf, is_load=False)
            dbg_bufs.append(buf)

        def L(meth, *a, **kw):
            return lambda e: getattr(e, meth)(*a, **kw)

        for i in range(3):
            P.dma("pool" if i < 2 else "sp", L("dma_start", out=gbc[i].t[:], in_=gv_d[i].partition_broadcast(128)), gbc[i])
        P.dma("sp", L("dma_start", out=cols.t[:], in_=cols_d), cols)
        P.dma("sp", L("dma_start", out=cb.t[:], in_=cb_d.rearrange("p (a b) -> p a b", a=26)), cb)
        P.op("pool", L("memset", smask.t[:, :, :], 1.0), [], [smask])
        P.op("pool", L("memset", smask.t[:, :, 0:1], 0.0), [], [smask])
        for a in range(2):
            P.op("dve", L("tensor_tensor", out=lbt.t[:, 0, a * 8:(a + 1) * 8],
                          in0=cols.t[:, 16 + (a * 2 + 1) * 8:16 + (a * 2 + 2) * 8],
                          in1=cols.t[:, 16 + (a * 2) * 8:16 + (a * 2 + 1) * 8], op=ALU.subtract), [cols], [lbt])
        P.op("act", L("activation", out=lbt.t[:, 4, :], in_=lbt.t[:, 0, :], func=AF.Exp), [lbt], [lbt])
        P.op("dve", L("tensor_scalar_add", out=lbt.t[:, 0, :], in0=lbt.t[:, 4, :], scalar1=1.0), [lbt], [lbt])
        P.op("dve", L("reciprocal", out=lbt.t[:, 1, :], in_=lbt.t[:, 0, :]), [lbt], [lbt])
        P.op("dve", L("tensor_scalar", out=lbt.t[:, 2, :], in0=lbt.t[:, 1, :], scalar1=-1.0, scalar2=1.0,
                      op0=ALU.mult, op1=ALU.add), [lbt], [lbt])
        P.op("dve", L("tensor_scalar_add", out=lbt.t[:, 3, :], in0=lbt.t[:, 1, :], scalar1=-1.0), [lbt], [lbt])

        def rms_stats(src_ap, src_buf, ssb, col, junk):
            P.op("act", L("activation", out=junk.t[:], in_=src_ap, func=AF.Square,
                          accum_out=ssb.t[:, 0, col:col + 1]), [src_buf], [junk, ssb])
            P.op("act", L("activation", out=ssb.t[:, 1, col:col + 1], in_=ssb.t[:, 0, col:col + 1], func=AF.Ln,
                          scale=1.0 / D, bias=EPS), [ssb], [ssb])
            P.op("act", L("activation", out=ssb.t[:, 2, col:col + 1], in_=ssb.t[:, 1, col:col + 1], func=AF.Exp,
                          scale=-0.5), [ssb], [ssb])

        def fm_prep(src_ap, src_buf, rstd_ap, rstd_buf, gb, slot):
            P.op("dve", L("scalar_tensor_tensor", out=xs[slot].t[:], in0=src_ap, scalar=rstd_ap, in1=gb.t[:],
                          op0=ALU.mult, op1=ALU.mult), [src_buf, rstd_buf, gb], [xs[slot]])

        def fm_transpose(slot, dst_ap, dst_buf, k):
            pb = bank()
            for c in range(8):
                P.op("pe", L("transpose", vt(pb)[:, c, :], xs[slot].t[:, c * 128:(c + 1) * 128], ident),
                     [xs[slot], cb], [pb], signal=(c == 7))
            if k % 2 == 0:
                P.op("act", L("activation", out=dst_ap, in_=vt(pb), func=AF.Copy), [pb], [dst_buf])
            else:
                P.op("dve", L("tensor_copy", out=dst_ap, in_=vt(pb)), [pb], [dst_buf])

        def to_feature_major(src_ap, src_buf, rstd_ap, rstd_buf, gb, slot, dst_ap, dst_buf, k):
            fm_prep(src_ap, src_buf, rstd_ap, rstd_buf, gb, slot)
            fm_transpose(slot, dst_ap, dst_buf, k)

        wa_i = [0]

        def load_slab(idx):
            b = WA[wa_i[0] % 3]
            wa_i[0] += 1
            P.dma("sp", L("dma_start", out=b.t[:, :, :], in_=s_slab[idx]), b)
            return b

        def load_head(h):
            b = wh[h % 2]
            for seg in range(5):
                P.dma("pool", L("dma_start", out=b.t[:, seg, :, :],
                                in_=win_d[:, seg * D + h * 128:seg * D + (h + 1) * 128].rearrange("(c p) n -> p c n", p=128)), b)
            return b

        conv_dummy = [Buf(None) for _ in range(4)]
        conv_jobs = []
        for c in range(8):
            rows = slice(c * 128, (c + 1) * 128)
            conv_jobs.append((s_slab[0:6, :, c, :].rearrange("s p n -> p s n"),
                              win_d[rows, 5 * D:8 * D].rearrange("p (s n) -> p s n", n=512)))
            for mi, wd in enumerate((wa_d, wb_d, wo_d)):
                conv_jobs.append((s_slab[6 + 2 * mi:8 + 2 * mi, :, c, :].rearrange("s p n -> p s n"),
                                  wd[rows, :].rearrange("p (s n) -> p s n", n=512)))
            conv_jobs.append((s_w1[:, :, :, c, :].rearrange("f p k n -> p k f n"),
                              w1_d[rows, :].rearrange("p (k f n) -> p k f n", k=2, n=128)))
        for f2 in range(0, NFC, 2):
            conv_jobs.append((s_w2[f2 * 128:(f2 + 2) * 128, :], w2_d[f2 * 128:(f2 + 2) * 128, :]))
        conv_i = [0]

        def issue_conv(n):
            for _ in range(n):
                if conv_i[0] >= len(conv_jobs):
                    return
                o_ap, i_ap = conv_jobs[conv_i[0]]
                P.dma("pool", L("dma_start", out=o_ap, in_=i_ap), conv_dummy[conv_i[0] % 4])
                conv_i[0] += 1

        def mm_group(out_ap, pairs, reads, pb, last_signal=True):
            n = len(pairs)
            for k, (lh, rh) in enumerate(pairs):
                P.op("pe", L("matmul", out_ap, lhsT=lh, rhs=rh, start=(k == 0), stop=(k == n - 1)), reads, [pb],
                     signal=(last_signal and k == n - 1))

        for s in range(NSEQ):
            row0 = s * SEQ
            for b in range(NB):
                sl = b % 2
                P.dma("sp", L("dma_start", out=xt[sl].t[:], in_=x_d[row0 + b * 128:row0 + (b + 1) * 128, :]), xt[sl])
                rms_stats(xt[sl].t[:], xt[sl], ssq, b, xs[sl])
                to_feature_major(xt[sl].t[:], xt[sl], ssq.t[:, 2, b:b + 1], ssq, gbc[0], sl,
                                 uT.t[:, :, b * 128:(b + 1) * 128], uT, b)

            if s == 0:
                dump("uT", uT.t[:, :, :], uT, [128, 8, SEQ], BF16)
            P.barrier()
            for d_ in range(2):
                P.op("pool", L("memset", used[d_].t[:, :, :], 0.0), [], [used[d_]])
                P.op("pool", L("memset", G[d_].t[:, :, 0:1], 0.0), [], [G[d_]])
                P.op("pool", L("memset", mlt[d_].t[:, :], 0.0), [], [mlt[d_]])

            def proj_and_sig(h, wcur):
                sg_o = sog[h % 2]
                for tt in range(4):
                    tsl = slice(tt * 512, (tt + 1) * 512)
                    banks = []
                    for si in (0, 1, 2, 4):
                        pb = bank()
                        banks.append(pb)
                        mm_group(pb.t[:, :], [(wcur.t[:, si, c, :], uT.t[:, c, tsl]) for c in range(8)], [wcur, uT], pb)
                    pq, pf, pbk, pog = banks
                    for (pz, dst, k) in ((pq, q_s, 0), (pog, sg_o, 1)):
                        sg = sgq[k]
                        P.op("act", L("activation", out=sg.t[:, :], in_=pz.t[:, :], func=AF.Sigmoid), [pz], [sg])
                        P.op("dve", L("tensor_tensor", out=dst.t[:, tsl], in0=pz.t[:, :], in1=sg.t[:, :], op=ALU.mult), [pz, sg], [dst])
                    for d_, pz in ((0, pf), (1, pbk)):
                        gsl = G[d_].t[:, tt * 4:(tt + 1) * 4, 1:129]
                        col = d_ * 8 + h
                        P.op("act", L("activation", out=gsl, in_=v4(pz), func=AF.Sigmoid), [pz], [G[d_]])
                        P.op("dve", L("tensor_scalar", out=kT[d_].t[:, tsl].rearrange("p (a b) -> p a b", a=4), in0=gsl,
                                      scalar1=lbt.t[:, 3, col:col + 1], scalar2=lbt.t[:, 2, col:col + 1],
                                      op0=ALU.mult, op1=ALU.add), [G[d_], lbt], [kT[d_]])

            def decay(h):
                for d_ in range(2):
                    col = d_ * 8 + h
                    gin = G[d_].t[:, :, 1:129]
                    P.op("act", L("activation", out=gin, in_=gin, func=AF.Ln, scale=lbt.t[:, 2, col:col + 1],
                                  bias=lbt.t[:, 1, col:col + 1]), [G[d_], lbt], [G[d_]])
                for d_ in range(2):
                    gfl = G[d_].t[:, :, :].rearrange("p a b -> p (a b)")
                    P.op("dve", L("tensor_tensor_scan", out=gfl, data0=smask.t[:, :, :].rearrange("p a b -> p (a b)"), data1=gfl,
                                  initial=0.0, op0=ALU.mult, op1=ALU.add), [G[d_], smask], [G[d_]])
                for d_ in range(2):
                    qv = qd[d_].t[:, :].rearrange("p (a b) -> p a b", a=NB)
                    ev = eA[d_].t[:, :].rearrange("p (a b) -> p a b", a=NB)
                    if d_ == 0:
                        cq, sq_, sk_ = G[0].t[:, :, 1:129], 1.0, -1.0
                    else:
                        cq, sq_, sk_ = G[1].t[:, :, 0:128], -1.0, 1.0
                    P.op("act", L("activation", out=ev, in_=cq, func=AF.Exp, scale=sk_), [G[d_]], [eA[d_]])
                    P.op("act", L("activation", out=qv, in_=cq, func=AF.Exp, scale=sq_), [G[d_]], [qd[d_]])
                    P.op("act", L("activation", out=Ed[d_].t[:, :, :], in_=G[d_].t[:, :, 128:129], func=AF.Exp), [G[d_]], [Ed[d_]])
                    P.op("dve", L("tensor_tensor", out=kiT[d_].t[:, :], in0=kT[d_].t[:, :], in1=eA[d_].t[:, :], op=ALU.mult),
                         [kT[d_], eA[d_]], [kiT[d_]])
                    P.op("pool", L("tensor_tensor", out=qd[d_].t[:, :], in0=qd[d_].t[:, :], in1=q_s.t[:, :], op=ALU.mult),
                         [qd[d_], q_s], [qd[d_]])

            def vproj(h, wcur):
                for jg in range(4):
                    pb = bank()
                    for jj in range(4):
                        j = jg * 4 + jj
                        mm_group(v4(pb)[:, jj, :], [(uT.t[:, c, j * 128:(j + 1) * 128], wcur.t[:, 3, c, :]) for c in range(8)],
                                 [uT, wcur], pb, last_signal=(jj == 3))
                    P.op("act", L("activation", out=v_h.t[:, jg * 4:(jg + 1) * 4, :], in_=v4(pb), func=AF.Copy), [pb], [v_h])

            def rec(h):
                sg_o = sog[h % 2]
                for d_ in range(2):
                    for jg in range(2):
                        pb = bank()
                        for jj in range(8):
                            j = jg * 8 + jj
                            P.op("pe", L("transpose", vt(pb)[:, jj, :], kiT[d_].t[:, j * 128:(j + 1) * 128], ident),
                                 [kiT[d_], cb], [pb], signal=(jj == 7))
                        if jg == 0:
                            P.op("dve", L("tensor_copy", out=ki[d_].t[:, jg * 8:(jg + 1) * 8, :], in_=vt(pb)), [pb], [ki[d_]])
                        else:
                            P.op("act", L("activation", out=ki[d_].t[:, jg * 8:(jg + 1) * 8, :], in_=vt(pb), func=AF.Copy), [pb], [ki[d_]])
                for d_ in range(2):
                    order = list(range(NB)) if d_ == 0 else list(range(NB - 1, -1, -1))
                    if d_ == 0:
                        P.op("pool", L("tensor_copy", out=mlt[0].t[:, 0:15], in_=Ed[0].t[:, 0:15, 0]), [Ed[0]], [mlt[0]])
                    else:
                        P.op("pool", L("tensor_copy", out=mlt[1].t[:, 0:15], in_=Ed[1].t[:, 14::-1, 0]), [Ed[1]], [mlt[1]])
                    em3 = Emat.t[:, :].rearrange("p (v j) -> p v j", j=16)
                    P.op("pool", L("tensor_copy", out=em3, in_=mlt[d_].t[:, :].unsqueeze(1).broadcast_to([128, 64, 16])), [mlt[d_]], [Emat])
                    for g4 in range(4):
                        pb = bank()
                        for slot in range(4):
                            j = order[g4 * 4 + slot]
                            P.op("pe", L("matmul", v4(pb)[:, slot, :], lhsT=ki[d_].t[:, j, :], rhs=v_h.t[:, j, :], start=True, stop=True),
                                 [ki[d_], v_h], [pb], signal=(slot == 3))
                        for hf in range(2):
                            dst = xt[hf].t[:, :].rearrange("p (v j) -> p v j", j=16)[:, :, g4 * 4:(g4 + 1) * 4].rearrange("p v j -> p j v")
                            src = v4(pb)[:, :, hf * 64:(hf + 1) * 64]
                            if g4 % 2 == 0:
                                P.op("act", L("activation", out=dst, in_=src, func=AF.Copy), [pb], [xt[hf]])
                            else:
                                P.op("dve", L("tensor_copy", out=dst, in_=src), [pb], [xt[hf]])
                    for hf in range(2):
                        P.op("dve", L("tensor_tensor_scan", out=xt[hf].t[:, :], data0=xt[hf].t[:, :], data1=Emat.t[:, :], initial=0.0,
                                      op0=ALU.add, op1=ALU.mult), [Emat, xt[hf]], [xt[hf]])
                        w3 = xt[hf].t[:, :].rearrange("p (v j) -> p v j", j=16)
                        P.op("act", L("activation", out=used[d_].t[:, 1:16, hf * 64:(hf + 1) * 64],
                                      in_=w3[:, :, 0:15].rearrange("p v j -> p j v"), func=AF.Copy), [xt[hf]], [used[d_]])
                pscs = {}

                def scores(g4):
                    pscs[g4] = [bank(), bank()]
                    for jj in range(4):
                        j = g4 * 4 + jj
                        bsl = slice(j * 128, (j + 1) * 128)
                        psc = pscs[g4][jj // 2]
                        so = 2 * (jj % 2)
                        for d_ in range(2):
                            P.op("pe", L("matmul", v4(psc)[:, so + d_, :], lhsT=kiT[d_].t[:, bsl], rhs=qd[d_].t[:, bsl], start=True, stop=True),
                                 [kiT[d_], qd[d_]], [psc], signal=(jj % 2 == 1 and d_ == 1))

                pos = {}

                def omain(g4):
                    for b2 in range(2):
                        mk = msk[2 * (g4 % 2) + b2]
                        P.op("dve", L("tensor_tensor", out=mk.t[:, :, :], in0=v4(pscs[g4][b2]), in1=masks4, op=ALU.mult),
                             [pscs[g4][b2], cb], [mk])
                    po = bank()
                    pos[g4] = po
                    for jj in range(4):
                        j = g4 * 4 + jj
                        bsl = slice(j * 128, (j + 1) * 128)
                        mk = msk[2 * (g4 % 2) + jj // 2]
                        so = 2 * (jj % 2)
                        osl = slice(jj * 128, (jj + 1) * 128)
                        mm_group(po.t[:, osl], [(v_h.t[:, j, :], mk.t[:, so, :]), (v_h.t[:, j, :], mk.t[:, so + 1, :]),
                                                (used[0].t[:, j, :], qd[0].t[:, bsl]), (used[1].t[:, NB - 1 - j, :], qd[1].t[:, bsl])],
                                 [v_h, mk, used[0], used[1], qd[0], qd[1]], po)
                    P.op("act", L("activation", out=sq[g4 % 2].t[:, :], in_=po.t[:, :], func=AF.Square), [po], [sq[g4 % 2]])

                def otail(g4):
                    po = pos[g4]
                    tsl = slice(g4 * 512, (g4 + 1) * 512)
                    pm = bank()
                    P.op("pe", L("matmul", pm.t[:, :], lhsT=onesm, rhs=sq[g4 % 2].t[:, :], start=True, stop=True), [cb, sq[g4 % 2]], [pm])
                    P.op("act", L("activation", out=lnr.t[:, :], in_=pm.t[:, :], func=AF.Ln, bias=EPS), [pm], [lnr])
                    P.op("act", L("activation", out=lnr.t[:, :], in_=lnr.t[:, :], func=AF.Exp, scale=-0.5), [lnr], [lnr])
                    P.op("dve", L("tensor_tensor", out=lnr.t[:, :], in0=po.t[:, :], in1=lnr.t[:, :], op=ALU.mult), [po, lnr], [lnr])
                    P.op("dve", L("scalar_tensor_tensor", out=yaT.t[:, h, tsl], in0=lnr.t[:, :], scalar=cols.t[:, h:h + 1], in1=sg_o.t[:, tsl],
                                  op0=ALU.mult, op1=ALU.mult), [lnr, cols, sg_o], [yaT])

                scores(0)
                scores(1)
                omain(0)
                for g4 in range(1, 4):
                    if g4 + 1 < 4:
                        scores(g4 + 1)
                    omain(g4)
                    otail(g4 - 1)
                otail(3)

            whs = {0: load_head(0)}
            proj_and_sig(0, whs[0])
            for h in range(8):
                if h + 1 < 8:
                    whs[h + 1] = load_head(h + 1)
                if s == 0:
                    issue_conv(7)
                decay(h)
                vproj(h, whs[h])
                if h + 1 < 8:
                    proj_and_sig(h + 1, whs[h + 1])
                rec(h)

            if s == 0:
                dump("yaT", yaT.t[:, :, :], yaT, [128, 8, SEQ], BF16)
            P.barrier()
            P.dma("pool", L("dma_start", out=wpool.t[:, :, :, :], in_=pw_d.rearrange("g (k p) n -> p g k n", p=128)), wpool)
            for tt in range(4):
                tsl = slice(tt * 512, (tt + 1) * 512)
                blks = [jb for jb in range(4 * tt - 1, 4 * tt + 5) if 0 <= jb < NB]
                slot_of = {jb: i for i, jb in enumerate(blks)}
                for half in range(2):
                    wsl = load_slab(half)
                    for k, jb in enumerate(blks):
                        pb = bank()
                        mm_group(pb.t[:, :], [(uT.t[:, c, jb * 128:(jb + 1) * 128], wsl.t[:, c, :]) for c in range(8)], [uT, wsl], pb)
                        dst = pblk.t[:, slot_of[jb], half * 512:(half + 1) * 512]
                        if k % 2 == 0:
                            P.op("act", L("activation", out=dst, in_=pb.t[:, :], func=AF.Copy), [pb], [pblk])
                        else:
                            P.op("dve", L("tensor_copy", out=dst, in_=pb.t[:, :]), [pb], [pblk])
                for c in range(8):
                    g = c // 2
                    pb = bank()
                    for jj in range(4):
                        j = 4 * tt + jj
                        srcs = []
                        if j - 1 >= 0:
                            srcs.append((j - 1, 0))
                        srcs.append((j, 3 if j == 0 else (4 if j == NB - 1 else 1)))
                        if j + 1 < NB:
                            srcs.append((j + 1, 2))
                        mm_group(pb.t[:, jj * 128:(jj + 1) * 128],
                                 [(pblk.t[:, slot_of[jb], c * 128:(c + 1) * 128], pmat(g, kind)) for (jb, kind) in srcs],
                                 [pblk, cb], pb, last_signal=(jj == 3))
                    if c % 2 == 0:
                        P.op("act", L("activation", out=yT.t[:, c, :], in_=pb.t[:, :], func=AF.Copy), [pb], [yT])
                    else:
                        P.op("dve", L("tensor_copy", out=yT.t[:, c, :], in_=pb.t[:, :]), [pb], [yT])
                for c2 in range(8):
                    g, hf = c2 // 2, c2 % 2
                    pb = bank()
                    mm_group(pb.t[:, :], [(wpool.t[:, g, kc, hf * 128:(hf + 1) * 128], yT.t[:, 2 * g + kc, :]) for kc in range(2)], [wpool, yT], pb)
                    P.op("dve", L("tensor_scalar", out=ybT.t[:, c2, :], in0=pb.t[:, :], scalar1=cols.t[:, 8 + c2:9 + c2], scalar2=None,
                                  op0=ALU.mult), [pb, cols], [ybT])
                if s == 0 and tt == 0:
                    dump("ybT", ybT.t[:, :, :], ybT, [128, 8, 512], BF16)
                for gi, (gbuf, seg) in enumerate(((sga, 6), (sgb, 7))):
                    for half in range(2):
                        wsl = load_slab((seg - 5) * 2 + half)
                        for cc in range(4):
                            pb = bank()
                            mm_group(pb.t[:, :], [(wsl.t[:, c, cc * 128:(cc + 1) * 128], uT.t[:, c, tsl]) for c in range(8)], [wsl, uT], pb)
                            P.op("act", L("activation", out=gbuf.t[:, half * 4 + cc, :], in_=pb.t[:, :], func=AF.Sigmoid), [pb], [gbuf])
                for half in range(2):
                    wsa = load_slab(6 + half)
                    wsb = load_slab(8 + half)
                    for cc in range(4):
                        dc = half * 4 + cc
                        pa = bank()
                        pb2 = bank()
                        mm_group(pa.t[:, :], [(wsa.t[:, c, cc * 128:(cc + 1) * 128], yaT.t[:, c, tsl]) for c in range(8)], [wsa, yaT], pa)
                        mm_group(pb2.t[:, :], [(wsb.t[:, c, cc * 128:(cc + 1) * 128], ybT.t[:, c, :]) for c in range(8)], [wsb, ybT], pb2)
                        P.op("dve", L("tensor_tensor", out=tz[0].t[:, :], in0=pa.t[:, :], in1=sga.t[:, dc, :], op=ALU.mult), [pa, sga], [tz[0]])
                        P.op("dve", L("tensor_tensor", out=tz[1].t[:, :], in0=pb2.t[:, :], in1=sgb.t[:, dc, :], op=ALU.mult), [pb2, sgb], [tz[1]])
                        P.op("pool", L("tensor_tensor", out=mT.t[:, dc, :], in0=tz[0].t[:, :], in1=tz[1].t[:, :], op=ALU.add), [tz[0], tz[1]], [mT])
                if s == 0 and tt == 0:
                    dump("sga", sga.t[:, :, :], sga, [128, 8, 512], BF16)
                    dump("mT", mT.t[:, :, :], mT, [128, 8, 512], BF16)
                wso = [load_slab(10 + half) for half in range(2)]
                def blk_mm(jj):
                    j = 4 * tt + jj
                    sl = jj % 2
                    P.dma("sp", L("dma_start", out=xt[sl].t[:], in_=x_d[row0 + j * 128:row0 + (j + 1) * 128, :]), xt[sl])
                    for half in range(2):
                        pb = bank()
                        mm_group(pb.t[:, :], [(mT.t[:, c, jj * 128:(jj + 1) * 128], wso[half].t[:, c, :]) for c in range(8)], [mT, wso[half]], pb)
                        P.op("dve", L("tensor_tensor", out=h2.t[:, jj, half * 512:(half + 1) * 512], in0=pb.t[:, :],
                                      in1=xt[sl].t[:, half * 512:(half + 1) * 512], op=ALU.add), [pb, xt[sl]], [h2])
                    rms_stats(h2.t[:, jj, :], h2, ss2, jj, xs[sl])
                    fm_prep(h2.t[:, jj, :], h2, ss2.t[:, 2, jj:jj + 1], ss2, gbc[1], sl)

                def blk_tr(jj):
                    fm_transpose(jj % 2, u2T.t[:, :, jj * 128:(jj + 1) * 128], u2T, jj)

                blk_mm(0)
                for jj in range(1, 4):
                    blk_mm(jj)
                    blk_tr(jj - 1)
                blk_tr(3)
                if s == 0 and tt == 0:
                    dump("h2", h2.t[:, :, :], h2, [128, 4, D], F32)
                    dump("u2T", u2T.t[:, :, :], u2T, [128, 8, 512], BF16)
                for fc in range(NFC):
                    wg = wgu[fc % 2]
                    P.dma("sp", L("dma_start", out=wg.t[:, :, :, :], in_=s_w1[fc]), wg)
                    pg = bank()
                    pu = bank()
                    mm_group(pg.t[:, :], [(wg.t[:, 0, c, :], u2T.t[:, c, :]) for c in range(8)], [wg, u2T], pg)
                    mm_group(pu.t[:, :], [(wg.t[:, 1, c, :], u2T.t[:, c, :]) for c in range(8)], [wg, u2T], pu)
                    sg = sgt[fc % 2]
                    P.op("act", L("activation", out=sg.t[:, :], in_=pg.t[:, :], func=AF.Silu), [pg], [sg])
                    P.op("dve", L("tensor_tensor", out=actT.t[:, fc, :], in0=pu.t[:, :], in1=sg.t[:, :], op=ALU.mult), [pu, sg], [actT])
                if s == 0 and tt == 0:
                    dump("actT", actT.t[:, :, :], actT, [128, NFC, 512], BF16)
                for half in range(2):
                    hsl = slice(half * 512, (half + 1) * 512)
                    obanks = [bank() for _ in range(4)]
                    for fc in range(NFC):
                        w2 = w2s[(half * NFC + fc) % 4]
                        P.dma("sp", L("dma_start", out=w2.t[:, 0:512], in_=s_w2[fc * 128:(fc + 1) * 128, hsl]), w2)
                        for jj in range(4):
                            pb = obanks[jj]
                            P.op("pe", L("matmul", pb.t[:, :], lhsT=actT.t[:, fc, jj * 128:(jj + 1) * 128], rhs=w2.t[:, 0:512],
                                         start=(fc == 0), stop=(fc == NFC - 1)), [actT, w2], [pb],
                                 signal=(fc == NFC - 1) or (jj == 3))
                    for jj in range(4):
                        pb = obanks[jj]
                        P.op("dve", L("tensor_tensor", out=h2.t[:, jj, hsl], in0=pb.t[:, :], in1=h2.t[:, jj, hsl], op=ALU.add), [pb, h2], [h2])
                for jj in range(4):
                    j = 4 * tt + jj
                    rms_stats(h2.t[:, jj, :], h2, ss3, jj, xs[jj % 2])
                    ob = ost[jj % 2]
                    P.op("dve", L("scalar_tensor_tensor", out=ob.t[:, :], in0=h2.t[:, jj, :], scalar=ss3.t[:, 2, jj:jj + 1], in1=gbc[2].t[:, :],
                                  op0=ALU.mult, op1=ALU.mult), [h2, ss3, gbc[2]], [ob])
                    P.dma("sp", L("dma_start", out=out_d[row0 + j * 128:row0 + (j + 1) * 128, :], in_=ob.t[:, :]), ob, is_load=False)
        P.wait_all("sp", list(ost) + dbg_bufs)

        sems_eng = {k: es.enter_context(nc.semaphore("sem_" + k)) for k in ["pe", "act", "dve", "pool"]}
        sems_dma = [es.enter_context(nc.semaphore("dsem%d" % i)) for i in range(P.ndma)]
        with nc.Block() as block:
            P.emit(block, sems_eng, sems_dma)
    return nc


_NC_CACHE = {}


def kernel(x, g_mix, w_in, lb_logits, hgrn_norm_g, pool_w, pool_scale, w_branch_a, w_branch_b, w_out,
           g_ffn, w_ffn_in, w_ffn_out, g_final):
    f32 = np.float32
    x = np.asarray(x, f32)
    B = x.shape[0]
    xs_ = x.reshape(NCORES, NSEQ * SEQ, D)
    gvec = np.ascontiguousarray(np.stack([np.asarray(g_mix, f32)[0], np.asarray(g_ffn, f32)[0], np.asarray(g_final, f32)]))
    cols = np.zeros((128, 48), f32)
    cols[:, 0:8] = np.asarray(hgrn_norm_g, f32)[0].reshape(8, 128).T
    cols[:, 8:16] = np.asarray(pool_scale, f32)[0].reshape(8, 128).T
    lbl = np.asarray(lb_logits, f32)
    cols[:, 16:48] = lbl.reshape(2, 2, 8, 128).transpose(3, 0, 1, 2).reshape(128, 32)
    shared = {
        "w_in": np.ascontiguousarray(np.asarray(w_in, f32)[0]),
        "w_a": np.ascontiguousarray(np.asarray(w_branch_a, f32)[0]),
        "w_b": np.ascontiguousarray(np.asarray(w_branch_b, f32)[0]),
        "w_o": np.ascontiguousarray(np.asarray(w_out, f32)[0]),
        "pool_w": np.ascontiguousarray(np.asarray(pool_w, f32)[0]),
        "w_ffn_in": np.ascontiguousarray(np.asarray(w_ffn_in, f32)[0]),
        "w_ffn_out": np.ascontiguousarray(np.asarray(w_ffn_out, f32)[0]),
        "gvec": gvec,
        "cols": cols,
        "cb": _const_bf16(),
    }
    if "nc" not in _NC_CACHE:
        _NC_CACHE["nc"] = build_program()
    nc = _NC_CACHE["nc"]
    in_maps = []
    for c in range(NCORES):
        m = dict(shared)
        m["x"] = np.ascontiguousarray(xs_[c])
        in_maps.append(m)
    res = run_bass_kernel_spmd(nc, in_maps, core_ids=list(range(NCORES)))
    out = np.stack([np.asarray(r["out"], f32) for r in res.results], axis=0)
    return out.reshape(B, SEQ, D)
```

```python
import numpy as np
import ml_dtypes
from contextlib import ExitStack
import concourse.bass as bass
import concourse.mybir as mybir
from concourse.bass_utils import run_bass_kernel_spmd

F32 = mybir.dt.float32
BF16 = mybir.dt.bfloat16
AF = mybir.ActivationFunctionType
ALU = mybir.AluOpType

NCORES = 8
D = 1024
SEQ = 2048
NSEQ = 2
NB = SEQ // 128
DFF = 2816
NFC = DFF // 128
EPS = 1e-6
ENGS = ["pe", "act", "dve", "pool", "sp"]
DEBUG = False


class St:
    __slots__ = ("w", "r", "dsem", "dcount")

    def __init__(self):
        self.w = {}
        self.r = {}
        self.dsem = None
        self.dcount = 0


class Buf:
    def __init__(self, t, st=None):
        self.t = t
        self.st = st if st is not None else St()


class Item:
    __slots__ = ("waits", "fn", "inc")

    def __init__(self, waits, fn, inc):
        self.waits = waits
        self.fn = fn
        self.inc = inc


class Prog:
    def __init__(self):
        self.q = {e: [] for e in ENGS}
        self.cnt = {e: 0 for e in ENGS}
        self.seen = {e: {} for e in ENGS}
        self.ndma = 0
        self.dsts = []

    def barrier(self):
        for eng in ENGS:
            waits = []
            seen = self.seen[eng]
            for f in ("pe", "act", "dve", "pool"):
                if f != eng and self.cnt[f] > seen.get(f, 0):
                    seen[f] = self.cnt[f]
                    waits.append((f, self.cnt[f]))
            for st in self.dsts:
                key = ("dma", st.dsem)
                if st.dcount > seen.get(key, 0):
                    seen[key] = st.dcount
                    waits.append((key, st.dcount))
            if waits:
                self.q[eng].append(Item(waits, None, None))

    def _waits(self, eng, reads, writes):
        need = {}

        def add(key, val):
            if val > need.get(key, 0):
                need[key] = val

        for st in reads:
            for k, c in st.w.items():
                add(k, c)
        for st in writes:
            for k, c in st.w.items():
                if k != eng:
                    add(k, c)
            for k, c in st.r.items():
                if k != eng:
                    add(k, c)
        if eng == "pe":
            need.pop("pe", None)
        out = []
        seen = self.seen[eng]
        for key, val in need.items():
            if seen.get(key, 0) < val:
                seen[key] = val
                out.append((key, val))
        return out

    def op(self, eng, fn, reads=(), writes=(), signal=True):
        reads = [b.st for b in reads]
        writes = [b.st for b in writes]
        waits = self._waits(eng, reads, writes)
        if signal:
            self.cnt[eng] += 1
            c = self.cnt[eng]
            inc = (eng, 1)
        else:
            c = self.cnt[eng] + 1
            inc = None
        for st in writes:
            st.w = {eng: c}
            st.r = {}
        for st in reads:
            st.r[eng] = c
        self.q[eng].append(Item(waits, fn, inc))

    def dma(self, queue, fn, buf, is_load=True):
        st = buf.st
        if st.dsem is None:
            st.dsem = self.ndma
            self.ndma += 1
            self.dsts.append(st)
        if is_load:
            own = ("dma", st.dsem)
            prev_load = st.w.pop(own, None)
            waits = self._waits(queue, [], [st])
            if prev_load is not None:
                st.w[own] = prev_load
        else:
            waits = self._waits(queue, [st], [])
        st.dcount += 16
        key = ("dma", st.dsem)
        if is_load:
            st.w = {key: st.dcount}
            st.r = {}
        else:
            st.r[key] = st.dcount
        self.q[queue].append(Item(waits, fn, (key, 16)))

    def wait_all(self, eng, bufs):
        waits = self._waits(eng, [], [b.st for b in bufs])
        if waits:
            self.q[eng].append(Item(waits, None, None))

    def emit(self, block, sems_eng, sems_dma):
        def semof(key):
            if isinstance(key, tuple):
                return sems_dma[key[1]]
            return sems_eng[key]

        def body(engname):
            def _f(e):
                for it in self.q[engname]:
                    for key, val in it.waits:
                        e.wait_ge(semof(key), val)
                    if it.fn is not None:
                        ins = it.fn(e)
                        if it.inc is not None:
                            ins.then_inc(semof(it.inc[0]), it.inc[1])
            return _f

        block.tensor(body("pe"))
        block.scalar(body("act"))
        block.vector(body("dve"))
        block.gpsimd(body("pool"))
        block.sync(body("sp"))


def _pool_mats():
    wins = (2, 4, 8, 16)
    L = SEQ
    mats = np.zeros((128, 20, 128), np.float32)
    t = np.arange(L)
    for g, w in enumerate(wins):
        half = w // 2
        lo = np.clip(t - half + 1, 0, L)
        hi = np.clip(t + half + 1, 0, L)
        Pm = np.zeros((L, L), np.float32)
        for tt in range(L):
            Pm[tt, lo[tt]:hi[tt]] = 1.0 / float(hi[tt] - lo[tt])
        Pm -= np.eye(L, dtype=np.float32)

        def blk(tb, sb):
            return Pm[tb * 128:(tb + 1) * 128, sb * 128:(sb + 1) * 128].T

        mats[:, g * 5 + 0, :] = blk(5, 4)
        mats[:, g * 5 + 1, :] = blk(5, 5)
        mats[:, g * 5 + 2, :] = blk(5, 6)
        mats[:, g * 5 + 3, :] = blk(0, 0)
        mats[:, g * 5 + 4, :] = blk(NB - 1, NB - 1)
    return mats


def _const_bf16():
    cb = np.zeros((128, 26, 128), np.float32)
    cb[:, 0, :] = np.eye(128)
    cb[:, 1, :] = 1.0 / 128.0
    s = np.arange(128)[:, None]
    t = np.arange(128)[None, :]
    cb[:, 2, :] = (s <= t)
    cb[:, 3, :] = (s >= t)
    cb[:, 4, :] = (s <= t)
    cb[:, 5, :] = (s >= t)
    cb[:, 6:26, :] = _pool_mats()
    return cb.reshape(128, 26 * 128).astype(ml_dtypes.bfloat16)


def build_program():
    nc = bass.Bass("TRN2", target_bir_lowering=False)
    x_d = nc.dram_tensor("x", [NSEQ * SEQ, D], F32, kind="ExternalInput").ap()
    win_d = nc.dram_tensor("w_in", [D, 8 * D], F32, kind="ExternalInput").ap()
    wa_d = nc.dram_tensor("w_a", [D, D], F32, kind="ExternalInput").ap()
    wb_d = nc.dram_tensor("w_b", [D, D], F32, kind="ExternalInput").ap()
    wo_d = nc.dram_tensor("w_o", [D, D], F32, kind="ExternalInput").ap()
    pw_d = nc.dram_tensor("pool_w", [4, 256, 256], F32, kind="ExternalInput").ap()
    w1_d = nc.dram_tensor("w_ffn_in", [D, 2 * DFF], F32, kind="ExternalInput").ap()
    w2_d = nc.dram_tensor("w_ffn_out", [DFF, D], F32, kind="ExternalInput").ap()
    gv_d = nc.dram_tensor("gvec", [3, D], F32, kind="ExternalInput").ap()
    cols_d = nc.dram_tensor("cols", [128, 48], F32, kind="ExternalInput").ap()
    cb_d = nc.dram_tensor("cb", [128, 26 * 128], BF16, kind="ExternalInput").ap()
    out_d = nc.dram_tensor("out", [NSEQ * SEQ, D], F32, kind="ExternalOutput").ap()
    s_slab = nc.dram_tensor("s_slab", [12, 128, 8, 512], BF16).ap()
    s_w1 = nc.dram_tensor("s_w1", [NFC, 128, 2, 8, 128], BF16).ap()
    s_w2 = nc.dram_tensor("s_w2", [DFF, D], BF16).ap()

    P = Prog()
    with ExitStack() as es:
        def sb(name, shape, dt):
            return Buf(es.enter_context(nc.sbuf_tensor("sb_" + name, shape, dt)))

        gbc = [sb("gbc%d" % i, [128, D], BF16 if i < 2 else F32) for i in range(3)]
        cols = sb("cols", [128, 48], F32)
        cb = sb("cb", [128, 26, 128], BF16)
        lbt = sb("lbt", [128, 5, 16], F32)
        uT = sb("uT", [128, 8, SEQ], BF16)
        yaT = sb("yaT", [128, 8, SEQ], BF16)
        ident = cb.t[:, 0, :]
        onesm = cb.t[:, 1, :]
        masks4 = cb.t[:, 2:6, :]

        def pmat(g, kind):
            return cb.t[:, 6 + g * 5 + kind, :]

        PSB = [Buf(es.enter_context(nc.psum_tensor("psb%d" % i, [128, 512], F32))) for i in range(8)]
        ps_i = [0]

        def bank():
            b = PSB[ps_i[0] % 8]
            ps_i[0] += 1
            return b

        def v4(b):
            return b.t[:, :].rearrange("p (a b) -> p a b", a=4)

        def vt(b):
            return b.t[:, :].bitcast(BF16).rearrange("p (a b) -> p a b", a=8)

        xt = [sb("xt%d" % i, [128, D], F32) for i in range(2)]
        xs = [sb("xs%d" % i, [128, D], BF16) for i in range(2)]
        smask = sb("smask", [128, NB, 129], BF16)
        ssq = sb("ssq", [128, 3, 16], F32)
        ss2 = sb("ss2", [128, 3, 4], F32)
        ss3 = sb("ss3", [128, 3, 4], F32)

        ARENA = 111104
        AR = es.enter_context(nc.sbuf_tensor("AR", [128, ARENA // 2], BF16))
        cur = [0]

        def sb(name, shape, dt):
            n = 1
            for k in shape[1:]:
                n *= k
            nbytes = n * (4 if dt == F32 else 2)
            off = cur[0]
            cur[0] += (nbytes + 3) // 4 * 4
            assert cur[0] <= ARENA, (name, cur[0])
            ap = AR[:, off // 2:(off + nbytes) // 2]
            if dt == F32:
                ap = ap.bitcast(F32)
            if len(shape) == 3:
                ap = ap.rearrange("p (a b) -> p a b", a=shape[1])
            elif len(shape) == 4:
                ap = ap.rearrange("p (a b c) -> p a b c", a=shape[1], b=shape[2])
            return Buf(ap)

        wh = [sb("wh%d" % i, [128, 5, 8, 128], BF16) for i in range(2)]
        sgq = [sb("sgq%d" % i, [128, 512], BF16) for i in range(2)]
        q_s = sb("q_s", [128, SEQ], BF16)
        sog = [sb("sog%d" % i, [128, SEQ], BF16) for i in range(2)]
        kT = [sb("kT%d" % i, [128, SEQ], BF16) for i in range(2)]
        G = [sb("G%d" % i, [128, NB, 129], F32) for i in range(2)]
        qd = [sb("qd%d" % i, [128, SEQ], BF16) for i in range(2)]
        kiT = [sb("kiT%d" % i, [128, SEQ], BF16) for i in range(2)]
        eA0 = sb("eA", [128, SEQ], BF16)
        eA = [eA0, eA0]
        ki = [sb("ki%d" % i, [128, NB, 128], BF16) for i in range(2)]
        v_h = sb("v_h", [128, NB, 128], BF16)
        used = [sb("used%d" % i, [128, NB, 128], BF16) for i in range(2)]
        Ed = [sb("Ed%d" % i, [128, NB, 1], F32) for i in range(2)]
        Emat = sb("Emat", [128, 1024], F32)
        mlt = [sb("mlt%d" % i, [128, 16], F32) for i in range(2)]
        msk = [sb("msk%d" % i, [128, 4, 128], BF16) for i in range(4)]
        sq = sgq
        lnr = sb("lnr", [128, 512], F32)

        print("arena phase2 bytes", cur[0])
        cur[0] = 0
        wpool = sb("wpool", [128, 4, 2, 256], BF16)
        WA = [sb("WA%d" % i, [128, 8, 512], BF16) for i in range(3)]
        wgu = [sb("wgu%d" % i, [128, 2, 8, 128], BF16) for i in range(2)]
        w2s = [sb("w2s%d" % i, [128, D], BF16) for i in range(4)]
        R1 = sb("R1", [128, 12288], BF16).t
        pblk = Buf(R1[:, 0:6144].rearrange("p (a b) -> p a b", a=6))
        yT = Buf(R1[:, 6144:10240].rearrange("p (a b) -> p a b", a=8))
        ybT = Buf(R1[:, 0:4096].rearrange("p (a b) -> p a b", a=8))
        sga = Buf(R1[:, 4096:8192].rearrange("p (a b) -> p a b", a=8))
        sgb = Buf(R1[:, 8192:12288].rearrange("p (a b) -> p a b", a=8))
        actT = Buf(R1[:, 0:11264].rearrange("p (a b) -> p a b", a=NFC))
        h2 = sb("h2", [128, 4, D], F32)
        mT = sb("mT", [128, 8, 512], BF16)
        u2T = sb("u2T", [128, 8, 512], BF16)
        ost = xt
        tz = [sb("tz%d" % i, [128, 512], BF16) for i in range(2)]
        sgt = [sb("sgt%d" % i, [128, 512], BF16) for i in range(2)]
        print("arena phase3 bytes", cur[0])

        dbg_bufs = []

        def dump(name, ap, buf, shape, dt):
            if not DEBUG:
                return
            dd = nc.dram_tensor("dbg_" + name, list(shape), dt, kind="ExternalOutput").ap()
            P.dma("sp", (lambda dd, ap: lambda e: e.dma_start(out=dd, in_=ap))(dd, ap), buf, is_load=False)
            dbg_bufs.append(buf)

        def L(meth, *a, **kw):
            return lambda e: getattr(e, meth)(*a, **kw)

        for i in range(3):
            P.dma("pool" if i < 2 else "sp", L("dma_start", out=gbc[i].t[:], in_=gv_d[i].partition_broadcast(128)), gbc[i])
        P.dma("sp", L("dma_start", out=cols.t[:], in_=cols_d), cols)
        P.dma("sp", L("dma_start", out=cb.t[:], in_=cb_d.rearrange("p (a b) -> p a b", a=26)), cb)
        P.op("pool", L("memset", smask.t[:, :, :], 1.0), [], [smask])
        P.op("pool", L("memset", smask.t[:, :, 0:1], 0.0), [], [smask])
        for a in range(2):
            P.op("dve", L("tensor_tensor", out=lbt.t[:, 0, a * 8:(a + 1) * 8],
                          in0=cols.t[:, 16 + (a * 2 + 1) * 8:16 + (a * 2 + 2) * 8],
                          in1=cols.t[:, 16 + (a * 2) * 8:16 + (a * 2 + 1) * 8], op=ALU.subtract), [cols], [lbt])
        P.op("act", L("activation", out=lbt.t[:, 4, :], in_=lbt.t[:, 0, :], func=AF.Exp), [lbt], [lbt])
        P.op("dve", L("tensor_scalar_add", out=lbt.t[:, 0, :], in0=lbt.t[:, 4, :], scalar1=1.0), [lbt], [lbt])
        P.op("dve", L("reciprocal", out=lbt.t[:, 1, :], in_=lbt.t[:, 0, :]), [lbt], [lbt])
        P.op("dve", L("tensor_scalar", out=lbt.t[:, 2, :], in0=lbt.t[:, 1, :], scalar1=-1.0, scalar2=1.0,
                      op0=ALU.mult, op1=ALU.add), [lbt], [lbt])
        P.op("dve", L("tensor_scalar_add", out=lbt.t[:, 3, :], in0=lbt.t[:, 1, :], scalar1=-1.0), [lbt], [lbt])

        def rms_stats(src_ap, src_buf, ssb, col, junk):
            P.op("act", L("activation", out=junk.t[:], in_=src_ap, func=AF.Square,
                          accum_out=ssb.t[:, 0, col:col + 1]), [src_buf], [junk, ssb])
            P.op("act", L("activation", out=ssb.t[:, 1, col:col + 1], in_=ssb.t[:, 0, col:col + 1], func=AF.Ln,
                          scale=1.0 / D, bias=EPS), [ssb], [ssb])
            P.op("act", L("activation", out=ssb.t[:, 2, col:col + 1], in_=ssb.t[:, 1, col:col + 1], func=AF.Exp,
                          scale=-0.5), [ssb], [ssb])

        def fm_prep(src_ap, src_buf, rstd_ap, rstd_buf, gb, slot):
            P.op("dve", L("scalar_tensor_tensor", out=xs[slot].t[:], in0=src_ap, scalar=rstd_ap, in1=gb.t[:],
                          op0=ALU.mult, op1=ALU.mult), [src_buf, rstd_buf, gb], [xs[slot]])

        def fm_transpose(slot, dst_ap, dst_buf, k):
            pb = bank()
            for c in range(8):
                P.op("pe", L("transpose", vt(pb)[:, c, :], xs[slot].t[:, c * 128:(c + 1) * 128], ident),
                     [xs[slot], cb], [pb], signal=(c == 7))
            if k % 2 == 0:
                P.op("act", L("activation", out=dst_ap, in_=vt(pb), func=AF.Copy), [pb], [dst_buf])
            else:
                P.op("dve", L("tensor_copy", out=dst_ap, in_=vt(pb)), [pb], [dst_buf])

        def to_feature_major(src_ap, src_buf, rstd_ap, rstd_buf, gb, slot, dst_ap, dst_buf, k):
            fm_prep(src_ap, src_buf, rstd_ap, rstd_buf, gb, slot)
            fm_transpose(slot, dst_ap, dst_buf, k)

        wa_i = [0]

        def load_slab(idx):
            b = WA[wa_i[0] % 3]
            wa_i[0] += 1
            P.dma("sp", L("dma_start", out=b.t[:, :, :], in_=s_slab[idx]), b)
            return b

        def load_head(h):
            b = wh[h % 2]
            for seg in range(5):
                P.dma("pool", L("dma_start", out=b.t[:, seg, :, :],
                                in_=win_d[:, seg * D + h * 128:seg * D + (h + 1) * 128].rearrange("(c p) n -> p c n", p=128)), b)
            return b

        conv_dummy = [Buf(None) for _ in range(4)]
        conv_jobs = []
        for c in range(8):
            rows = slice(c * 128, (c + 1) * 128)
            conv_jobs.append((s_slab[0:6, :, c, :].rearrange("s p n -> p s n"),
                              win_d[rows, 5 * D:8 * D].rearrange("p (s n) -> p s n", n=512)))
            for mi, wd in enumerate((wa_d, wb_d, wo_d)):
                conv_jobs.append((s_slab[6 + 2 * mi:8 + 2 * mi, :, c, :].rearrange("s p n -> p s n"),
                                  wd[rows, :].rearrange("p (s n) -> p s n", n=512)))
            conv_jobs.append((s_w1[:, :, :, c, :].rearrange("f p k n -> p k f n"),
                              w1_d[rows, :].rearrange("p (k f n) -> p k f n", k=2, n=128)))
        for f2 in range(0, NFC, 2):
            conv_jobs.append((s_w2[f2 * 128:(f2 + 2) * 128, :], w2_d[f2 * 128:(f2 + 2) * 128, :]))
        conv_i = [0]

        def issue_conv(n):
            for _ in range(n):
                if conv_i[0] >= len(conv_jobs):
                    return
                o_ap, i_ap = conv_jobs[conv_i[0]]
                P.dma("pool", L("dma_start", out=o_ap, in_=i_ap), conv_dummy[conv_i[0] % 4])
                conv_i[0] += 1

        def mm_group(out_ap, pairs, reads, pb, last_signal=True):
            n = len(pairs)
            for k, (lh, rh) in enumerate(pairs):
                P.op("pe", L("matmul", out_ap, lhsT=lh, rhs=rh, start=(k == 0), stop=(k == n - 1)), reads, [pb],
                     signal=(last_signal and k == n - 1))

        for s in range(NSEQ):
            row0 = s * SEQ
            for b in range(NB):
                sl = b % 2
                P.dma("sp", L("dma_start", out=xt[sl].t[:], in_=x_d[row0 + b * 128:row0 + (b + 1) * 128, :]), xt[sl])
                rms_stats(xt[sl].t[:], xt[sl], ssq, b, xs[sl])
                to_feature_major(xt[sl].t[:], xt[sl], ssq.t[:, 2, b:b + 1], ssq, gbc[0], sl,
                                 uT.t[:, :, b * 128:(b + 1) * 128], uT, b)

            if s == 0:
                dump("uT", uT.t[:, :, :], uT, [128, 8, SEQ], BF16)
            P.barrier()
            for d_ in range(2):
                P.op("pool", L("memset", used[d_].t[:, :, :], 0.0), [], [used[d_]])
                P.op("pool", L("memset", G[d_].t[:, :, 0:1], 0.0), [], [G[d_]])
                P.op("pool", L("memset", mlt[d_].t[:, :], 0.0), [], [mlt[d_]])

            def proj_and_sig(h, wcur):
                sg_o = sog[h % 2]
                for tt in range(4):
                    tsl = slice(tt * 512, (tt + 1) * 512)
                    banks = []
                    for si in (0, 1, 2, 4):
                        pb = bank()
                        banks.append(pb)
                        mm_group(pb.t[:, :], [(wcur.t[:, si, c, :], uT.t[:, c, tsl]) for c in range(8)], [wcur, uT], pb)
                    pq, pf, pbk, pog = banks
                    for (pz, dst, k) in ((pq, q_s, 0), (pog, sg_o, 1)):
                        sg = sgq[k]
                        P.op("act", L("activation", out=sg.t[:, :], in_=pz.t[:, :], func=AF.Sigmoid), [pz], [sg])
                        P.op("dve", L("tensor_tensor", out=dst.t[:, tsl], in0=pz.t[:, :], in1=sg.t[:, :], op=ALU.mult), [pz, sg], [dst])
                    for d_, pz in ((0, pf), (1, pbk)):
                        gsl = G[d_].t[:, tt * 4:(tt + 1) * 4, 1:129]
                        col = d_ * 8 + h
                        P.op("act", L("activation", out=gsl, in_=v4(pz), func=AF.Sigmoid), [pz], [G[d_]])
                        P.op("dve", L("tensor_scalar", out=kT[d_].t[:, tsl].rearrange("p (a b) -> p a b", a=4), in0=gsl,
                                      scalar1=lbt.t[:, 3, col:col + 1], scalar2=lbt.t[:, 2, col:col + 1],
                                      op0=ALU.mult, op1=ALU.add), [G[d_], lbt], [kT[d_]])

            def decay_a(h):
                for d_ in range(2):
                    col = d_ * 8 + h
                    gin = G[d_].t[:, :, 1:129]
                    P.op("act", L("activation", out=gin, in_=gin, func=AF.Ln, scale=lbt.t[:, 2, col:col + 1],
                                  bias=lbt.t[:, 1, col:col + 1]), [G[d_], lbt], [G[d_]])
                for d_ in range(2):
                    gfl = G[d_].t[:, :, :].rearrange("p a b -> p (a b)")
                    P.op("dve", L("tensor_tensor_scan", out=gfl, data0=smask.t[:, :, :].rearrange("p a b -> p (a b)"), data1=gfl,
                                  initial=0.0, op0=ALU.mult, op1=ALU.add), [G[d_], smask], [G[d_]])

            def decay_b(h):
                for d_ in range(2):
                    qv = qd[d_].t[:, :].rearrange("p (a b) -> p a b", a=NB)
                    ev = eA[d_].t[:, :].rearrange("p (a b) -> p a b", a=NB)
                    if d_ == 0:
                        cq, sq_, sk_ = G[0].t[:, :, 1:129], 1.0, -1.0
                    else:
                        cq, sq_, sk_ = G[1].t[:, :, 0:128], -1.0, 1.0
                    P.op("act", L("activation", out=ev, in_=cq, func=AF.Exp, scale=sk_), [G[d_]], [eA[d_]])
                    P.op("act", L("activation", out=qv, in_=cq, func=AF.Exp, scale=sq_), [G[d_]], [qd[d_]])
                    P.op("act", L("activation", out=Ed[d_].t[:, :, :], in_=G[d_].t[:, :, 128:129], func=AF.Exp), [G[d_]], [Ed[d_]])
                    P.op("dve", L("tensor_tensor", out=kiT[d_].t[:, :], in0=kT[d_].t[:, :], in1=eA[d_].t[:, :], op=ALU.mult),
                         [kT[d_], eA[d_]], [kiT[d_]])
                    P.op("pool", L("tensor_tensor", out=qd[d_].t[:, :], in0=qd[d_].t[:, :], in1=q_s.t[:, :], op=ALU.mult),
                         [qd[d_], q_s], [qd[d_]])

            def vproj(h, wcur):
                for jg in range(4):
                    pb = bank()
                    for jj in range(4):
                        j = jg * 4 + jj
                        mm_group(v4(pb)[:, jj, :], [(uT.t[:, c, j * 128:(j + 1) * 128], wcur.t[:, 3, c, :]) for c in range(8)],
                                 [uT, wcur], pb, last_signal=(jj == 3))
                    P.op("act", L("activation", out=v_h.t[:, jg * 4:(jg + 1) * 4, :], in_=v4(pb), func=AF.Copy), [pb], [v_h])

            def rec_chain(h):
                for d_ in range(2):
                    for jg in range(2):
                        pb = bank()
                        for jj in range(8):
                            j = jg * 8 + jj
                            P.op("pe", L("transpose", vt(pb)[:, jj, :], kiT[d_].t[:, j * 128:(j + 1) * 128], ident),
                                 [kiT[d_], cb], [pb], signal=(jj == 7))
                        if jg == 0:
                            P.op("dve", L("tensor_copy", out=ki[d_].t[:, jg * 8:(jg + 1) * 8, :], in_=vt(pb)), [pb], [ki[d_]])
                        else:
                            P.op("act", L("activation", out=ki[d_].t[:, jg * 8:(jg + 1) * 8, :], in_=vt(pb), func=AF.Copy), [pb], [ki[d_]])
                for d_ in range(2):
                    order = list(range(NB)) if d_ == 0 else list(range(NB - 1, -1, -1))
                    if d_ == 0:
                        P.op("pool", L("tensor_copy", out=mlt[0].t[:, 0:15], in_=Ed[0].t[:, 0:15, 0]), [Ed[0]], [mlt[0]])
                    else:
                        P.op("pool", L("tensor_copy", out=mlt[1].t[:, 0:15], in_=Ed[1].t[:, 14::-1, 0]), [Ed[1]], [mlt[1]])
                    em3 = Emat.t[:, :].rearrange("p (v j) -> p v j", j=16)
                    P.op("pool", L("tensor_copy", out=em3, in_=mlt[d_].t[:, :].unsqueeze(1).broadcast_to([128, 64, 16])), [mlt[d_]], [Emat])
                    for g4 in range(4):
                        pb = bank()
                        for slot in range(4):
                            j = order[g4 * 4 + slot]
                            P.op("pe", L("matmul", v4(pb)[:, slot, :], lhsT=ki[d_].t[:, j, :], rhs=v_h.t[:, j, :], start=True, stop=True),
                                 [ki[d_], v_h], [pb], signal=(slot == 3))
                        for hf in range(2):
                            dst = xt[hf].t[:, :].rearrange("p (v j) -> p v j", j=16)[:, :, g4 * 4:(g4 + 1) * 4].rearrange("p v j -> p j v")
                            src = v4(pb)[:, :, hf * 64:(hf + 1) * 64]
                            if g4 % 2 == 0:
                                P.op("act", L("activation", out=dst, in_=src, func=AF.Copy), [pb], [xt[hf]])
                            else:
                                P.op("dve", L("tensor_copy", out=dst, in_=src), [pb], [xt[hf]])
                    for hf in range(2):
                        P.op("dve", L("tensor_tensor_scan", out=xt[hf].t[:, :], data0=xt[hf].t[:, :], data1=Emat.t[:, :], initial=0.0,
                                      op0=ALU.add, op1=ALU.mult), [Emat, xt[hf]], [xt[hf]])
                        w3 = xt[hf].t[:, :].rearrange("p (v j) -> p v j", j=16)
                        P.op("act", L("activation", out=used[d_].t[:, 1:16, hf * 64:(hf + 1) * 64],
                                      in_=w3[:, :, 0:15].rearrange("p v j -> p j v"), func=AF.Copy), [xt[hf]], [used[d_]])
            def rec_sweep(h):
                sg_o = sog[h % 2]
                pscs = {}

                def scores(g4):
                    pscs[g4] = [bank(), bank()]
                    for jj in range(4):
                        j = g4 * 4 + jj
                        bsl = slice(j * 128, (j + 1) * 128)
                        psc = pscs[g4][jj // 2]
                        so = 2 * (jj % 2)
                        for d_ in range(2):
                            P.op("pe", L("matmul", v4(psc)[:, so + d_, :], lhsT=kiT[d_].t[:, bsl], rhs=qd[d_].t[:, bsl], start=True, stop=True),
                                 [kiT[d_], qd[d_]], [psc], signal=(jj % 2 == 1 and d_ == 1))
                    for b2 in range(2):
                        mk = msk[2 * (g4 % 2) + b2]
                        P.op("dve", L("tensor_tensor", out=mk.t[:, :, :], in0=v4(pscs[g4][b2]), in1=masks4, op=ALU.mult),
                             [pscs[g4][b2], cb], [mk])

                pos = {}

                def omain(g4):
                    po = bank()
                    pos[g4] = po
                    for jj in range(4):
                        j = g4 * 4 + jj
                        bsl = slice(j * 128, (j + 1) * 128)
                        mk = msk[2 * (g4 % 2) + jj // 2]
                        so = 2 * (jj % 2)
                        osl = slice(jj * 128, (jj + 1) * 128)
                        mm_group(po.t[:, osl], [(v_h.t[:, j, :], mk.t[:, so, :]), (v_h.t[:, j, :], mk.t[:, so + 1, :]),
                                                (used[0].t[:, j, :], qd[0].t[:, bsl]), (used[1].t[:, NB - 1 - j, :], qd[1].t[:, bsl])],
                                 [v_h, mk, used[0], used[1], qd[0], qd[1]], po)
                    P.op("act", L("activation", out=sq[g4 % 2].t[:, :], in_=po.t[:, :], func=AF.Square), [po], [sq[g4 % 2]])

                def otail(g4):
                    po = pos[g4]
                    tsl = slice(g4 * 512, (g4 + 1) * 512)
                    pm = bank()
                    P.op("pe", L("matmul", pm.t[:, :], lhsT=onesm, rhs=sq[g4 % 2].t[:, :], start=True, stop=True), [cb, sq[g4 % 2]], [pm])
                    P.op("act", L("activation", out=lnr.t[:, :], in_=pm.t[:, :], func=AF.Ln, bias=EPS), [pm], [lnr])
                    P.op("act", L("activation", out=lnr.t[:, :], in_=lnr.t[:, :], func=AF.Exp, scale=-0.5), [lnr], [lnr])
                    P.op("dve", L("tensor_tensor", out=lnr.t[:, :], in0=po.t[:, :], in1=lnr.t[:, :], op=ALU.mult), [po, lnr], [lnr])
                    P.op("dve", L("scalar_tensor_tensor", out=yaT.t[:, h, tsl], in0=lnr.t[:, :], scalar=cols.t[:, h:h + 1], in1=sg_o.t[:, tsl],
                                  op0=ALU.mult, op1=ALU.mult), [lnr, cols, sg_o], [yaT])

                scores(0)
                scores(1)
                omain(0)
                for g4 in range(1, 4):
                    if g4 + 1 < 4:
                        scores(g4 + 1)
                    omain(g4)
                    otail(g4 - 1)
                otail(3)

            whs = {0: load_head(0)}
            proj_and_sig(0, whs[0])
            decay_a(0)
            for h in range(8):
                if h + 1 < 8:
                    whs[h + 1] = load_head(h + 1)
                if s == 0:
                    issue_conv(7)
                decay_b(h)
                vproj(h, whs[h])
                rec_chain(h)
                if h + 1 < 8:
                    proj_and_sig(h + 1, whs[h + 1])
                    decay_a(h + 1)
                rec_sweep(h)
            if s == 0:
                dump("yaT", yaT.t[:, :, :], yaT, [128, 8, SEQ], BF16)
            P.barrier()
            P.dma("pool", L("dma_start", out=wpool.t[:, :, :, :], in_=pw_d.rearrange("g (k p) n -> p g k n", p=128)), wpool)
            for tt in range(4):
                tsl = slice(tt * 512, (tt + 1) * 512)
                blks = [jb for jb in range(4 * tt - 1, 4 * tt + 5) if 0 <= jb < NB]
                slot_of = {jb: i for i, jb in enumerate(blks)}
                for half in range(2):
                    wsl = load_slab(half)
                    for k, jb in enumerate(blks):
                        pb = bank()
                        mm_group(pb.t[:, :], [(uT.t[:, c, jb * 128:(jb + 1) * 128], wsl.t[:, c, :]) for c in range(8)], [uT, wsl], pb)
                        dst = pblk.t[:, slot_of[jb], half * 512:(half + 1) * 512]
                        if k % 2 == 0:
                            P.op("act", L("activation", out=dst, in_=pb.t[:, :], func=AF.Copy), [pb], [pblk])
                        else:
                            P.op("dve", L("tensor_copy", out=dst, in_=pb.t[:, :]), [pb], [pblk])
                for c in range(8):
                    g = c // 2
                    pb = bank()
                    for jj in range(4):
                        j = 4 * tt + jj
                        srcs = []
                        if j - 1 >= 0:
                            srcs.append((j - 1, 0))
                        srcs.append((j, 3 if j == 0 else (4 if j == NB - 1 else 1)))
                        if j + 1 < NB:
                            srcs.append((j + 1, 2))
                        mm_group(pb.t[:, jj * 128:(jj + 1) * 128],
                                 [(pblk.t[:, slot_of[jb], c * 128:(c + 1) * 128], pmat(g, kind)) for (jb, kind) in srcs],
                                 [pblk, cb], pb, last_signal=(jj == 3))
                    if c % 2 == 0:
                        P.op("act", L("activation", out=yT.t[:, c, :], in_=pb.t[:, :], func=AF.Copy), [pb], [yT])
                    else:
                        P.op("dve", L("tensor_copy", out=yT.t[:, c, :], in_=pb.t[:, :]), [pb], [yT])
                for c2 in range(8):
                    g, hf = c2 // 2, c2 % 2
                    pb = bank()
                    mm_group(pb.t[:, :], [(wpool.t[:, g, kc, hf * 128:(hf + 1) * 128], yT.t[:, 2 * g + kc, :]) for kc in range(2)], [wpool, yT], pb)
                    P.op("dve", L("tensor_scalar", out=ybT.t[:, c2, :], in0=pb.t[:, :], scalar1=cols.t[:, 8 + c2:9 + c2], scalar2=None,
                                  op0=ALU.mult), [pb, cols], [ybT])
                if s == 0 and tt == 0:
                    dump("ybT", ybT.t[:, :, :], ybT, [128, 8, 512], BF16)
                for gi, (gbuf, seg) in enumerate(((sga, 6), (sgb, 7))):
                    for half in range(2):
                        wsl = load_slab((seg - 5) * 2 + half)
                        for cc in range(4):
                            pb = bank()
                            mm_group(pb.t[:, :], [(wsl.t[:, c, cc * 128:(cc + 1) * 128], uT.t[:, c, tsl]) for c in range(8)], [wsl, uT], pb)
                            P.op("act", L("activation", out=gbuf.t[:, half * 4 + cc, :], in_=pb.t[:, :], func=AF.Sigmoid), [pb], [gbuf])
                for half in range(2):
                    wsa = load_slab(6 + half)
                    wsb = load_slab(8 + half)
                    for cc in range(4):
                        dc = half * 4 + cc
                        pa = bank()
                        pb2 = bank()
                        mm_group(pa.t[:, :], [(wsa.t[:, c, cc * 128:(cc + 1) * 128], yaT.t[:, c, tsl]) for c in range(8)], [wsa, yaT], pa)
                        mm_group(pb2.t[:, :], [(wsb.t[:, c, cc * 128:(cc + 1) * 128], ybT.t[:, c, :]) for c in range(8)], [wsb, ybT], pb2)
                        P.op("dve", L("tensor_tensor", out=tz[0].t[:, :], in0=pa.t[:, :], in1=sga.t[:, dc, :], op=ALU.mult), [pa, sga], [tz[0]])
                        P.op("dve", L("tensor_tensor", out=tz[1].t[:, :], in0=pb2.t[:, :], in1=sgb.t[:, dc, :], op=ALU.mult), [pb2, sgb], [tz[1]])
                        P.op("pool", L("tensor_tensor", out=mT.t[:, dc, :], in0=tz[0].t[:, :], in1=tz[1].t[:, :], op=ALU.add), [tz[0], tz[1]], [mT])
                if s == 0 and tt == 0:
                    dump("sga", sga.t[:, :, :], sga, [128, 8, 512], BF16)
                    dump("mT", mT.t[:, :, :], mT, [128, 8, 512], BF16)
                wso = [load_slab(10 + half) for half in range(2)]
                def blk_mm(jj):
                    j = 4 * tt + jj
                    sl = jj % 2
                    P.dma("sp", L("dma_start", out=xt[sl].t[:], in_=x_d[row0 + j * 128:row0 + (j + 1) * 128, :]), xt[sl])
                    for half in range(2):
                        pb = bank()
                        mm_group(pb.t[:, :], [(mT.t[:, c, jj * 128:(jj + 1) * 128], wso[half].t[:, c, :]) for c in range(8)], [mT, wso[half]], pb)
                        P.op("dve", L("tensor_tensor", out=h2.t[:, jj, half * 512:(half + 1) * 512], in0=pb.t[:, :],
                                      in1=xt[sl].t[:, half * 512:(half + 1) * 512], op=ALU.add), [pb, xt[sl]], [h2])
                    rms_stats(h2.t[:, jj, :], h2, ss2, jj, xs[sl])
                    fm_prep(h2.t[:, jj, :], h2, ss2.t[:, 2, jj:jj + 1], ss2, gbc[1], sl)

                def blk_tr(jj):
                    fm_transpose(jj % 2, u2T.t[:, :, jj * 128:(jj + 1) * 128], u2T, jj)

                blk_mm(0)
                for jj in range(1, 4):
                    blk_mm(jj)
                    blk_tr(jj - 1)
                blk_tr(3)
                if s == 0 and tt == 0:
                    dump("h2", h2.t[:, :, :], h2, [128, 4, D], F32)
                    dump("u2T", u2T.t[:, :, :], u2T, [128, 8, 512], BF16)
                for fc in range(NFC):
                    wg = wgu[fc % 2]
                    P.dma("sp", L("dma_start", out=wg.t[:, :, :, :], in_=s_w1[fc]), wg)
                    pg = bank()
                    pu = bank()
                    mm_group(pg.t[:, :], [(wg.t[:, 0, c, :], u2T.t[:, c, :]) for c in range(8)], [wg, u2T], pg)
                    mm_group(pu.t[:, :], [(wg.t[:, 1, c, :], u2T.t[:, c, :]) for c in range(8)], [wg, u2T], pu)
                    sg = sgt[fc % 2]
                    P.op("act", L("activation", out=sg.t[:, :], in_=pg.t[:, :], func=AF.Silu), [pg], [sg])
                    P.op("dve", L("tensor_tensor", out=actT.t[:, fc, :], in0=pu.t[:, :], in1=sg.t[:, :], op=ALU.mult), [pu, sg], [actT])
                if s == 0 and tt == 0:
                    dump("actT", actT.t[:, :, :], actT, [128, NFC, 512], BF16)
                for half in range(2):
                    hsl = slice(half * 512, (half + 1) * 512)
                    obanks = [bank() for _ in range(4)]
                    for fc in range(NFC):
                        w2 = w2s[(half * NFC + fc) % 4]
                        P.dma("sp", L("dma_start", out=w2.t[:, 0:512], in_=s_w2[fc * 128:(fc + 1) * 128, hsl]), w2)
                        for jj in range(4):
                            pb = obanks[jj]
                            P.op("pe", L("matmul", pb.t[:, :], lhsT=actT.t[:, fc, jj * 128:(jj + 1) * 128], rhs=w2.t[:, 0:512],
                                         start=(fc == 0), stop=(fc == NFC - 1)), [actT, w2], [pb],
                                 signal=(fc == NFC - 1) or (jj == 3))
                    for jj in range(4):
                        pb = obanks[jj]
                        P.op("dve", L("tensor_tensor", out=h2.t[:, jj, hsl], in0=pb.t[:, :], in1=h2.t[:, jj, hsl], op=ALU.add), [pb, h2], [h2])
                for jj in range(4):
                    j = 4 * tt + jj
                    rms_stats(h2.t[:, jj, :], h2, ss3, jj, xs[jj % 2])
                    ob = ost[jj % 2]
                    P.op("dve", L("scalar_tensor_tensor", out=ob.t[:, :], in0=h2.t[:, jj, :], scalar=ss3.t[:, 2, jj:jj + 1], in1=gbc[2].t[:, :],
                                  op0=ALU.mult, op1=ALU.mult), [h2, ss3, gbc[2]], [ob])
                    P.dma("sp", L("dma_start", out=out_d[row0 + j * 128:row0 + (j + 1) * 128, :], in_=ob.t[:, :]), ob, is_load=False)
        P.wait_all("sp", list(ost) + dbg_bufs)

        sems_eng = {k: es.enter_context(nc.semaphore("sem_" + k)) for k in ["pe", "act", "dve", "pool"]}
        sems_dma = [es.enter_context(nc.semaphore("dsem%d" % i)) for i in range(P.ndma)]
        with nc.Block() as block:
            P.emit(block, sems_eng, sems_dma)
    return nc


_NC_CACHE = {}


def kernel(x, g_mix, w_in, lb_logits, hgrn_norm_g, pool_w, pool_scale, w_branch_a, w_branch_b, w_out,
           g_ffn, w_ffn_in, w_ffn_out, g_final):
    f32 = np.float32
    x = np.asarray(x, f32)
    B = x.shape[0]
    xs_ = x.reshape(NCORES, NSEQ * SEQ, D)
    gvec = np.ascontiguousarray(np.stack([np.asarray(g_mix, f32)[0], np.asarray(g_ffn, f32)[0], np.asarray(g_final, f32)]))
    cols = np.zeros((128, 48), f32)
    cols[:, 0:8] = np.asarray(hgrn_norm_g, f32)[0].reshape(8, 128).T
    cols[:, 8:16] = np.asarray(pool_scale, f32)[0].reshape(8, 128).T
    lbl = np.asarray(lb_logits, f32)
    cols[:, 16:48] = lbl.reshape(2, 2, 8, 128).transpose(3, 0, 1, 2).reshape(128, 32)
    shared = {
        "w_in": np.ascontiguousarray(np.asarray(w_in, f32)[0]),
        "w_a": np.ascontiguousarray(np.asarray(w_branch_a, f32)[0]),
        "w_b": np.ascontiguousarray(np.asarray(w_branch_b, f32)[0]),
        "w_o": np.ascontiguousarray(np.asarray(w_out, f32)[0]),
        "pool_w": np.ascontiguousarray(np.asarray(pool_w, f32)[0]),
        "w_ffn_in": np.ascontiguousarray(np.asarray(w_ffn_in, f32)[0]),
        "w_ffn_out": np.ascontiguousarray(np.asarray(w_ffn_out, f32)[0]),
        "gvec": gvec,
        "cols": cols,
        "cb": _const_bf16(),
    }
    if "nc" not in _NC_CACHE:
        _NC_CACHE["nc"] = build_program()
    nc = _NC_CACHE["nc"]
    in_maps = []
    for c in range(NCORES):
        m = dict(shared)
        m["x"] = np.ascontiguousarray(xs_[c])
        in_maps.append(m)
    res = run_bass_kernel_spmd(nc, in_maps, core_ids=list(range(NCORES)))
    out = np.stack([np.asarray(r["out"], f32) for r in res.results], axis=0)
    return out.reshape(B, SEQ, D)
```

```python
import numpy as np
import ml_dtypes
from contextlib import ExitStack
import concourse.bass as bass
import concourse.mybir as mybir
from concourse.bass_utils import run_bass_kernel_spmd

F32 = mybir.dt.float32
BF16 = mybir.dt.bfloat16
AF = mybir.ActivationFunctionType
ALU = mybir.AluOpType

NCORES = 8
D = 1024
SEQ = 2048
NSEQ = 2
NB = SEQ // 128
DFF = 2816
NFC = DFF // 128
EPS = 1e-6
ENGS = ["pe", "act", "dve", "pool", "sp"]
DEBUG = False


class St:
    __slots__ = ("w", "r", "dsem", "dcount")

    def __init__(self):
        self.w = {}
        self.r = {}
        self.dsem = None
        self.dcount = 0


class Buf:
    def __init__(self, t, st=None):
        self.t = t
        self.st = st if st is not None else St()


class Item:
    __slots__ = ("waits", "fn", "inc")

    def __init__(self, waits, fn, inc):
        self.waits = waits
        self.fn = fn
        self.inc = inc


class Prog:
    def __init__(self):
        self.q = {e: [] for e in ENGS}
        self.cnt = {e: 0 for e in ENGS}
        self.seen = {e: {} for e in ENGS}
        self.ndma = 0
        self.dsts = []

    def barrier(self):
        for eng in ENGS:
            waits = []
            seen = self.seen[eng]
            for f in ("pe", "act", "dve", "pool"):
                if f != eng and self.cnt[f] > seen.get(f, 0):
                    seen[f] = self.cnt[f]
                    waits.append((f, self.cnt[f]))
            for st in self.dsts:
                key = ("dma", st.dsem)
                if st.dcount > seen.get(key, 0):
                    seen[key] = st.dcount
                    waits.append((key, st.dcount))
            if waits:
                self.q[eng].append(Item(waits, None, None))

    def _waits(self, eng, reads, writes):
        need = {}

        def add(key, val):
            if val > need.get(key, 0):
                need[key] = val

        for st in reads:
            for k, c in st.w.items():
                add(k, c)
        for st in writes:
            for k, c in st.w.items():
                if k != eng:
                    add(k, c)
            for k, c in st.r.items():
                if k != eng:
                    add(k, c)
        if eng == "pe":
            need.pop("pe", None)
        out = []
        seen = self.seen[eng]
        for key, val in need.items():
            if seen.get(key, 0) < val:
                seen[key] = val
                out.append((key, val))
        return out

    def op(self, eng, fn, reads=(), writes=(), signal=True):
        reads = [b.st for b in reads]
        writes = [b.st for b in writes]
        waits = self._waits(eng, reads, writes)
        if signal:
            self.cnt[eng] += 1
            c = self.cnt[eng]
            inc = (eng, 1)
        else:
            c = self.cnt[eng] + 1
            inc = None
        for st in writes:
            st.w = {eng: c}
            st.r = {}
        for st in reads:
            st.r[eng] = c
        self.q[eng].append(Item(waits, fn, inc))

    def dma(self, queue, fn, buf, is_load=True):
        st = buf.st
        if st.dsem is None:
            st.dsem = self.ndma
            self.ndma += 1
            self.dsts.append(st)
        if is_load:
            own = ("dma", st.dsem)
            prev_load = st.w.pop(own, None)
            waits = self._waits(queue, [], [st])
            if prev_load is not None:
                st.w[own] = prev_load
        else:
            waits = self._waits(queue, [st], [])
        st.dcount += 16
        key = ("dma", st.dsem)
        if is_load:
            st.w = {key: st.dcount}
            st.r = {}
        else:
            st.r[key] = st.dcount
        self.q[queue].append(Item(waits, fn, (key, 16)))

    def wait_all(self, eng, bufs):
        waits = self._waits(eng, [], [b.st for b in bufs])
        if waits:
            self.q[eng].append(Item(waits, None, None))

    def emit(self, block, sems_eng, sems_dma):
        def semof(key):
            if isinstance(key, tuple):
                return sems_dma[key[1]]
            return sems_eng[key]

        def body(engname):
            def _f(e):
                for it in self.q[engname]:
                    for key, val in it.waits:
                        e.wait_ge(semof(key), val)
                    if it.fn is not None:
                        ins = it.fn(e)
                        if it.inc is not None:
                            ins.then_inc(semof(it.inc[0]), it.inc[1])
            return _f

        block.tensor(body("pe"))
        block.scalar(body("act"))
        block.vector(body("dve"))
        block.gpsimd(body("pool"))
        block.sync(body("sp"))


def _pool_mats():
    wins = (2, 4, 8, 16)
    L = SEQ
    mats = np.zeros((128, 20, 128), np.float32)
    t = np.arange(L)
    for g, w in enumerate(wins):
        half = w // 2
        lo = np.clip(t - half + 1, 0, L)
        hi = np.clip(t + half + 1, 0, L)
        Pm = np.zeros((L, L), np.float32)
        for tt in range(L):
            Pm[tt, lo[tt]:hi[tt]] = 1.0 / float(hi[tt] - lo[tt])
        Pm -= np.eye(L, dtype=np.float32)

        def blk(tb, sb):
            return Pm[tb * 128:(tb + 1) * 128, sb * 128:(sb + 1) * 128].T

        mats[:, g * 5 + 0, :] = blk(5, 4)
        mats[:, g * 5 + 1, :] = blk(5, 5)
        mats[:, g * 5 + 2, :] = blk(5, 6)
        mats[:, g * 5 + 3, :] = blk(0, 0)
        mats[:, g * 5 + 4, :] = blk(NB - 1, NB - 1)
    return mats


def _const_bf16():
    cb = np.zeros((128, 26, 128), np.float32)
    cb[:, 0, :] = np.eye(128)
    cb[:, 1, :] = 1.0 / 128.0
    s = np.arange(128)[:, None]
    t = np.arange(128)[None, :]
    cb[:, 2, :] = (s <= t)
    cb[:, 3, :] = (s >= t)
    cb[:, 4, :] = (s <= t)
    cb[:, 5, :] = (s >= t)
    cb[:, 6:26, :] = _pool_mats()
    return cb.reshape(128, 26 * 128).astype(ml_dtypes.bfloat16)


def build_program():
    nc = bass.Bass("TRN2", target_bir_lowering=False)
    x_d = nc.dram_tensor("x", [NSEQ * SEQ, D], F32, kind="ExternalInput").ap()
    win_d = nc.dram_tensor("w_in", [D, 8 * D], F32, kind="ExternalInput").ap()
    wa_d = nc.dram_tensor("w_a", [D, D], F32, kind="ExternalInput").ap()
    wb_d = nc.dram_tensor("w_b", [D, D], F32, kind="ExternalInput").ap()
    wo_d = nc.dram_tensor("w_o", [D, D], F32, kind="ExternalInput").ap()
    pw_d = nc.dram_tensor("pool_w", [4, 256, 256], F32, kind="ExternalInput").ap()
    w1_d = nc.dram_tensor("w_ffn_in", [D, 2 * DFF], F32, kind="ExternalInput").ap()
    w2_d = nc.dram_tensor("w_ffn_out", [DFF, D], F32, kind="ExternalInput").ap()
    gv_d = nc.dram_tensor("gvec", [3, D], F32, kind="ExternalInput").ap()
    cols_d = nc.dram_tensor("cols", [128, 48], F32, kind="ExternalInput").ap()
    cb_d = nc.dram_tensor("cb", [128, 26 * 128], BF16, kind="ExternalInput").ap()
    out_d = nc.dram_tensor("out", [NSEQ * SEQ, D], F32, kind="ExternalOutput").ap()
    s_slab = nc.dram_tensor("s_slab", [12, 128, 8, 512], BF16).ap()
    s_w1 = nc.dram_tensor("s_w1", [NFC, 128, 2, 8, 128], BF16).ap()
    s_w2 = nc.dram_tensor("s_w2", [DFF, D], BF16).ap()

    P = Prog()
    with ExitStack() as es:
        def sb(name, shape, dt):
            return Buf(es.enter_context(nc.sbuf_tensor("sb_" + name, shape, dt)))

        gbc = [sb("gbc%d" % i, [128, D], BF16 if i < 2 else F32) for i in range(3)]
        cols = sb("cols", [128, 48], F32)
        cb = sb("cb", [128, 26, 128], BF16)
        lbt = sb("lbt", [128, 5, 16], F32)
        uT = sb("uT", [128, 8, SEQ], BF16)
        yaT = sb("yaT", [128, 8, SEQ], BF16)
        ident = cb.t[:, 0, :]
        onesm = cb.t[:, 1, :]
        masks4 = cb.t[:, 2:6, :]

        def pmat(g, kind):
            return cb.t[:, 6 + g * 5 + kind, :]

        PSB = [Buf(es.enter_context(nc.psum_tensor("psb%d" % i, [128, 512], F32))) for i in range(8)]
        ps_i = [0]

        def bank():
            b = PSB[ps_i[0] % 8]
            ps_i[0] += 1
            return b

        def v4(b):
            return b.t[:, :].rearrange("p (a b) -> p a b", a=4)

        def vt(b):
            return b.t[:, :].bitcast(BF16).rearrange("p (a b) -> p a b", a=8)

        xt = [sb("xt%d" % i, [128, D], F32) for i in range(2)]
        xs = [sb("xs%d" % i, [128, D], BF16) for i in range(2)]
        smask = sb("smask", [128, NB, 129], BF16)
        ssq = sb("ssq", [128, 3, 16], F32)
        ss2 = sb("ss2", [128, 3, 4], F32)
        ss3 = sb("ss3", [128, 3, 4], F32)

        ARENA = 111104
        AR = es.enter_context(nc.sbuf_tensor("AR", [128, ARENA // 2], BF16))
        cur = [0]

        def sb(name, shape, dt):
            n = 1
            for k in shape[1:]:
                n *= k
            nbytes = n * (4 if dt == F32 else 2)
            off = cur[0]
            cur[0] += (nbytes + 3) // 4 * 4
            assert cur[0] <= ARENA, (name, cur[0])
            ap = AR[:, off // 2:(off + nbytes) // 2]
            if dt == F32:
                ap = ap.bitcast(F32)
            if len(shape) == 3:
                ap = ap.rearrange("p (a b) -> p a b", a=shape[1])
            elif len(shape) == 4:
                ap = ap.rearrange("p (a b c) -> p a b c", a=shape[1], b=shape[2])
            return Buf(ap)

        wh = [sb("wh%d" % i, [128, 5, 8, 128], BF16) for i in range(2)]
        sgq = [sb("sgq%d" % i, [128, 512], BF16) for i in range(2)]
        q_s = sb("q_s", [128, SEQ], BF16)
        sog = [sb("sog%d" % i, [128, SEQ], BF16) for i in range(2)]
        kT = [sb("kT%d" % i, [128, SEQ], BF16) for i in range(2)]
        G = [sb("G%d" % i, [128, NB, 129], F32) for i in range(2)]
        qd = [sb("qd%d" % i, [128, SEQ], BF16) for i in range(2)]
        kiT = [sb("kiT%d" % i, [128, SEQ], BF16) for i in range(2)]
        eA0 = sb("eA", [128, SEQ], BF16)
        eA = [eA0, eA0]
        ki = [sb("ki%d" % i, [128, NB, 128], BF16) for i in range(2)]
        v_h = sb("v_h", [128, NB, 128], BF16)
        used = [sb("used%d" % i, [128, NB, 128], BF16) for i in range(2)]
        Ed = [sb("Ed%d" % i, [128, NB, 1], F32) for i in range(2)]
        Emat = sb("Emat", [128, 1024], F32)
        mlt = [sb("mlt%d" % i, [128, 16], F32) for i in range(2)]
        msk = [sb("msk%d" % i, [128, 4, 128], BF16) for i in range(4)]
        sq = sgq
        lnr = sb("lnr", [128, 512], F32)

        print("arena phase2 bytes", cur[0])
        cur[0] = 0
        wpool = sb("wpool", [128, 4, 2, 256], BF16)
        WA = [sb("WA%d" % i, [128, 8, 512], BF16) for i in range(3)]
        wgu = [sb("wgu%d" % i, [128, 2, 8, 128], BF16) for i in range(2)]
        w2s = [sb("w2s%d" % i, [128, D], BF16) for i in range(4)]
        R1 = sb("R1", [128, 12288], BF16).t
        pblk = Buf(R1[:, 0:6144].rearrange("p (a b) -> p a b", a=6))
        yT = Buf(R1[:, 6144:10240].rearrange("p (a b) -> p a b", a=8))
        ybT = Buf(R1[:, 0:4096].rearrange("p (a b) -> p a b", a=8))
        sga = Buf(R1[:, 4096:8192].rearrange("p (a b) -> p a b", a=8))
        sgb = Buf(R1[:, 8192:12288].rearrange("p (a b) -> p a b", a=8))
        actT = Buf(R1[:, 0:11264].rearrange("p (a b) -> p a b", a=NFC))
        h2 = sb("h2", [128, 4, D], F32)
        mT = sb("mT", [128, 8, 512], BF16)
        u2T = sb("u2T", [128, 8, 512], BF16)
        ost = xt
        tz = [sb("tz%d" % i, [128, 512], BF16) for i in range(2)]
        sgt = [sb("sgt%d" % i, [128, 512], BF16) for i in range(2)]
        print("arena phase3 bytes", cur[0])

        dbg_bufs = []

        def dump(name, ap, buf, shape, dt):
            if not DEBUG:
                return
            dd = nc.dram_tensor("dbg_" + name, list(shape), dt, kind="ExternalOutput").ap()
            P.dma("sp", (lambda dd, ap: lambda e: e.dma_start(out=dd, in_=ap))(dd, ap), buf, is_load=False)
            dbg_bufs.append(buf)

        def L(meth, *a, **kw):
            return lambda e: getattr(e, meth)(*a, **kw)

        for i in range(3):
            P.dma("pool" if i < 2 else "sp", L("dma_start", out=gbc[i].t[:], in_=gv_d[i].partition_broadcast(128)), gbc[i])
        P.dma("sp", L("dma_start", out=cols.t[:], in_=cols_d), cols)
        P.dma("sp", L("dma_start", out=cb.t[:], in_=cb_d.rearrange("p (a b) -> p a b", a=26)), cb)
        P.op("pool", L("memset", smask.t[:, :, :], 1.0), [], [smask])
        P.op("pool", L("memset", smask.t[:, :, 0:1], 0.0), [], [smask])
        for a in range(2):
            P.op("dve", L("tensor_tensor", out=lbt.t[:, 0, a * 8:(a + 1) * 8],
                          in0=cols.t[:, 16 + (a * 2 + 1) * 8:16 + (a * 2 + 2) * 8],
                          in1=cols.t[:, 16 + (a * 2) * 8:16 + (a * 2 + 1) * 8], op=ALU.subtract), [cols], [lbt])
        P.op("act", L("activation", out=lbt.t[:, 4, :], in_=lbt.t[:, 0, :], func=AF.Exp), [lbt], [lbt])
        P.op("dve", L("tensor_scalar_add", out=lbt.t[:, 0, :], in0=lbt.t[:, 4, :], scalar1=1.0), [lbt], [lbt])
        P.op("dve", L("reciprocal", out=lbt.t[:, 1, :], in_=lbt.t[:, 0, :]), [lbt], [lbt])
        P.op("dve", L("tensor_scalar", out=lbt.t[:, 2, :], in0=lbt.t[:, 1, :], scalar1=-1.0, scalar2=1.0,
                      op0=ALU.mult, op1=ALU.add), [lbt], [lbt])
        P.op("dve", L("tensor_scalar_add", out=lbt.t[:, 3, :], in0=lbt.t[:, 1, :], scalar1=-1.0), [lbt], [lbt])

        def rms_stats(src_ap, src_buf, ssb, col, junk):
            P.op("act", L("activation", out=junk.t[:], in_=src_ap, func=AF.Square,
                          accum_out=ssb.t[:, 0, col:col + 1]), [src_buf], [junk, ssb])
            P.op("act", L("activation", out=ssb.t[:, 1, col:col + 1], in_=ssb.t[:, 0, col:col + 1], func=AF.Ln,
                          scale=1.0 / D, bias=EPS), [ssb], [ssb])
            P.op("act", L("activation", out=ssb.t[:, 2, col:col + 1], in_=ssb.t[:, 1, col:col + 1], func=AF.Exp,
                          scale=-0.5), [ssb], [ssb])

        def fm_prep(src_ap, src_buf, rstd_ap, rstd_buf, gb, slot):
            P.op("dve", L("scalar_tensor_tensor", out=xs[slot].t[:], in0=src_ap, scalar=rstd_ap, in1=gb.t[:],
                          op0=ALU.mult, op1=ALU.mult), [src_buf, rstd_buf, gb], [xs[slot]])

        def fm_transpose(slot, dst_ap, dst_buf, k):
            pb = bank()
            for c in range(8):
                P.op("pe", L("transpose", vt(pb)[:, c, :], xs[slot].t[:, c * 128:(c + 1) * 128], ident),
                     [xs[slot], cb], [pb], signal=(c == 7))
            if k % 2 == 0:
                P.op("act", L("activation", out=dst_ap, in_=vt(pb), func=AF.Copy), [pb], [dst_buf])
            else:
                P.op("dve", L("tensor_copy", out=dst_ap, in_=vt(pb)), [pb], [dst_buf])

        def to_feature_major(src_ap, src_buf, rstd_ap, rstd_buf, gb, slot, dst_ap, dst_buf, k):
            fm_prep(src_ap, src_buf, rstd_ap, rstd_buf, gb, slot)
            fm_transpose(slot, dst_ap, dst_buf, k)

        wa_i = [0]

        def load_slab(idx):
            b = WA[wa_i[0] % 3]
            wa_i[0] += 1
            P.dma("sp", L("dma_start", out=b.t[:, :, :], in_=s_slab[idx]), b)
            return b

        def load_head(h):
            b = wh[h % 2]
            for seg in range(5):
                P.dma("pool", L("dma_start", out=b.t[:, seg, :, :],
                                in_=win_d[:, seg * D + h * 128:seg * D + (h + 1) * 128].rearrange("(c p) n -> p c n", p=128)), b)
            return b

        conv_dummy = [Buf(None) for _ in range(4)]
        conv_jobs = []
        for c in range(8):
            rows = slice(c * 128, (c + 1) * 128)
            conv_jobs.append((s_slab[0:6, :, c, :].rearrange("s p n -> p s n"),
                              win_d[rows, 5 * D:8 * D].rearrange("p (s n) -> p s n", n=512)))
            for mi, wd in enumerate((wa_d, wb_d, wo_d)):
                conv_jobs.append((s_slab[6 + 2 * mi:8 + 2 * mi, :, c, :].rearrange("s p n -> p s n"),
                                  wd[rows, :].rearrange("p (s n) -> p s n", n=512)))
            conv_jobs.append((s_w1[:, :, :, c, :].rearrange("f p k n -> p k f n"),
                              w1_d[rows, :].rearrange("p (k f n) -> p k f n", k=2, n=128)))
        for f2 in range(0, NFC, 2):
            conv_jobs.append((s_w2[f2 * 128:(f2 + 2) * 128, :], w2_d[f2 * 128:(f2 + 2) * 128, :]))
        conv_i = [0]

        def issue_conv(n):
            for _ in range(n):
                if conv_i[0] >= len(conv_jobs):
                    return
                o_ap, i_ap = conv_jobs[conv_i[0]]
                P.dma("pool", L("dma_start", out=o_ap, in_=i_ap), conv_dummy[conv_i[0] % 4])
                conv_i[0] += 1

        def mm_group(out_ap, pairs, reads, pb, last_signal=True):
            n = len(pairs)
            for k, (lh, rh) in enumerate(pairs):
                P.op("pe", L("matmul", out_ap, lhsT=lh, rhs=rh, start=(k == 0), stop=(k == n - 1)), reads, [pb],
                     signal=(last_signal and k == n - 1))

        for s in range(NSEQ):
            row0 = s * SEQ
            for b in range(NB):
                sl = b % 2
                P.dma("sp", L("dma_start", out=xt[sl].t[:], in_=x_d[row0 + b * 128:row0 + (b + 1) * 128, :]), xt[sl])
                rms_stats(xt[sl].t[:], xt[sl], ssq, b, xs[sl])
                to_feature_major(xt[sl].t[:], xt[sl], ssq.t[:, 2, b:b + 1], ssq, gbc[0], sl,
                                 uT.t[:, :, b * 128:(b + 1) * 128], uT, b)

            if s == 0:
                dump("uT", uT.t[:, :, :], uT, [128, 8, SEQ], BF16)
            P.barrier()
            for d_ in range(2):
                P.op("pool", L("memset", used[d_].t[:, :, :], 0.0), [], [used[d_]])
                P.op("pool", L("memset", G[d_].t[:, :, 0:1], 0.0), [], [G[d_]])
                P.op("pool", L("memset", mlt[d_].t[:, :], 0.0), [], [mlt[d_]])

            def proj_and_sig(h, wcur):
                sg_o = sog[h % 2]
                for tt in range(4):
                    tsl = slice(tt * 512, (tt + 1) * 512)
                    banks = []
                    for si in (0, 1, 2, 4):
                        pb = bank()
                        banks.append(pb)
                        mm_group(pb.t[:, :], [(wcur.t[:, si, c, :], uT.t[:, c, tsl]) for c in range(8)], [wcur, uT], pb)
                    pq, pf, pbk, pog = banks
                    for (pz, dst, k) in ((pq, q_s, 0), (pog, sg_o, 1)):
                        sg = sgq[k]
                        P.op("act", L("activation", out=sg.t[:, :], in_=pz.t[:, :], func=AF.Sigmoid), [pz], [sg])
                        P.op("dve", L("tensor_tensor", out=dst.t[:, tsl], in0=pz.t[:, :], in1=sg.t[:, :], op=ALU.mult), [pz, sg], [dst])
                    for d_, pz in ((0, pf), (1, pbk)):
                        gsl = G[d_].t[:, tt * 4:(tt + 1) * 4, 1:129]
                        col = d_ * 8 + h
                        P.op("act", L("activation", out=gsl, in_=v4(pz), func=AF.Sigmoid), [pz], [G[d_]])
                        P.op("dve", L("tensor_scalar", out=kT[d_].t[:, tsl].rearrange("p (a b) -> p a b", a=4), in0=gsl,
                                      scalar1=lbt.t[:, 3, col:col + 1], scalar2=lbt.t[:, 2, col:col + 1],
                                      op0=ALU.mult, op1=ALU.add), [G[d_], lbt], [kT[d_]])

            def decay_ln(h):
                for d_ in range(2):
                    col = d_ * 8 + h
                    gin = G[d_].t[:, :, 1:129]
                    P.op("act", L("activation", out=gin, in_=gin, func=AF.Ln, scale=lbt.t[:, 2, col:col + 1],
                                  bias=lbt.t[:, 1, col:col + 1]), [G[d_], lbt], [G[d_]])

            def decay_scan(d_):
                gfl = G[d_].t[:, :, :].rearrange("p a b -> p (a b)")
                P.op("dve", L("tensor_tensor_scan", out=gfl, data0=smask.t[:, :, :].rearrange("p a b -> p (a b)"), data1=gfl,
                              initial=0.0, op0=ALU.mult, op1=ALU.add), [G[d_], smask], [G[d_]])

            def decay_a(h):
                decay_ln(h)
                decay_scan(0)
                decay_scan(1)

            def decay_b(h):
                for d_ in range(2):
                    qv = qd[d_].t[:, :].rearrange("p (a b) -> p a b", a=NB)
                    ev = eA[d_].t[:, :].rearrange("p (a b) -> p a b", a=NB)
                    if d_ == 0:
                        cq, sq_, sk_ = G[0].t[:, :, 1:129], 1.0, -1.0
                    else:
                        cq, sq_, sk_ = G[1].t[:, :, 0:128], -1.0, 1.0
                    P.op("act", L("activation", out=ev, in_=cq, func=AF.Exp, scale=sk_), [G[d_]], [eA[d_]])
                    P.op("act", L("activation", out=qv, in_=cq, func=AF.Exp, scale=sq_), [G[d_]], [qd[d_]])
                    P.op("act", L("activation", out=Ed[d_].t[:, :, :], in_=G[d_].t[:, :, 128:129], func=AF.Exp), [G[d_]], [Ed[d_]])
                    P.op("dve", L("tensor_tensor", out=kiT[d_].t[:, :], in0=kT[d_].t[:, :], in1=eA[d_].t[:, :], op=ALU.mult),
                         [kT[d_], eA[d_]], [kiT[d_]])
                    P.op("pool", L("tensor_tensor", out=qd[d_].t[:, :], in0=qd[d_].t[:, :], in1=q_s.t[:, :], op=ALU.mult),
                         [qd[d_], q_s], [qd[d_]])

            def vproj(h, wcur):
                for jg in range(4):
                    pb = bank()
                    for jj in range(4):
                        j = jg * 4 + jj
                        mm_group(v4(pb)[:, jj, :], [(uT.t[:, c, j * 128:(j + 1) * 128], wcur.t[:, 3, c, :]) for c in range(8)],
                                 [uT, wcur], pb, last_signal=(jj == 3))
                    P.op("act", L("activation", out=v_h.t[:, jg * 4:(jg + 1) * 4, :], in_=v4(pb), func=AF.Copy), [pb], [v_h])

            def rec_chain(h):
                for d_ in range(2):
                    for jg in range(2):
                        pb = bank()
                        for jj in range(8):
                            j = jg * 8 + jj
                            P.op("pe", L("transpose", vt(pb)[:, jj, :], kiT[d_].t[:, j * 128:(j + 1) * 128], ident),
                                 [kiT[d_], cb], [pb], signal=(jj == 7))
                        if jg == 0:
                            P.op("dve", L("tensor_copy", out=ki[d_].t[:, jg * 8:(jg + 1) * 8, :], in_=vt(pb)), [pb], [ki[d_]])
                        else:
                            P.op("act", L("activation", out=ki[d_].t[:, jg * 8:(jg + 1) * 8, :], in_=vt(pb), func=AF.Copy), [pb], [ki[d_]])
                for d_ in range(2):
                    order = list(range(NB)) if d_ == 0 else list(range(NB - 1, -1, -1))
                    if d_ == 0:
                        P.op("pool", L("tensor_copy", out=mlt[0].t[:, 0:15], in_=Ed[0].t[:, 0:15, 0]), [Ed[0]], [mlt[0]])
                    else:
                        P.op("pool", L("tensor_copy", out=mlt[1].t[:, 0:15], in_=Ed[1].t[:, 14::-1, 0]), [Ed[1]], [mlt[1]])
                    em3 = Emat.t[:, :].rearrange("p (v j) -> p v j", j=16)
                    P.op("pool", L("tensor_copy", out=em3, in_=mlt[d_].t[:, :].unsqueeze(1).broadcast_to([128, 64, 16])), [mlt[d_]], [Emat])
                    for g4 in range(4):
                        pb = bank()
                        for slot in range(4):
                            j = order[g4 * 4 + slot]
                            P.op("pe", L("matmul", v4(pb)[:, slot, :], lhsT=ki[d_].t[:, j, :], rhs=v_h.t[:, j, :], start=True, stop=True),
                                 [ki[d_], v_h], [pb], signal=(slot == 3))
                        for hf in range(2):
                            dst = xt[hf].t[:, :].rearrange("p (v j) -> p v j", j=16)[:, :, g4 * 4:(g4 + 1) * 4].rearrange("p v j -> p j v")
                            src = v4(pb)[:, :, hf * 64:(hf + 1) * 64]
                            if g4 % 2 == 0:
                                P.op("act", L("activation", out=dst, in_=src, func=AF.Copy), [pb], [xt[hf]])
                            else:
                                P.op("dve", L("tensor_copy", out=dst, in_=src), [pb], [xt[hf]])
                    for hf in range(2):
                        P.op("dve", L("tensor_tensor_scan", out=xt[hf].t[:, :], data0=xt[hf].t[:, :], data1=Emat.t[:, :], initial=0.0,
                                      op0=ALU.add, op1=ALU.mult), [Emat, xt[hf]], [xt[hf]])
                        w3 = xt[hf].t[:, :].rearrange("p (v j) -> p v j", j=16)
                        P.op("act", L("activation", out=used[d_].t[:, 1:16, hf * 64:(hf + 1) * 64],
                                      in_=w3[:, :, 0:15].rearrange("p v j -> p j v"), func=AF.Copy), [xt[hf]], [used[d_]])
            def rec_sweep(h, scan_next):
                sg_o = sog[h % 2]
                pscs = {}

                def scores(g4):
                    pscs[g4] = [bank(), bank()]
                    for jj in range(4):
                        j = g4 * 4 + jj
                        bsl = slice(j * 128, (j + 1) * 128)
                        psc = pscs[g4][jj // 2]
                        so = 2 * (jj % 2)
                        for d_ in range(2):
                            P.op("pe", L("matmul", v4(psc)[:, so + d_, :], lhsT=kiT[d_].t[:, bsl], rhs=qd[d_].t[:, bsl], start=True, stop=True),
                                 [kiT[d_], qd[d_]], [psc], signal=(jj % 2 == 1 and d_ == 1))
                    for b2 in range(2):
                        mk = msk[2 * (g4 % 2) + b2]
                        P.op("dve", L("tensor_tensor", out=mk.t[:, :, :], in0=v4(pscs[g4][b2]), in1=masks4, op=ALU.mult),
                             [pscs[g4][b2], cb], [mk])

                pos = {}

                def omain(g4):
                    po = bank()
                    pos[g4] = po
                    for jj in range(4):
                        j = g4 * 4 + jj
                        bsl = slice(j * 128, (j + 1) * 128)
                        mk = msk[2 * (g4 % 2) + jj // 2]
                        so = 2 * (jj % 2)
                        osl = slice(jj * 128, (jj + 1) * 128)
                        mm_group(po.t[:, osl], [(v_h.t[:, j, :], mk.t[:, so, :]), (v_h.t[:, j, :], mk.t[:, so + 1, :]),
                                                (used[0].t[:, j, :], qd[0].t[:, bsl]), (used[1].t[:, NB - 1 - j, :], qd[1].t[:, bsl])],
                                 [v_h, mk, used[0], used[1], qd[0], qd[1]], po)
                    P.op("act", L("activation", out=sq[g4 % 2].t[:, :], in_=po.t[:, :], func=AF.Square), [po], [sq[g4 % 2]])

                def otail(g4):
                    po = pos[g4]
                    tsl = slice(g4 * 512, (g4 + 1) * 512)
                    pm = bank()
                    P.op("pe", L("matmul", pm.t[:, :], lhsT=onesm, rhs=sq[g4 % 2].t[:, :], start=True, stop=True), [cb, sq[g4 % 2]], [pm])
                    P.op("act", L("activation", out=lnr.t[:, :], in_=pm.t[:, :], func=AF.Ln, bias=EPS), [pm], [lnr])
                    P.op("act", L("activation", out=lnr.t[:, :], in_=lnr.t[:, :], func=AF.Exp, scale=-0.5), [lnr], [lnr])
                    P.op("dve", L("tensor_tensor", out=lnr.t[:, :], in0=po.t[:, :], in1=lnr.t[:, :], op=ALU.mult), [po, lnr], [lnr])
                    P.op("dve", L("scalar_tensor_tensor", out=yaT.t[:, h, tsl], in0=lnr.t[:, :], scalar=cols.t[:, h:h + 1], in1=sg_o.t[:, tsl],
                                  op0=ALU.mult, op1=ALU.mult), [lnr, cols, sg_o], [yaT])

                scores(0)
                scores(1)
                omain(0)
                for g4 in range(1, 4):
                    if g4 + 1 < 4:
                        scores(g4 + 1)
                    omain(g4)
                    if scan_next and g4 >= 2:
                        decay_scan(g4 - 2)
                    otail(g4 - 1)
                otail(3)

            whs = {0: load_head(0)}
            proj_and_sig(0, whs[0])
            decay_a(0)
            for h in range(8):
                if h + 1 < 8:
                    whs[h + 1] = load_head(h + 1)
                if s == 0:
                    issue_conv(7)
                decay_b(h)
                vproj(h, whs[h])
                rec_chain(h)
                if h + 1 < 8:
                    proj_and_sig(h + 1, whs[h + 1])
                    decay_ln(h + 1)
                rec_sweep(h, h + 1 < 8)
            if s == 0:
                dump("yaT", yaT.t[:, :, :], yaT, [128, 8, SEQ], BF16)
            P.barrier()
            P.dma("pool", L("dma_start", out=wpool.t[:, :, :, :], in_=pw_d.rearrange("g (k p) n -> p g k n", p=128)), wpool)
            for tt in range(4):
                tsl = slice(tt * 512, (tt + 1) * 512)
                blks = [jb for jb in range(4 * tt - 1, 4 * tt + 5) if 0 <= jb < NB]
                slot_of = {jb: i for i, jb in enumerate(blks)}
                for half in range(2):
                    wsl = load_slab(half)
                    for k, jb in enumerate(blks):
                        pb = bank()
                        mm_group(pb.t[:, :], [(uT.t[:, c, jb * 128:(jb + 1) * 128], wsl.t[:, c, :]) for c in range(8)], [uT, wsl], pb)
                        dst = pblk.t[:, slot_of[jb], half * 512:(half + 1) * 512]
                        if k % 2 == 0:
                            P.op("act", L("activation", out=dst, in_=pb.t[:, :], func=AF.Copy), [pb], [pblk])
                        else:
                            P.op("dve", L("tensor_copy", out=dst, in_=pb.t[:, :]), [pb], [pblk])
                for c in range(8):
                    g = c // 2
                    pb = bank()
                    for jj in range(4):
                        j = 4 * tt + jj
                        srcs = []
                        if j - 1 >= 0:
                            srcs.append((j - 1, 0))
                        srcs.append((j, 3 if j == 0 else (4 if j == NB - 1 else 1)))
                        if j + 1 < NB:
                            srcs.append((j + 1, 2))
                        mm_group(pb.t[:, jj * 128:(jj + 1) * 128],
                                 [(pblk.t[:, slot_of[jb], c * 128:(c + 1) * 128], pmat(g, kind)) for (jb, kind) in srcs],
                                 [pblk, cb], pb, last_signal=(jj == 3))
                    if c % 2 == 0:
                        P.op("act", L("activation", out=yT.t[:, c, :], in_=pb.t[:, :], func=AF.Copy), [pb], [yT])
                    else:
                        P.op("dve", L("tensor_copy", out=yT.t[:, c, :], in_=pb.t[:, :]), [pb], [yT])
                for c2 in range(8):
                    g, hf = c2 // 2, c2 % 2
                    pb = bank()
                    mm_group(pb.t[:, :], [(wpool.t[:, g, kc, hf * 128:(hf + 1) * 128], yT.t[:, 2 * g + kc, :]) for kc in range(2)], [wpool, yT], pb)
                    P.op("dve", L("tensor_scalar", out=ybT.t[:, c2, :], in0=pb.t[:, :], scalar1=cols.t[:, 8 + c2:9 + c2], scalar2=None,
                                  op0=ALU.mult), [pb, cols], [ybT])
                if s == 0 and tt == 0:
                    dump("ybT", ybT.t[:, :, :], ybT, [128, 8, 512], BF16)
                for gi, (gbuf, seg) in enumerate(((sga, 6), (sgb, 7))):
                    for half in range(2):
                        wsl = load_slab((seg - 5) * 2 + half)
                        for cc in range(4):
                            pb = bank()
                            mm_group(pb.t[:, :], [(wsl.t[:, c, cc * 128:(cc + 1) * 128], uT.t[:, c, tsl]) for c in range(8)], [wsl, uT], pb)
                            P.op("act", L("activation", out=gbuf.t[:, half * 4 + cc, :], in_=pb.t[:, :], func=AF.Sigmoid), [pb], [gbuf])
                for half in range(2):
                    wsa = load_slab(6 + half)
                    wsb = load_slab(8 + half)
                    for cc in range(4):
                        dc = half * 4 + cc
                        pa = bank()
                        pb2 = bank()
                        mm_group(pa.t[:, :], [(wsa.t[:, c, cc * 128:(cc + 1) * 128], yaT.t[:, c, tsl]) for c in range(8)], [wsa, yaT], pa)
                        mm_group(pb2.t[:, :], [(wsb.t[:, c, cc * 128:(cc + 1) * 128], ybT.t[:, c, :]) for c in range(8)], [wsb, ybT], pb2)
                        P.op("dve", L("tensor_tensor", out=tz[0].t[:, :], in0=pa.t[:, :], in1=sga.t[:, dc, :], op=ALU.mult), [pa, sga], [tz[0]])
                        P.op("dve", L("tensor_tensor", out=tz[1].t[:, :], in0=pb2.t[:, :], in1=sgb.t[:, dc, :], op=ALU.mult), [pb2, sgb], [tz[1]])
                        P.op("pool", L("tensor_tensor", out=mT.t[:, dc, :], in0=tz[0].t[:, :], in1=tz[1].t[:, :], op=ALU.add), [tz[0], tz[1]], [mT])
                if s == 0 and tt == 0:
                    dump("sga", sga.t[:, :, :], sga, [128, 8, 512], BF16)
                    dump("mT", mT.t[:, :, :], mT, [128, 8, 512], BF16)
                wso = [load_slab(10 + half) for half in range(2)]
                def blk_mm(jj):
                    j = 4 * tt + jj
                    sl = jj % 2
                    P.dma("sp", L("dma_start", out=xt[sl].t[:], in_=x_d[row0 + j * 128:row0 + (j + 1) * 128, :]), xt[sl])
                    for half in range(2):
                        pb = bank()
                        mm_group(pb.t[:, :], [(mT.t[:, c, jj * 128:(jj + 1) * 128], wso[half].t[:, c, :]) for c in range(8)], [mT, wso[half]], pb)
                        P.op("dve", L("tensor_tensor", out=h2.t[:, jj, half * 512:(half + 1) * 512], in0=pb.t[:, :],
                                      in1=xt[sl].t[:, half * 512:(half + 1) * 512], op=ALU.add), [pb, xt[sl]], [h2])
                    rms_stats(h2.t[:, jj, :], h2, ss2, jj, xs[sl])
                    fm_prep(h2.t[:, jj, :], h2, ss2.t[:, 2, jj:jj + 1], ss2, gbc[1], sl)

                def blk_tr(jj):
                    fm_transpose(jj % 2, u2T.t[:, :, jj * 128:(jj + 1) * 128], u2T, jj)

                blk_mm(0)
                for jj in range(1, 4):
                    blk_mm(jj)
                    blk_tr(jj - 1)
                blk_tr(3)
                if s == 0 and tt == 0:
                    dump("h2", h2.t[:, :, :], h2, [128, 4, D], F32)
                    dump("u2T", u2T.t[:, :, :], u2T, [128, 8, 512], BF16)
                for fc in range(NFC):
                    wg = wgu[fc % 2]
                    P.dma("sp", L("dma_start", out=wg.t[:, :, :, :], in_=s_w1[fc]), wg)
                    pg = bank()
                    pu = bank()
                    mm_group(pg.t[:, :], [(wg.t[:, 0, c, :], u2T.t[:, c, :]) for c in range(8)], [wg, u2T], pg)
                    mm_group(pu.t[:, :], [(wg.t[:, 1, c, :], u2T.t[:, c, :]) for c in range(8)], [wg, u2T], pu)
                    sg = sgt[fc % 2]
                    P.op("act", L("activation", out=sg.t[:, :], in_=pg.t[:, :], func=AF.Silu), [pg], [sg])
                    P.op("dve", L("tensor_tensor", out=actT.t[:, fc, :], in0=pu.t[:, :], in1=sg.t[:, :], op=ALU.mult), [pu, sg], [actT])
                if s == 0 and tt == 0:
                    dump("actT", actT.t[:, :, :], actT, [128, NFC, 512], BF16)
                for half in range(2):
                    hsl = slice(half * 512, (half + 1) * 512)
                    obanks = [bank() for _ in range(4)]
                    for fc in range(NFC):
                        w2 = w2s[(half * NFC + fc) % 4]
                        P.dma("sp", L("dma_start", out=w2.t[:, 0:512], in_=s_w2[fc * 128:(fc + 1) * 128, hsl]), w2)
                        for jj in range(4):
                            pb = obanks[jj]
                            P.op("pe", L("matmul", pb.t[:, :], lhsT=actT.t[:, fc, jj * 128:(jj + 1) * 128], rhs=w2.t[:, 0:512],
                                         start=(fc == 0), stop=(fc == NFC - 1)), [actT, w2], [pb],
                                 signal=(fc == NFC - 1) or (jj == 3))
                    for jj in range(4):
                        pb = obanks[jj]
                        P.op("dve", L("tensor_tensor", out=h2.t[:, jj, hsl], in0=pb.t[:, :], in1=h2.t[:, jj, hsl], op=ALU.add), [pb, h2], [h2])
                for jj in range(4):
                    j = 4 * tt + jj
                    rms_stats(h2.t[:, jj, :], h2, ss3, jj, xs[jj % 2])
                    ob = ost[jj % 2]
                    P.op("dve", L("scalar_tensor_tensor", out=ob.t[:, :], in0=h2.t[:, jj, :], scalar=ss3.t[:, 2, jj:jj + 1], in1=gbc[2].t[:, :],
                                  op0=ALU.mult, op1=ALU.mult), [h2, ss3, gbc[2]], [ob])
                    P.dma("sp", L("dma_start", out=out_d[row0 + j * 128:row0 + (j + 1) * 128, :], in_=ob.t[:, :]), ob, is_load=False)
        P.wait_all("sp", list(ost) + dbg_bufs)

        sems_eng = {k: es.enter_context(nc.semaphore("sem_" + k)) for k in ["pe", "act", "dve", "pool"]}
        sems_dma = [es.enter_context(nc.semaphore("dsem%d" % i)) for i in range(P.ndma)]
        with nc.Block() as block:
            P.emit(block, sems_eng, sems_dma)
    return nc


_NC_CACHE = {}


def kernel(x, g_mix, w_in, lb_logits, hgrn_norm_g, pool_w, pool_scale, w_branch_a, w_branch_b, w_out,
           g_ffn, w_ffn_in, w_ffn_out, g_final):
    f32 = np.float32
    x = np.asarray(x, f32)
    B = x.shape[0]
    xs_ = x.reshape(NCORES, NSEQ * SEQ, D)
    gvec = np.ascontiguousarray(np.stack([np.asarray(g_mix, f32)[0], np.asarray(g_ffn, f32)[0], np.asarray(g_final, f32)]))
    cols = np.zeros((128, 48), f32)
    cols[:, 0:8] = np.asarray(hgrn_norm_g, f32)[0].reshape(8, 128).T
    cols[:, 8:16] = np.asarray(pool_scale, f32)[0].reshape(8, 128).T
    lbl = np.asarray(lb_logits, f32)
    cols[:, 16:48] = lbl.reshape(2, 2, 8, 128).transpose(3, 0, 1, 2).reshape(128, 32)
    shared = {
        "w_in": np.ascontiguousarray(np.asarray(w_in, f32)[0]),
        "w_a": np.ascontiguousarray(np.asarray(w_branch_a, f32)[0]),
        "w_b": np.ascontiguousarray(np.asarray(w_branch_b, f32)[0]),
        "w_o": np.ascontiguousarray(np.asarray(w_out, f32)[0]),
        "pool_w": np.ascontiguousarray(np.asarray(pool_w, f32)[0]),
        "w_ffn_in": np.ascontiguousarray(np.asarray(w_ffn_in, f32)[0]),
        "w_ffn_out": np.ascontiguousarray(np.asarray(w_ffn_out, f32)[0]),
        "gvec": gvec,
        "cols": cols,
        "cb": _const_bf16(),
    }
    if "nc" not in _NC_CACHE:
        _NC_CACHE["nc"] = build_program()
    nc = _NC_CACHE["nc"]
    in_maps = []
    for c in range(NCORES):
        m = dict(shared)
        m["x"] = np.ascontiguousarray(xs_[c])
        in_maps.append(m)
    res = run_bass_kernel_spmd(nc, in_maps, core_ids=list(range(NCORES)))
    out = np.stack([np.asarray(r["out"], f32) for r in res.results], axis=0)
    return out.reshape(B, SEQ, D)
```

```python
import numpy as np
import ml_dtypes
from contextlib import ExitStack
import concourse.bass as bass
import concourse.mybir as mybir
from concourse.bass_utils import run_bass_kernel_spmd

F32 = mybir.dt.float32
BF16 = mybir.dt.bfloat16
AF = mybir.ActivationFunctionType
ALU = mybir.AluOpType

NCORES = 8
D = 1024
SEQ = 2048
NSEQ = 2
NB = SEQ // 128
DFF = 2816
NFC = DFF // 128
EPS = 1e-6
ENGS = ["pe", "act", "dve", "pool", "sp"]
DEBUG = False


class St:
    __slots__ = ("w", "r", "dsem", "dcount")

    def __init__(self):
        self.w = {}
        self.r = {}
        self.dsem = None
        self.dcount = 0


class Buf:
    def __init__(self, t, st=None):
        self.t = t
        self.st = st if st is not None else St()


class Item:
    __slots__ = ("waits", "fn", "inc")

    def __init__(self, waits, fn, inc):
        self.waits = waits
        self.fn = fn
        self.inc = inc


class Prog:
    def __init__(self):
        self.q = {e: [] for e in ENGS}
        self.cnt = {e: 0 for e in ENGS}
        self.seen = {e: {} for e in ENGS}
        self.ndma = 0
        self.dsts = []

    def barrier(self):
        for eng in ENGS:
            waits = []
            seen = self.seen[eng]
            for f in ("pe", "act", "dve", "pool"):
                if f != eng and self.cnt[f] > seen.get(f, 0):
                    seen[f] = self.cnt[f]
                    waits.append((f, self.cnt[f]))
            for st in self.dsts:
                key = ("dma", st.dsem)
                if st.dcount > seen.get(key, 0):
                    seen[key] = st.dcount
                    waits.append((key, st.dcount))
            if waits:
                self.q[eng].append(Item(waits, None, None))

    def _waits(self, eng, reads, writes):
        need = {}

        def add(key, val):
            if val > need.get(key, 0):
                need[key] = val

        for st in reads:
            for k, c in st.w.items():
                add(k, c)
        for st in writes:
            for k, c in st.w.items():
                if k != eng:
                    add(k, c)
            for k, c in st.r.items():
                if k != eng:
                    add(k, c)
        if eng == "pe":
            need.pop("pe", None)
        out = []
        seen = self.seen[eng]
        for key, val in need.items():
            if seen.get(key, 0) < val:
                seen[key] = val
                out.append((key, val))
        return out

    def op(self, eng, fn, reads=(), writes=(), signal=True):
        reads = [b.st for b in reads]
        writes = [b.st for b in writes]
        waits = self._waits(eng, reads, writes)
        if signal:
            self.cnt[eng] += 1
            c = self.cnt[eng]
            inc = (eng, 1)
        else:
            c = self.cnt[eng] + 1
            inc = None
        for st in writes:
            st.w = {eng: c}
            st.r = {}
        for st in reads:
            st.r[eng] = c
        self.q[eng].append(Item(waits, fn, inc))

    def dma(self, queue, fn, buf, is_load=True):
        st = buf.st
        if st.dsem is None:
            st.dsem = self.ndma
            self.ndma += 1
            self.dsts.append(st)
        if is_load:
            own = ("dma", st.dsem)
            prev_load = st.w.pop(own, None)
            waits = self._waits(queue, [], [st])
            if prev_load is not None:
                st.w[own] = prev_load
        else:
            waits = self._waits(queue, [st], [])
        st.dcount += 16
        key = ("dma", st.dsem)
        if is_load:
            st.w = {key: st.dcount}
            st.r = {}
        else:
            st.r[key] = st.dcount
        self.q[queue].append(Item(waits, fn, (key, 16)))

    def wait_all(self, eng, bufs):
        waits = self._waits(eng, [], [b.st for b in bufs])
        if waits:
            self.q[eng].append(Item(waits, None, None))

    def emit(self, block, sems_eng, sems_dma):
        def semof(key):
            if isinstance(key, tuple):
                return sems_dma[key[1]]
            return sems_eng[key]

        def body(engname):
            def _f(e):
                for it in self.q[engname]:
                    for key, val in it.waits:
                        e.wait_ge(semof(key), val)
                    if it.fn is not None:
                        ins = it.fn(e)
                        if it.inc is not None:
                            ins.then_inc(semof(it.inc[0]), it.inc[1])
            return _f

        block.tensor(body("pe"))
        block.scalar(body("act"))
        block.vector(body("dve"))
        block.gpsimd(body("pool"))
        block.sync(body("sp"))


def _pool_mats():
    wins = (2, 4, 8, 16)
    L = SEQ
    mats = np.zeros((128, 20, 128), np.float32)
    t = np.arange(L)
    for g, w in enumerate(wins):
        half = w // 2
        lo = np.clip(t - half + 1, 0, L)
        hi = np.clip(t + half + 1, 0, L)
        Pm = np.zeros((L, L), np.float32)
        for tt in range(L):
            Pm[tt, lo[tt]:hi[tt]] = 1.0 / float(hi[tt] - lo[tt])
        Pm -= np.eye(L, dtype=np.float32)

        def blk(tb, sb):
            return Pm[tb * 128:(tb + 1) * 128, sb * 128:(sb + 1) * 128].T

        mats[:, g * 5 + 0, :] = blk(5, 4)
        mats[:, g * 5 + 1, :] = blk(5, 5)
        mats[:, g * 5 + 2, :] = blk(5, 6)
        mats[:, g * 5 + 3, :] = blk(0, 0)
        mats[:, g * 5 + 4, :] = blk(NB - 1, NB - 1)
    return mats


def _const_bf16():
    cb = np.zeros((128, 26, 128), np.float32)
    cb[:, 0, :] = np.eye(128)
    cb[:, 1, :] = 1.0 / 128.0
    s = np.arange(128)[:, None]
    t = np.arange(128)[None, :]
    cb[:, 2, :] = (s <= t)
    cb[:, 3, :] = (s >= t)
    cb[:, 4, :] = (s <= t)
    cb[:, 5, :] = (s >= t)
    cb[:, 6:26, :] = _pool_mats()
    return cb.reshape(128, 26 * 128).astype(ml_dtypes.bfloat16)


def build_program():
    nc = bass.Bass("TRN2", target_bir_lowering=False)
    x_d = nc.dram_tensor("x", [NSEQ * SEQ, D], F32, kind="ExternalInput").ap()
    win_d = nc.dram_tensor("w_in", [D, 8 * D], F32, kind="ExternalInput").ap()
    wa_d = nc.dram_tensor("w_a", [D, D], F32, kind="ExternalInput").ap()
    wb_d = nc.dram_tensor("w_b", [D, D], F32, kind="ExternalInput").ap()
    wo_d = nc.dram_tensor("w_o", [D, D], F32, kind="ExternalInput").ap()
    pw_d = nc.dram_tensor("pool_w", [4, 256, 256], F32, kind="ExternalInput").ap()
    w1_d = nc.dram_tensor("w_ffn_in", [D, 2 * DFF], F32, kind="ExternalInput").ap()
    w2_d = nc.dram_tensor("w_ffn_out", [DFF, D], F32, kind="ExternalInput").ap()
    gv_d = nc.dram_tensor("gvec", [3, D], F32, kind="ExternalInput").ap()
    cols_d = nc.dram_tensor("cols", [128, 48], F32, kind="ExternalInput").ap()
    cb_d = nc.dram_tensor("cb", [128, 26 * 128], BF16, kind="ExternalInput").ap()
    out_d = nc.dram_tensor("out", [NSEQ * SEQ, D], F32, kind="ExternalOutput").ap()
    s_slab = nc.dram_tensor("s_slab", [12, 128, 8, 512], BF16).ap()
    s_w1 = nc.dram_tensor("s_w1", [NFC, 128, 2, 8, 128], BF16).ap()
    s_w2 = nc.dram_tensor("s_w2", [DFF, D], BF16).ap()

    P = Prog()
    with ExitStack() as es:
        def sb(name, shape, dt):
            return Buf(es.enter_context(nc.sbuf_tensor("sb_" + name, shape, dt)))

        gbc = [sb("gbc%d" % i, [128, D], BF16 if i < 2 else F32) for i in range(3)]
        cols = sb("cols", [128, 48], F32)
        cb = sb("cb", [128, 26, 128], BF16)
        lbt = sb("lbt", [128, 5, 16], F32)
        uT = sb("uT", [128, 8, SEQ], BF16)
        yaT = sb("yaT", [128, 8, SEQ], BF16)
        ident = cb.t[:, 0, :]
        onesm = cb.t[:, 1, :]
        masks4 = cb.t[:, 2:6, :]

        def pmat(g, kind):
            return cb.t[:, 6 + g * 5 + kind, :]

        PSB = [Buf(es.enter_context(nc.psum_tensor("psb%d" % i, [128, 512], F32))) for i in range(8)]
        ps_i = [0]

        def bank():
            b = PSB[ps_i[0] % 8]
            ps_i[0] += 1
            return b

        def v4(b):
            return b.t[:, :].rearrange("p (a b) -> p a b", a=4)

        def vt(b):
            return b.t[:, :].bitcast(BF16).rearrange("p (a b) -> p a b", a=8)

        xt = [sb("xt%d" % i, [128, D], F32) for i in range(2)]
        xs = [sb("xs%d" % i, [128, D], BF16) for i in range(2)]
        smask = sb("smask", [128, NB, 129], BF16)
        ssq = sb("ssq", [128, 3, 16], F32)
        ss2 = sb("ss2", [128, 3, 4], F32)
        ss3 = sb("ss3", [128, 3, 4], F32)

        ARENA = 111104
        AR = es.enter_context(nc.sbuf_tensor("AR", [128, ARENA // 2], BF16))
        cur = [0]

        def sb(name, shape, dt):
            n = 1
            for k in shape[1:]:
                n *= k
            nbytes = n * (4 if dt == F32 else 2)
            off = cur[0]
            cur[0] += (nbytes + 3) // 4 * 4
            assert cur[0] <= ARENA, (name, cur[0])
            ap = AR[:, off // 2:(off + nbytes) // 2]
            if dt == F32:
                ap = ap.bitcast(F32)
            if len(shape) == 3:
                ap = ap.rearrange("p (a b) -> p a b", a=shape[1])
            elif len(shape) == 4:
                ap = ap.rearrange("p (a b c) -> p a b c", a=shape[1], b=shape[2])
            return Buf(ap)

        wh = [sb("wh%d" % i, [128, 5, 8, 128], BF16) for i in range(2)]
        sgq = [sb("sgq%d" % i, [128, 512], BF16) for i in range(2)]
        q_s = sb("q_s", [128, SEQ], BF16)
        sog = [sb("sog%d" % i, [128, SEQ], BF16) for i in range(2)]
        kT = [sb("kT%d" % i, [128, SEQ], BF16) for i in range(2)]
        G = [sb("G%d" % i, [128, NB, 129], F32) for i in range(2)]
        qd = [sb("qd%d" % i, [128, SEQ], BF16) for i in range(2)]
        kiT = [sb("kiT%d" % i, [128, SEQ], BF16) for i in range(2)]
        eA0 = sb("eA", [128, SEQ], BF16)
        eA = [eA0, eA0]
        ki = [sb("ki%d" % i, [128, NB, 128], BF16) for i in range(2)]
        v_h = sb("v_h", [128, NB, 128], BF16)
        used = [sb("used%d" % i, [128, NB, 128], BF16) for i in range(2)]
        Ed = [sb("Ed%d" % i, [128, NB, 1], F32) for i in range(2)]
        Emat = sb("Emat", [128, 1024], F32)
        mlt = [sb("mlt%d" % i, [128, 16], F32) for i in range(2)]
        msk = [sb("msk%d" % i, [128, 4, 128], BF16) for i in range(4)]
        sq = sgq
        lnr = sb("lnr", [128, 512], F32)

        print("arena phase2 bytes", cur[0])
        cur[0] = 0
        wpool = sb("wpool", [128, 4, 2, 256], BF16)
        WA = [sb("WA%d" % i, [128, 8, 512], BF16) for i in range(3)]
        wgu = [sb("wgu%d" % i, [128, 2, 8, 128], BF16) for i in range(2)]
        w2s = [sb("w2s%d" % i, [128, D], BF16) for i in range(4)]
        R1 = sb("R1", [128, 12288], BF16).t
        pblk = Buf(R1[:, 0:6144].rearrange("p (a b) -> p a b", a=6))
        yT = Buf(R1[:, 6144:10240].rearrange("p (a b) -> p a b", a=8))
        ybT = Buf(R1[:, 0:4096].rearrange("p (a b) -> p a b", a=8))
        sga = Buf(R1[:, 4096:8192].rearrange("p (a b) -> p a b", a=8))
        sgb = Buf(R1[:, 8192:12288].rearrange("p (a b) -> p a b", a=8))
        actT = Buf(R1[:, 0:11264].rearrange("p (a b) -> p a b", a=NFC))
        h2 = sb("h2", [128, 4, D], F32)
        mT = sb("mT", [128, 8, 512], BF16)
        u2T = sb("u2T", [128, 8, 512], BF16)
        ost = xt
        tz = [sb("tz%d" % i, [128, 512], BF16) for i in range(2)]
        sgt = [sb("sgt%d" % i, [128, 512], BF16) for i in range(2)]
        print("arena phase3 bytes", cur[0])

        dbg_bufs = []

        def dump(name, ap, buf, shape, dt):
            if not DEBUG:
                return
            dd = nc.dram_tensor("dbg_" + name, list(shape), dt, kind="ExternalOutput").ap()
            P.dma("sp", (lambda dd, ap: lambda e: e.dma_start(out=dd, in_=ap))(dd, ap), buf, is_load=False)
            dbg_bufs.append(buf)

        def L(meth, *a, **kw):
            return lambda e: getattr(e, meth)(*a, **kw)

        for i in range(3):
            P.dma("pool" if i < 2 else "sp", L("dma_start", out=gbc[i].t[:], in_=gv_d[i].partition_broadcast(128)), gbc[i])
        P.dma("sp", L("dma_start", out=cols.t[:], in_=cols_d), cols)
        P.dma("sp", L("dma_start", out=cb.t[:], in_=cb_d.rearrange("p (a b) -> p a b", a=26)), cb)
        P.op("pool", L("memset", smask.t[:, :, :], 1.0), [], [smask])
        P.op("pool", L("memset", smask.t[:, :, 0:1], 0.0), [], [smask])
        for a in range(2):
            P.op("dve", L("tensor_tensor", out=lbt.t[:, 0, a * 8:(a + 1) * 8],
                          in0=cols.t[:, 16 + (a * 2 + 1) * 8:16 + (a * 2 + 2) * 8],
                          in1=cols.t[:, 16 + (a * 2) * 8:16 + (a * 2 + 1) * 8], op=ALU.subtract), [cols], [lbt])
        P.op("act", L("activation", out=lbt.t[:, 4, :], in_=lbt.t[:, 0, :], func=AF.Exp), [lbt], [lbt])
        P.op("dve", L("tensor_scalar_add", out=lbt.t[:, 0, :], in0=lbt.t[:, 4, :], scalar1=1.0), [lbt], [lbt])
        P.op("dve", L("reciprocal", out=lbt.t[:, 1, :], in_=lbt.t[:, 0, :]), [lbt], [lbt])
        P.op("dve", L("tensor_scalar", out=lbt.t[:, 2, :], in0=lbt.t[:, 1, :], scalar1=-1.0, scalar2=1.0,
                      op0=ALU.mult, op1=ALU.add), [lbt], [lbt])
        P.op("dve", L("tensor_scalar_add", out=lbt.t[:, 3, :], in0=lbt.t[:, 1, :], scalar1=-1.0), [lbt], [lbt])

        def rms_stats(src_ap, src_buf, ssb, col, junk):
            P.op("act", L("activation", out=junk.t[:], in_=src_ap, func=AF.Square,
                          accum_out=ssb.t[:, 0, col:col + 1]), [src_buf], [junk, ssb])
            P.op("act", L("activation", out=ssb.t[:, 1, col:col + 1], in_=ssb.t[:, 0, col:col + 1], func=AF.Ln,
                          scale=1.0 / D, bias=EPS), [ssb], [ssb])
            P.op("act", L("activation", out=ssb.t[:, 2, col:col + 1], in_=ssb.t[:, 1, col:col + 1], func=AF.Exp,
                          scale=-0.5), [ssb], [ssb])

        def fm_prep(src_ap, src_buf, rstd_ap, rstd_buf, gb, slot):
            P.op("dve", L("scalar_tensor_tensor", out=xs[slot].t[:], in0=src_ap, scalar=rstd_ap, in1=gb.t[:],
                          op0=ALU.mult, op1=ALU.mult), [src_buf, rstd_buf, gb], [xs[slot]])

        def fm_transpose(slot, dst_ap, dst_buf, k):
            pb = bank()
            for c in range(8):
                P.op("pe", L("transpose", vt(pb)[:, c, :], xs[slot].t[:, c * 128:(c + 1) * 128], ident),
                     [xs[slot], cb], [pb], signal=(c == 7))
            if k % 2 == 0:
                P.op("act", L("activation", out=dst_ap, in_=vt(pb), func=AF.Copy), [pb], [dst_buf])
            else:
                P.op("dve", L("tensor_copy", out=dst_ap, in_=vt(pb)), [pb], [dst_buf])

        def to_feature_major(src_ap, src_buf, rstd_ap, rstd_buf, gb, slot, dst_ap, dst_buf, k):
            fm_prep(src_ap, src_buf, rstd_ap, rstd_buf, gb, slot)
            fm_transpose(slot, dst_ap, dst_buf, k)

        wa_i = [0]

        def load_slab(idx):
            b = WA[wa_i[0] % 3]
            wa_i[0] += 1
            P.dma("sp", L("dma_start", out=b.t[:, :, :], in_=s_slab[idx]), b)
            return b

        def load_head(h):
            b = wh[h % 2]
            for seg in range(5):
                P.dma("pool", L("dma_start", out=b.t[:, seg, :, :],
                                in_=win_d[:, seg * D + h * 128:seg * D + (h + 1) * 128].rearrange("(c p) n -> p c n", p=128)), b)
            return b

        conv_dummy = [Buf(None) for _ in range(4)]
        conv_jobs = []
        for c in range(8):
            rows = slice(c * 128, (c + 1) * 128)
            conv_jobs.append((s_slab[0:6, :, c, :].rearrange("s p n -> p s n"),
                              win_d[rows, 5 * D:8 * D].rearrange("p (s n) -> p s n", n=512)))
            for mi, wd in enumerate((wa_d, wb_d, wo_d)):
                conv_jobs.append((s_slab[6 + 2 * mi:8 + 2 * mi, :, c, :].rearrange("s p n -> p s n"),
                                  wd[rows, :].rearrange("p (s n) -> p s n", n=512)))
            conv_jobs.append((s_w1[:, :, :, c, :].rearrange("f p k n -> p k f n"),
                              w1_d[rows, :].rearrange("p (k f n) -> p k f n", k=2, n=128)))
        for f2 in range(0, NFC, 2):
            conv_jobs.append((s_w2[f2 * 128:(f2 + 2) * 128, :], w2_d[f2 * 128:(f2 + 2) * 128, :]))
        conv_i = [0]

        def issue_conv(n):
            for _ in range(n):
                if conv_i[0] >= len(conv_jobs):
                    return
                o_ap, i_ap = conv_jobs[conv_i[0]]
                P.dma("pool", L("dma_start", out=o_ap, in_=i_ap), conv_dummy[conv_i[0] % 4])
                conv_i[0] += 1

        def mm_group(out_ap, pairs, reads, pb, last_signal=True):
            n = len(pairs)
            for k, (lh, rh) in enumerate(pairs):
                P.op("pe", L("matmul", out_ap, lhsT=lh, rhs=rh, start=(k == 0), stop=(k == n - 1)), reads, [pb],
                     signal=(last_signal and k == n - 1))

        for s in range(NSEQ):
            row0 = s * SEQ
            for b in range(NB):
                sl = b % 2
                P.dma("sp", L("dma_start", out=xt[sl].t[:], in_=x_d[row0 + b * 128:row0 + (b + 1) * 128, :]), xt[sl])
                rms_stats(xt[sl].t[:], xt[sl], ssq, b, xs[sl])
                to_feature_major(xt[sl].t[:], xt[sl], ssq.t[:, 2, b:b + 1], ssq, gbc[0], sl,
                                 uT.t[:, :, b * 128:(b + 1) * 128], uT, b)

            if s == 0:
                dump("uT", uT.t[:, :, :], uT, [128, 8, SEQ], BF16)
            P.barrier()
            for d_ in range(2):
                P.op("pool", L("memset", used[d_].t[:, :, :], 0.0), [], [used[d_]])
                P.op("pool", L("memset", G[d_].t[:, :, 0:1], 0.0), [], [G[d_]])
                P.op("pool", L("memset", mlt[d_].t[:, :], 0.0), [], [mlt[d_]])

            def proj_and_sig(h, wcur):
                sg_o = sog[h % 2]
                for tt in range(4):
                    tsl = slice(tt * 512, (tt + 1) * 512)
                    banks = []
                    for si in (0, 1, 2, 4):
                        pb = bank()
                        banks.append(pb)
                        mm_group(pb.t[:, :], [(wcur.t[:, si, c, :], uT.t[:, c, tsl]) for c in range(8)], [wcur, uT], pb)
                    pq, pf, pbk, pog = banks
                    for (pz, dst, k) in ((pq, q_s, 0), (pog, sg_o, 1)):
                        sg = sgq[k]
                        P.op("act", L("activation", out=sg.t[:, :], in_=pz.t[:, :], func=AF.Sigmoid), [pz], [sg])
                        P.op("dve", L("tensor_tensor", out=dst.t[:, tsl], in0=pz.t[:, :], in1=sg.t[:, :], op=ALU.mult), [pz, sg], [dst])
                    for d_, pz in ((0, pf), (1, pbk)):
                        gsl = G[d_].t[:, tt * 4:(tt + 1) * 4, 1:129]
                        col = d_ * 8 + h
                        P.op("act", L("activation", out=gsl, in_=v4(pz), func=AF.Sigmoid), [pz], [G[d_]])
                        P.op("dve", L("tensor_scalar", out=kT[d_].t[:, tsl].rearrange("p (a b) -> p a b", a=4), in0=gsl,
                                      scalar1=lbt.t[:, 3, col:col + 1], scalar2=lbt.t[:, 2, col:col + 1],
                                      op0=ALU.mult, op1=ALU.add), [G[d_], lbt], [kT[d_]])

            def decay_ln(h):
                for d_ in range(2):
                    col = d_ * 8 + h
                    gin = G[d_].t[:, :, 1:129]
                    P.op("act", L("activation", out=gin, in_=gin, func=AF.Ln, scale=lbt.t[:, 2, col:col + 1],
                                  bias=lbt.t[:, 1, col:col + 1]), [G[d_], lbt], [G[d_]])

            def decay_scan(d_):
                gfl = G[d_].t[:, :, :].rearrange("p a b -> p (a b)")
                P.op("dve", L("tensor_tensor_scan", out=gfl, data0=smask.t[:, :, :].rearrange("p a b -> p (a b)"), data1=gfl,
                              initial=0.0, op0=ALU.mult, op1=ALU.add), [G[d_], smask], [G[d_]])

            def decay_a(h):
                decay_ln(h)
                decay_scan(0)
                decay_scan(1)

            def decay_b(h):
                for d_ in range(2):
                    qv = qd[d_].t[:, :].rearrange("p (a b) -> p a b", a=NB)
                    ev = eA[d_].t[:, :].rearrange("p (a b) -> p a b", a=NB)
                    if d_ == 0:
                        cq, sq_, sk_ = G[0].t[:, :, 1:129], 1.0, -1.0
                    else:
                        cq, sq_, sk_ = G[1].t[:, :, 0:128], -1.0, 1.0
                    P.op("act", L("activation", out=ev, in_=cq, func=AF.Exp, scale=sk_), [G[d_]], [eA[d_]])
                    P.op("act", L("activation", out=qv, in_=cq, func=AF.Exp, scale=sq_), [G[d_]], [qd[d_]])
                    P.op("act", L("activation", out=Ed[d_].t[:, :, :], in_=G[d_].t[:, :, 128:129], func=AF.Exp), [G[d_]], [Ed[d_]])
                    P.op("dve", L("tensor_tensor", out=kiT[d_].t[:, :], in0=kT[d_].t[:, :], in1=eA[d_].t[:, :], op=ALU.mult),
                         [kT[d_], eA[d_]], [kiT[d_]])
                    P.op("pool", L("tensor_tensor", out=qd[d_].t[:, :], in0=qd[d_].t[:, :], in1=q_s.t[:, :], op=ALU.mult),
                         [qd[d_], q_s], [qd[d_]])

            def vproj(h, wcur):
                for jg in range(4):
                    pb = bank()
                    for jj in range(4):
                        j = jg * 4 + jj
                        mm_group(v4(pb)[:, jj, :], [(uT.t[:, c, j * 128:(j + 1) * 128], wcur.t[:, 3, c, :]) for c in range(8)],
                                 [uT, wcur], pb, last_signal=(jj == 3))
                    P.op("act", L("activation", out=v_h.t[:, jg * 4:(jg + 1) * 4, :], in_=v4(pb), func=AF.Copy), [pb], [v_h])

            def rec_chain(h):
                for d_ in range(2):
                    for jg in range(2):
                        pb = bank()
                        for jj in range(8):
                            j = jg * 8 + jj
                            P.op("pe", L("transpose", vt(pb)[:, jj, :], kiT[d_].t[:, j * 128:(j + 1) * 128], ident),
                                 [kiT[d_], cb], [pb], signal=(jj == 7))
                        if jg == 0:
                            P.op("dve", L("tensor_copy", out=ki[d_].t[:, jg * 8:(jg + 1) * 8, :], in_=vt(pb)), [pb], [ki[d_]])
                        else:
                            P.op("act", L("activation", out=ki[d_].t[:, jg * 8:(jg + 1) * 8, :], in_=vt(pb), func=AF.Copy), [pb], [ki[d_]])
                for d_ in range(2):
                    order = list(range(NB)) if d_ == 0 else list(range(NB - 1, -1, -1))
                    if d_ == 0:
                        P.op("pool", L("tensor_copy", out=mlt[0].t[:, 0:15], in_=Ed[0].t[:, 0:15, 0]), [Ed[0]], [mlt[0]])
                    else:
                        P.op("pool", L("tensor_copy", out=mlt[1].t[:, 0:15], in_=Ed[1].t[:, 14::-1, 0]), [Ed[1]], [mlt[1]])
                    em3 = Emat.t[:, :].rearrange("p (v j) -> p v j", j=16)
                    P.op("pool", L("tensor_copy", out=em3, in_=mlt[d_].t[:, :].unsqueeze(1).broadcast_to([128, 64, 16])), [mlt[d_]], [Emat])
                    for g4 in range(4):
                        pb = bank()
                        for slot in range(4):
                            j = order[g4 * 4 + slot]
                            P.op("pe", L("matmul", v4(pb)[:, slot, :], lhsT=ki[d_].t[:, j, :], rhs=v_h.t[:, j, :], start=True, stop=True),
                                 [ki[d_], v_h], [pb], signal=(slot == 3))
                        for hf in range(2):
                            dst = xt[hf].t[:, :].rearrange("p (v j) -> p v j", j=16)[:, :, g4 * 4:(g4 + 1) * 4].rearrange("p v j -> p j v")
                            src = v4(pb)[:, :, hf * 64:(hf + 1) * 64]
                            if g4 % 2 == 0:
                                P.op("act", L("activation", out=dst, in_=src, func=AF.Copy), [pb], [xt[hf]])
                            else:
                                P.op("dve", L("tensor_copy", out=dst, in_=src), [pb], [xt[hf]])
                    for hf in range(2):
                        P.op("dve", L("tensor_tensor_scan", out=xt[hf].t[:, :], data0=xt[hf].t[:, :], data1=Emat.t[:, :], initial=0.0,
                                      op0=ALU.add, op1=ALU.mult), [Emat, xt[hf]], [xt[hf]])
                        w3 = xt[hf].t[:, :].rearrange("p (v j) -> p v j", j=16)
                        P.op("act", L("activation", out=used[d_].t[:, 1:16, hf * 64:(hf + 1) * 64],
                                      in_=w3[:, :, 0:15].rearrange("p v j -> p j v"), func=AF.Copy), [xt[hf]], [used[d_]])
            def rec_sweep(h, scan_next):
                sg_o = sog[h % 2]
                pscs = {}

                def scores(g4):
                    pscs[g4] = [bank(), bank()]
                    for jj in range(4):
                        j = g4 * 4 + jj
                        bsl = slice(j * 128, (j + 1) * 128)
                        psc = pscs[g4][jj // 2]
                        so = 2 * (jj % 2)
                        for d_ in range(2):
                            P.op("pe", L("matmul", v4(psc)[:, so + d_, :], lhsT=kiT[d_].t[:, bsl], rhs=qd[d_].t[:, bsl], start=True, stop=True),
                                 [kiT[d_], qd[d_]], [psc], signal=(jj % 2 == 1 and d_ == 1))
                    for b2 in range(2):
                        mk = msk[2 * (g4 % 2) + b2]
                        P.op("dve", L("tensor_tensor", out=mk.t[:, :, :], in0=v4(pscs[g4][b2]), in1=masks4, op=ALU.mult),
                             [pscs[g4][b2], cb], [mk])

                pos = {}

                def omain(g4):
                    po = bank()
                    pos[g4] = po
                    for jj in range(4):
                        j = g4 * 4 + jj
                        bsl = slice(j * 128, (j + 1) * 128)
                        mk = msk[2 * (g4 % 2) + jj // 2]
                        so = 2 * (jj % 2)
                        osl = slice(jj * 128, (jj + 1) * 128)
                        mm_group(po.t[:, osl], [(v_h.t[:, j, :], mk.t[:, so, :]), (v_h.t[:, j, :], mk.t[:, so + 1, :]),
                                                (used[0].t[:, j, :], qd[0].t[:, bsl]), (used[1].t[:, NB - 1 - j, :], qd[1].t[:, bsl])],
                                 [v_h, mk, used[0], used[1], qd[0], qd[1]], po)
                    P.op("act", L("activation", out=sq[g4 % 2].t[:, :], in_=po.t[:, :], func=AF.Square), [po], [sq[g4 % 2]])

                def otail(g4):
                    po = pos[g4]
                    tsl = slice(g4 * 512, (g4 + 1) * 512)
                    pm = bank()
                    P.op("pe", L("matmul", pm.t[:, :], lhsT=onesm, rhs=sq[g4 % 2].t[:, :], start=True, stop=True), [cb, sq[g4 % 2]], [pm])
                    P.op("act", L("activation", out=lnr.t[:, :], in_=pm.t[:, :], func=AF.Ln, bias=EPS), [pm], [lnr])
                    P.op("act", L("activation", out=lnr.t[:, :], in_=lnr.t[:, :], func=AF.Exp, scale=-0.5), [lnr], [lnr])
                    P.op("dve", L("tensor_tensor", out=lnr.t[:, :], in0=po.t[:, :], in1=lnr.t[:, :], op=ALU.mult), [po, lnr], [lnr])
                    P.op("dve", L("scalar_tensor_tensor", out=yaT.t[:, h, tsl], in0=lnr.t[:, :], scalar=cols.t[:, h:h + 1], in1=sg_o.t[:, tsl],
                                  op0=ALU.mult, op1=ALU.mult), [lnr, cols, sg_o], [yaT])

                scores(0)
                scores(1)
                omain(0)
                for g4 in range(1, 4):
                    if g4 + 1 < 4:
                        scores(g4 + 1)
                    omain(g4)
                    if scan_next and g4 >= 2:
                        decay_scan(g4 - 2)
                    otail(g4 - 1)
                otail(3)

            whs = {0: load_head(0)}
            proj_and_sig(0, whs[0])
            decay_a(0)
            for h in range(8):
                if h + 1 < 8:
                    whs[h + 1] = load_head(h + 1)
                if s == 0:
                    issue_conv(7)
                decay_b(h)
                vproj(h, whs[h])
                rec_chain(h)
                if h + 1 < 8:
                    proj_and_sig(h + 1, whs[h + 1])
                    decay_ln(h + 1)
                rec_sweep(h, h + 1 < 8)
            if s == 0:
                dump("yaT", yaT.t[:, :, :], yaT, [128, 8, SEQ], BF16)
            P.barrier()
            P.dma("pool", L("dma_start", out=wpool.t[:, :, :, :], in_=pw_d.rearrange("g (k p) n -> p g k n", p=128)), wpool)
            for tt in range(4):
                tsl = slice(tt * 512, (tt + 1) * 512)
                blks = [jb for jb in range(4 * tt - 1, 4 * tt + 5) if 0 <= jb < NB]
                slot_of = {jb: i for i, jb in enumerate(blks)}
                for half in range(2):
                    wsl = load_slab(half)
                    for k, jb in enumerate(blks):
                        pb = bank()
                        mm_group(pb.t[:, :], [(uT.t[:, c, jb * 128:(jb + 1) * 128], wsl.t[:, c, :]) for c in range(8)], [uT, wsl], pb)
                        dst = pblk.t[:, slot_of[jb], half * 512:(half + 1) * 512]
                        if k % 2 == 0:
                            P.op("act", L("activation", out=dst, in_=pb.t[:, :], func=AF.Copy), [pb], [pblk])
                        else:
                            P.op("dve", L("tensor_copy", out=dst, in_=pb.t[:, :]), [pb], [pblk])
                for c in range(8):
                    g = c // 2
                    pb = bank()
                    for jj in range(4):
                        j = 4 * tt + jj
                        srcs = []
                        if j - 1 >= 0:
                            srcs.append((j - 1, 0))
                        srcs.append((j, 3 if j == 0 else (4 if j == NB - 1 else 1)))
                        if j + 1 < NB:
                            srcs.append((j + 1, 2))
                        mm_group(pb.t[:, jj * 128:(jj + 1) * 128],
                                 [(pblk.t[:, slot_of[jb], c * 128:(c + 1) * 128], pmat(g, kind)) for (jb, kind) in srcs],
                                 [pblk, cb], pb, last_signal=(jj == 3))
                    if c % 2 == 0:
                        P.op("act", L("activation", out=yT.t[:, c, :], in_=pb.t[:, :], func=AF.Copy), [pb], [yT])
                    else:
                        P.op("dve", L("tensor_copy", out=yT.t[:, c, :], in_=pb.t[:, :]), [pb], [yT])
                for c2 in range(8):
                    g, hf = c2 // 2, c2 % 2
                    pb = bank()
                    mm_group(pb.t[:, :], [(wpool.t[:, g, kc, hf * 128:(hf + 1) * 128], yT.t[:, 2 * g + kc, :]) for kc in range(2)], [wpool, yT], pb)
                    P.op("dve", L("tensor_scalar", out=ybT.t[:, c2, :], in0=pb.t[:, :], scalar1=cols.t[:, 8 + c2:9 + c2], scalar2=None,
                                  op0=ALU.mult), [pb, cols], [ybT])
                if s == 0 and tt == 0:
                    dump("ybT", ybT.t[:, :, :], ybT, [128, 8, 512], BF16)
                for gi, (gbuf, seg) in enumerate(((sga, 6), (sgb, 7))):
                    for half in range(2):
                        wsl = load_slab((seg - 5) * 2 + half)
                        for cc in range(4):
                            pb = bank()
                            mm_group(pb.t[:, :], [(wsl.t[:, c, cc * 128:(cc + 1) * 128], uT.t[:, c, tsl]) for c in range(8)], [wsl, uT], pb)
                            P.op("act", L("activation", out=gbuf.t[:, half * 4 + cc, :], in_=pb.t[:, :], func=AF.Sigmoid), [pb], [gbuf])
                for half in range(2):
                    wsa = load_slab(6 + half)
                    wsb = load_slab(8 + half)
                    for cc in range(4):
                        dc = half * 4 + cc
                        pa = bank()
                        pb2 = bank()
                        mm_group(pa.t[:, :], [(wsa.t[:, c, cc * 128:(cc + 1) * 128], yaT.t[:, c, tsl]) for c in range(8)], [wsa, yaT], pa)
                        mm_group(pb2.t[:, :], [(wsb.t[:, c, cc * 128:(cc + 1) * 128], ybT.t[:, c, :]) for c in range(8)], [wsb, ybT], pb2)
                        P.op("dve", L("tensor_tensor", out=tz[0].t[:, :], in0=pa.t[:, :], in1=sga.t[:, dc, :], op=ALU.mult), [pa, sga], [tz[0]])
                        P.op("dve", L("tensor_tensor", out=tz[1].t[:, :], in0=pb2.t[:, :], in1=sgb.t[:, dc, :], op=ALU.mult), [pb2, sgb], [tz[1]])
                        P.op("pool", L("tensor_tensor", out=mT.t[:, dc, :], in0=tz[0].t[:, :], in1=tz[1].t[:, :], op=ALU.add), [tz[0], tz[1]], [mT])
                if s == 0 and tt == 0:
                    dump("sga", sga.t[:, :, :], sga, [128, 8, 512], BF16)
                    dump("mT", mT.t[:, :, :], mT, [128, 8, 512], BF16)
                wso = [load_slab(10 + half) for half in range(2)]
                def blk_mm(jj):
                    j = 4 * tt + jj
                    sl = jj % 2
                    P.dma("sp", L("dma_start", out=xt[sl].t[:], in_=x_d[row0 + j * 128:row0 + (j + 1) * 128, :]), xt[sl])
                    for half in range(2):
                        pb = bank()
                        mm_group(pb.t[:, :], [(mT.t[:, c, jj * 128:(jj + 1) * 128], wso[half].t[:, c, :]) for c in range(8)], [mT, wso[half]], pb)
                        P.op("dve", L("tensor_tensor", out=h2.t[:, jj, half * 512:(half + 1) * 512], in0=pb.t[:, :],
                                      in1=xt[sl].t[:, half * 512:(half + 1) * 512], op=ALU.add), [pb, xt[sl]], [h2])
                    rms_stats(h2.t[:, jj, :], h2, ss2, jj, xs[sl])
                    fm_prep(h2.t[:, jj, :], h2, ss2.t[:, 2, jj:jj + 1], ss2, gbc[1], sl)

                def blk_tr(jj):
                    fm_transpose(jj % 2, u2T.t[:, :, jj * 128:(jj + 1) * 128], u2T, jj)

                blk_mm(0)
                for jj in range(1, 4):
                    blk_mm(jj)
                    blk_tr(jj - 1)
                blk_tr(3)
                if s == 0 and tt == 0:
                    dump("h2", h2.t[:, :, :], h2, [128, 4, D], F32)
                    dump("u2T", u2T.t[:, :, :], u2T, [128, 8, 512], BF16)
                for fc in range(NFC):
                    wg = wgu[fc % 2]
                    P.dma("sp", L("dma_start", out=wg.t[:, :, :, :], in_=s_w1[fc]), wg)
                    pg = bank()
                    pu = bank()
                    mm_group(pg.t[:, :], [(wg.t[:, 0, c, :], u2T.t[:, c, :]) for c in range(8)], [wg, u2T], pg)
                    mm_group(pu.t[:, :], [(wg.t[:, 1, c, :], u2T.t[:, c, :]) for c in range(8)], [wg, u2T], pu)
                    sg = sgt[fc % 2]
                    P.op("act", L("activation", out=sg.t[:, :], in_=pg.t[:, :], func=AF.Silu), [pg], [sg])
                    P.op("dve", L("tensor_tensor", out=actT.t[:, fc, :], in0=pu.t[:, :], in1=sg.t[:, :], op=ALU.mult), [pu, sg], [actT])
                if s == 0 and tt == 0:
                    dump("actT", actT.t[:, :, :], actT, [128, NFC, 512], BF16)
                for half in range(2):
                    hsl = slice(half * 512, (half + 1) * 512)
                    obanks = [bank() for _ in range(4)]
                    for fc in range(NFC):
                        w2 = w2s[(half * NFC + fc) % 4]
                        P.dma("sp", L("dma_start", out=w2.t[:, 0:512], in_=s_w2[fc * 128:(fc + 1) * 128, hsl]), w2)
                        for jj in range(4):
                            pb = obanks[jj]
                            P.op("pe", L("matmul", pb.t[:, :], lhsT=actT.t[:, fc, jj * 128:(jj + 1) * 128], rhs=w2.t[:, 0:512],
                                         start=(fc == 0), stop=(fc == NFC - 1)), [actT, w2], [pb],
                                 signal=(fc == NFC - 1) or (jj == 3))
                    for jj in range(4):
                        pb = obanks[jj]
                        P.op("dve", L("tensor_tensor", out=h2.t[:, jj, hsl], in0=pb.t[:, :], in1=h2.t[:, jj, hsl], op=ALU.add), [pb, h2], [h2])
                for jj in range(4):
                    j = 4 * tt + jj
                    rms_stats(h2.t[:, jj, :], h2, ss3, jj, xs[jj % 2])
                    ob = ost[jj % 2]
                    P.op("dve", L("scalar_tensor_tensor", out=ob.t[:, :], in0=h2.t[:, jj, :], scalar=ss3.t[:, 2, jj:jj + 1], in1=gbc[2].t[:, :],
                                  op0=ALU.mult, op1=ALU.mult), [h2, ss3, gbc[2]], [ob])
                    P.dma("pool", L("dma_start", out=out_d[row0 + j * 128:row0 + (j + 1) * 128, :], in_=ob.t[:, :]), ob, is_load=False)
        P.wait_all("sp", list(ost) + dbg_bufs)

        sems_eng = {k: es.enter_context(nc.semaphore("sem_" + k)) for k in ["pe", "act", "dve", "pool"]}
        sems_dma = [es.enter_context(nc.semaphore("dsem%d" % i)) for i in range(P.ndma)]
        with nc.Block() as block:
            P.emit(block, sems_eng, sems_dma)
    return nc


_NC_CACHE = {}


def kernel(x, g_mix, w_in, lb_logits, hgrn_norm_g, pool_w, pool_scale, w_branch_a, w_branch_b, w_out,
           g_ffn, w_ffn_in, w_ffn_out, g_final):
    f32 = np.float32
    x = np.asarray(x, f32)
    B = x.shape[0]
    xs_ = x.reshape(NCORES, NSEQ * SEQ, D)
    gvec = np.ascontiguousarray(np.stack([np.asarray(g_mix, f32)[0], np.asarray(g_ffn, f32)[0], np.asarray(g_final, f32)]))
    cols = np.zeros((128, 48), f32)
    cols[:, 0:8] = np.asarray(hgrn_norm_g, f32)[0].reshape(8, 128).T
    cols[:, 8:16] = np.asarray(pool_scale, f32)[0].reshape(8, 128).T
    lbl = np.asarray(lb_logits, f32)
    cols[:, 16:48] = lbl.reshape(2, 2, 8, 128).transpose(3, 0, 1, 2).reshape(128, 32)
    shared = {
        "w_in": np.ascontiguousarray(np.asarray(w_in, f32)[0]),
        "w_a": np.ascontiguousarray(np.asarray(w_branch_a, f32)[0]),
        "w_b": np.ascontiguousarray(np.asarray(w_branch_b, f32)[0]),
        "w_o": np.ascontiguousarray(np.asarray(w_out, f32)[0]),
        "pool_w": np.ascontiguousarray(np.asarray(pool_w, f32)[0]),
        "w_ffn_in": np.ascontiguousarray(np.asarray(w_ffn_in, f32)[0]),
        "w_ffn_out": np.ascontiguousarray(np.asarray(w_ffn_out, f32)[0]),
        "gvec": gvec,
        "cols": cols,
        "cb": _const_bf16(),
    }
    if "nc" not in _NC_CACHE:
        _NC_CACHE["nc"] = build_program()
    nc = _NC_CACHE["nc"]
    in_maps = []
    for c in range(NCORES):
        m = dict(shared)
        m["x"] = np.ascontiguousarray(xs_[c])
        in_maps.append(m)
    res = run_bass_kernel_spmd(nc, in_maps, core_ids=list(range(NCORES)))
    out = np.stack([np.asarray(r["out"], f32) for r in res.results], axis=0)
    return out.reshape(B, SEQ, D)
```
